# Optimizing a Trainium2 kernel written in Bass

```python
import jax, jax.numpy as jnp
from jax import lax
import numpy as np

D_MODEL = 1024
BATCH = 4
SEQ = 8192
DEPTH = 2

N_META = 16
MIX_WIDTH = D_MODEL
ATTN_WIDTH = MIX_WIDTH // 2
CONV_WIDTH = MIX_WIDTH - ATTN_WIDTH
HEAD_DIM = 64
N_Q_HEADS = ATTN_WIDTH // HEAD_DIM
N_KV_HEADS = 2
GROUP = N_Q_HEADS // N_KV_HEADS
KV_WIDTH = N_KV_HEADS * HEAD_DIM
CONV_GROUPS = 8
CONV_K = 3
WINDOW = 128
BLOCK = 128
LEAD_PAD = BLOCK - N_META
ROPE_THETA = 500000.0
ROT_DIM = HEAD_DIM // 4
D_FF = 4 * D_MODEL
IN_WIDTH = ATTN_WIDTH + 2 * KV_WIDTH + 3 * CONV_WIDTH
EPS = 1e-6

kernel_name = 'hymba_swa_sink_shortconv_sandwich'


def rmsnorm(x, g):
    x32 = x.astype(jnp.float32)
    y = x32 * lax.rsqrt(jnp.mean(x32 * x32, axis=-1, keepdims=True) + EPS)
    return y.astype(x.dtype) * g.astype(x.dtype)


def rope_tables(n_pos):
    pos = jnp.arange(n_pos, dtype=jnp.float32)
    inv_freq = jnp.power(jnp.float32(ROPE_THETA), -jnp.arange(0, ROT_DIM, 2, dtype=jnp.float32) / ROT_DIM)
    ang = pos[:, None] * inv_freq[None, :]
    return jnp.cos(ang), jnp.sin(ang)


def partial_rope(t, cos, sin):
    half = ROT_DIM // 2
    t32 = t[..., :ROT_DIM].astype(jnp.float32)
    t1, t2 = t32[..., :half], t32[..., half:]
    c, s = cos[None, :, None, :], sin[None, :, None, :]
    rot = jnp.concatenate([t1 * c - t2 * s, t2 * c + t1 * s], axis=-1).astype(t.dtype)
    return jnp.concatenate([rot, t[..., ROT_DIM:]], axis=-1)


def sliding_window_gqa_sinks(q, k, v, sink):
    bsz, L = q.shape[0], q.shape[1]
    pad = ((0, 0), (LEAD_PAD, 0), (0, 0), (0, 0))
    q, k, v = jnp.pad(q, pad), jnp.pad(k, pad), jnp.pad(v, pad)
    Lp = L + LEAD_PAD
    nb = Lp // BLOCK
    qb = q.reshape(bsz, nb, BLOCK, N_KV_HEADS, GROUP, HEAD_DIM)

    def band(t):
        tb = t.reshape(bsz, nb, BLOCK, N_KV_HEADS, HEAD_DIM)
        prev = jnp.pad(tb, ((0, 0), (1, 0), (0, 0), (0, 0), (0, 0)))[:, :-1]
        return jnp.concatenate([prev, tb], axis=2)

    kw, vw = band(k), band(v)
    s = jnp.einsum('bnqhgd,bnkhd->bnhgqk', qb, kw,
                   preferred_element_type=jnp.float32) * (HEAD_DIM ** -0.5)
    blk = jnp.arange(nb)[:, None, None]
    qpos = blk * BLOCK + jnp.arange(BLOCK)[None, :, None]
    kpos = (blk - 1) * BLOCK + jnp.arange(2 * BLOCK)[None, None, :]
    mask = (kpos <= qpos) & (qpos - kpos < WINDOW) & (kpos >= LEAD_PAD)
    s = jnp.where(mask[None, :, None, None], s, -jnp.inf)
    sk = sink.astype(jnp.float32).reshape(1, 1, N_KV_HEADS, GROUP, 1, 1)
    m = jnp.maximum(jnp.max(s, axis=-1, keepdims=True), sk)
    e = jnp.exp(s - m)
    p = e / (jnp.sum(e, axis=-1, keepdims=True) + jnp.exp(sk - m))
    o = jnp.einsum('bnhgqk,bnkhd->bnqhgd', p.astype(v.dtype), vw)
    return o.reshape(bsz, Lp, N_Q_HEADS * HEAD_DIM)[:, LEAD_PAD:]


def short_gated_conv(b_gate, c_gate, h, w):
    u = c_gate * h
    y = lax.conv_general_dilated(u, w[:, None, :].astype(u.dtype), window_strides=(1,),
                                 padding=[(CONV_K - 1, 0)],
                                 dimension_numbers=('NWC', 'WIO', 'NWC'),
                                 feature_group_count=CONV_WIDTH)
    return b_gate * y


def setup_inputs(seed: int = 0) -> dict:
    key = jax.random.key(seed)
    ks = jax.random.split(key, 16)
    f32 = jnp.float32

    def nrm(k, shape, scale):
        return jax.random.normal(k, shape, f32) * scale

    def gain(k, shape):
        return 1.0 + 0.05 * jax.random.normal(k, shape, f32)

    return {
        'x': nrm(ks[0], (BATCH, SEQ, D_MODEL), 1.0),
        'meta_tokens': nrm(ks[1], (N_META, D_MODEL), 1.0),
        'mix_pre_g': gain(ks[2], (DEPTH, D_MODEL)),
        'w_in': nrm(ks[3], (DEPTH, D_MODEL, IN_WIDTH), D_MODEL ** -0.5),
        'conv_w': nrm(ks[4], (DEPTH, CONV_K, CONV_WIDTH), CONV_K ** -0.5),
        'sinks': nrm(ks[5], (DEPTH, N_Q_HEADS), 0.5),
        'attn_out_g': gain(ks[6], (DEPTH, ATTN_WIDTH)),
        'conv_out_g': gain(ks[7], (DEPTH, CONV_WIDTH)),
        'w_out': nrm(ks[8], (DEPTH, MIX_WIDTH, D_MODEL), MIX_WIDTH ** -0.5),
        'mix_post_g': gain(ks[9], (DEPTH, D_MODEL)),
        'mlp_pre_g': gain(ks[10], (DEPTH, D_MODEL)),
        'w_up': nrm(ks[11], (DEPTH, D_MODEL, D_FF), D_MODEL ** -0.5),
        'w_down': nrm(ks[12], (DEPTH, D_FF, D_MODEL), D_FF ** -0.5),
        'mlp_post_g': gain(ks[13], (DEPTH, D_MODEL)),
    }


def reference(x, meta_tokens, mix_pre_g, w_in, conv_w, sinks, attn_out_g, conv_out_g,
              w_out, mix_post_g, mlp_pre_g, w_up, w_down, mlp_post_g):
    bsz = x.shape[0]
    meta = jnp.broadcast_to(meta_tokens[None].astype(x.dtype), (bsz, N_META, D_MODEL))
    h = jnp.concatenate([meta, x], axis=1)
    L = h.shape[1]
    cos, sin = rope_tables(L)
    s_q = ATTN_WIDTH
    s_k = s_q + KV_WIDTH
    s_v = s_k + KV_WIDTH
    s_b = s_v + CONV_WIDTH
    s_c = s_b + CONV_WIDTH
    for l in range(DEPTH):
        a = rmsnorm(h, mix_pre_g[l])
        proj = a @ w_in[l]
        q = proj[..., :s_q].reshape(bsz, L, N_Q_HEADS, HEAD_DIM)
        k = proj[..., s_q:s_k].reshape(bsz, L, N_KV_HEADS, HEAD_DIM)
        v = proj[..., s_k:s_v].reshape(bsz, L, N_KV_HEADS, HEAD_DIM)
        b_gate = proj[..., s_v:s_b]
        c_gate = proj[..., s_b:s_c]
        hc = proj[..., s_c:]
        q = partial_rope(q, cos, sin)
        k = partial_rope(k, cos, sin)
        y_attn = sliding_window_gqa_sinks(q, k, v, sinks[l])
        y_conv = short_gated_conv(b_gate, c_gate, hc, conv_w[l])
        y = jnp.concatenate([rmsnorm(y_attn, attn_out_g[l]),
                             rmsnorm(y_conv, conv_out_g[l])], axis=-1)
        h = h + rmsnorm(y @ w_out[l], mix_post_g[l])
        a = rmsnorm(h, mlp_pre_g[l])
        f = jnp.square(jax.nn.relu(a @ w_up[l])) @ w_down[l]
        h = h + rmsnorm(f, mlp_post_g[l])
    return h[:, N_META:]
```

```python
import contextlib
import numpy as np
import concourse.bass as bass
import concourse.mybir as mybir
from concourse.bass_utils import run_bass_kernel_spmd

F32 = mybir.dt.float32
BF16 = mybir.dt.bfloat16
ALU = mybir.AluOpType
AF = mybir.ActivationFunctionType
AX = mybir.AxisListType

D = 1024
NCH = 8
DFF = 4096
NBLK = 34
TILE_BLOCKS = [4] * 8 + [2]
EPS = 1e-6
NEG = -30000.0
N_META = 16
ROT = 16
THETA = 500000.0
NSLOT = 3
DMA_SLOTS = 20

PE, ACT, DVE, POOL, SP = "pe", "act", "dve", "pool", "sp"
LIMIT = [10 ** 9]


class Op:
    __slots__ = ("eng", "fn", "reads", "writes", "dma", "deps_eng", "deps_dma", "inc", "ticket",
                 "slot", "value", "ndma")

    def __init__(self, eng, fn, reads, writes, dma=False, ndma=1):
        self.eng = eng
        self.fn = fn
        self.reads = reads
        self.writes = writes
        self.dma = dma
        self.ndma = ndma
        self.deps_eng = {}
        self.deps_dma = set()
        self.inc = False
        self.ticket = 0
        self.slot = -1
        self.value = 0


class Prog:
    def __init__(self):
        self.ops = []
        self.mode = "plan"
        self.pos = 0
        self.n = 0

    def add(self, eng, fn, reads=(), writes=(), dma=False, ndma=1):
        self.n += 1
        if self.n > LIMIT[0]:
            return
        if self.mode == "plan":
            writes = list(writes) + [k for k in reads if k.startswith("ps")]
            self.ops.append(Op(eng, None, list(reads), list(writes), dma, ndma))
            import sys as _s
            self.ops[-1].fn = _s._getframe(1).f_lineno
        else:
            op = self.ops[self.pos]
            assert op.eng == eng and op.dma == dma and op.ndma == ndma, (self.pos, op.eng, eng)
            self.emit_one(self.pos, op, fn)
            self.pos += 1

    def schedule(self):
        ops = self.ops
        last_w = {}
        readers = {}
        dma_count = 0
        dma_hist = []
        for i, op in enumerate(ops):
            deps = set()
            for k in op.reads:
                w = last_w.get(k)
                if w is not None:
                    deps.add((w, "raw"))
            for k in op.writes:
                w = last_w.get(k)
                if w is not None:
                    deps.add((w, "waw"))
                for r in readers.get(k, ()):
                    deps.add((r, "war"))
            if op.dma:
                if dma_count >= DMA_SLOTS:
                    deps.add((dma_hist[dma_count - DMA_SLOTS], "raw"))
                dma_hist.append(i)
                dma_count += 1
            for (j, kind) in deps:
                if j == i:
                    continue
                p = ops[j]
                if p.dma:
                    op.deps_dma.add(j)
                    continue
                if p.eng == op.eng and not op.dma:
                    if op.eng == PE:
                        continue
                    if kind != "raw":
                        continue
                cur = op.deps_eng.get(p.eng, -1)
                if j > cur:
                    op.deps_eng[p.eng] = j
            for k in op.reads:
                readers.setdefault(k, []).append(i)
            for k in op.writes:
                last_w[k] = i
                readers[k] = []
        for op in ops:
            for e, j in op.deps_eng.items():
                ops[j].inc = True
        cnt = {}
        for op in ops:
            if op.dma:
                continue
            if op.inc:
                cnt[op.eng] = cnt.get(op.eng, 0) + 1
            op.ticket = cnt.get(op.eng, 0)
        slot_val = [0] * DMA_SLOTS
        k = 0
        for op in ops:
            if op.dma:
                s = k % DMA_SLOTS
                slot_val[s] += 16 * op.ndma
                op.slot = s
                op.value = slot_val[s]
                k += 1

    def start_emit(self, nc, engs, sems, dsems):
        self.mode = "emit"
        self.pos = 0
        self.n = 0
        self.engs = engs
        self.sems = sems
        self.dsems = dsems
        self.waited = {e: {} for e in engs}
        self.waited_dma = {e: {} for e in engs}
        self.dmas = []

    def emit_one(self, i, op, fn):
        ops = self.ops
        e = self.engs[op.eng]
        waited = self.waited[op.eng]
        waited_dma = self.waited_dma[op.eng]
        for pe_, j in op.deps_eng.items():
            t = ops[j].ticket
            if waited.get(pe_, 0) < t:
                e.wait_ge(self.sems[pe_], t)
                waited[pe_] = t
        for j in sorted(op.deps_dma):
            p = ops[j]
            if waited_dma.get(p.slot, 0) < p.value:
                e.wait_ge(self.dsems[p.slot], p.value)
                waited_dma[p.slot] = p.value
        if op.dma:
            insts = fn(e)
            if not isinstance(insts, (list, tuple)):
                insts = [insts]
            assert len(insts) == op.ndma
            for ins in insts:
                ins.then_inc(self.dsems[op.slot], 16)
            self.dmas.append(i)
        else:
            ins = fn(e)
            if op.inc:
                ins.then_inc(self.sems[op.eng], 1)

    def finish_emit(self):
        assert self.pos == len(self.ops), (self.pos, len(self.ops))
        for i in self.dmas:
            op = self.ops[i]
            wd = self.waited_dma[op.eng]
            if wd.get(op.slot, 0) < op.value:
                self.engs[op.eng].wait_ge(self.dsems[op.slot], op.value)
                wd[op.slot] = op.value


def _w_in_perm():
    s_q, s_k, s_v = 0, 512, 640
    s_b, s_c, s_h = 768, 1280, 1792
    perm = []
    for j in range(4):
        perm += list(range(s_q + j * 64, s_q + j * 64 + 64))
        perm += list(range(s_q + (4 + j) * 64, s_q + (4 + j) * 64 + 64))
    perm += list(range(s_k, s_k + 128))
    perm += list(range(s_v, s_v + 128))
    for i in range(4):
        perm += list(range(s_c + i * 128, s_c + (i + 1) * 128))
        perm += list(range(s_h + i * 128, s_h + (i + 1) * 128))
        perm += list(range(s_b + i * 128, s_b + (i + 1) * 128))
    return np.array(perm, dtype=np.int64)


def _rope_tables(first_pos):
    n = NBLK * 128
    pos = (first_pos + np.arange(n)).astype(np.float32)
    inv_freq = np.power(np.float32(THETA), -np.arange(0, ROT, 2, dtype=np.float32) / np.float32(ROT)).astype(np.float32)
    ang = (pos[:, None] * inv_freq[None, :]).astype(np.float32)
    cos = np.cos(ang).astype(np.float32).T
    sin = np.sin(ang).astype(np.float32).T
    C = np.ones((128, n), np.float32)
    S = np.zeros((128, n), np.float32)
    for a in range(2):
        b0 = 64 * a
        C[b0:b0 + 8] = cos
        C[b0 + 8:b0 + 16] = cos
        S[b0:b0 + 8] = -sin
        S[b0 + 8:b0 + 16] = sin
    return C, S


def _perm_matrix():
    P = np.zeros((128, 128), np.float32)
    for a in range(2):
        b0 = 64 * a
        for m in range(8):
            P[b0 + m + 8, b0 + m] = 1.0
            P[b0 + m, b0 + m + 8] = 1.0
    return P


def _masks(is_first_half):
    q = np.arange(128)[:, None]
    k = np.arange(128)[None, :]
    band_prev = (k > q)
    causal = (k <= q)
    allm = np.zeros((128, 128), bool)
    kvalid = (k >= 112) & np.ones((128, 1), bool)
    m = np.zeros((4, 128, 256), bool)
    if is_first_half:
        m[0, :, :128] = allm
        m[0, :, 128:] = allm
        m[1, :, :128] = allm
        m[1, :, 128:] = causal & kvalid
        m[2, :, :128] = band_prev & kvalid
        m[2, :, 128:] = causal
    else:
        m[0, :, :128] = allm
        m[0, :, 128:] = causal
        m[1, :, :128] = band_prev
        m[1, :, 128:] = causal
        m[2, :, :128] = band_prev
        m[2, :, 128:] = causal
    m[3, :, :128] = band_prev
    m[3, :, 128:] = causal
    return np.where(m, 0.0, NEG).astype(np.float32)


G_PRE, G_POST, G_MPRE, G_MPOST, G_CONV, G_CW = 0, 8, 16, 24, 32, 36
NG_L = 48
NG = 2 * NG_L


def build(tile_blocks=None, n_out_blocks=None):
    tile_blocks = TILE_BLOCKS if tile_blocks is None else tile_blocks
    nblk = sum(tile_blocks)
    n_out = nblk - 2
    nc = bass.Bass("TRN2", target_bir_lowering=False)

    def dram_in(name, shape, dt=F32):
        return nc.dram_tensor(name, list(shape), dt, kind="ExternalInput").ap()

    xin = dram_in("xin", [NBLK * 128, D])
    w_in = dram_in("w_in", [2, D, 2304])
    w_out = dram_in("w_out", [2, D, D])
    w_up = dram_in("w_up", [2, D, DFF])
    w_down = dram_in("w_down", [2, DFF, D])
    gtab_d = dram_in("gtab", [128, NG])
    gattn_d = dram_in("gattn", [2, 128, 512])
    sink_d = dram_in("sinkb", [128, 16])
    ctab_d = dram_in("ctab", [128, NBLK * 128])
    stab_d = dram_in("stab", [128, NBLK * 128])
    mask_d = dram_in("masks", [128, 4 * 256])
    cmat_d = dram_in("cmat", [128, 3 * 128])
    out_d = nc.dram_tensor("out", [max(n_out, 1) * 128, D], F32, kind="ExternalOutput").ap()
    scr = nc.dram_tensor("wscr", [2 * 12, 128, 8192], BF16, kind="Internal").ap()

    es = contextlib.ExitStack()
    with es:
        def sb(name, shape, dt):
            return es.enter_context(nc.sbuf_tensor(name, list(shape), dt))

        def ps(name, shape, dt):
            return es.enter_context(nc.psum_tensor(name, list(shape), dt))

        ring = [sb(f"ring{i}", [128, 8192], BF16) for i in range(NSLOT)]
        hT = sb("hT", [128, NCH, 512], F32)
        aT = sb("aT", [128, NCH, 512], BF16)
        sq = sb("sq", [128, NCH, 512], BF16)
        z = sb("z", [128, NCH, 512], F32)
        U = sb("U", [128, 16384], BF16)
        hid = U[:, :].rearrange("p (f t) -> p f t", f=32)
        qpre = U[:, 0:2560].rearrange("p (j t) -> p j t", j=5)
        qT = U[:, 2560:4608].rearrange("p (j t) -> p j t", j=4)
        yT = U[:, 4608:8704].rearrange("p (c t) -> p c t", c=8)
        yconv = sb("yconv", [128, 4, 512], F32)
        rt1 = [sb(f"rt1_{i}", [128, 512], F32) for i in range(2)]
        rt2 = [sb(f"rt2_{i}", [128, 512], F32) for i in range(2)]
        ctmp = rt1
        btmp = rt2
        lnv = sb("lnv", [128, 512], F32)
        rstd = sb("rstd", [128, 512], F32)
        kA = [sb(f"kA{l}", [128, 640], BF16) for l in range(2)]
        kB = [sb(f"kB{l}", [128, 640], BF16) for l in range(2)]
        Vt = [sb(f"Vt{l}", [128, 5, 128], BF16) for l in range(2)]
        Ut = [sb(f"Ut{l}", [128, 4, 520], BF16) for l in range(2)]
        Pm = [sb(f"Pm{i}", [128, 4, 256], BF16) for i in range(2)]
        PTs = [sb(f"PTs{i}", [128, 8, 128], BF16) for i in range(2)]
        yats = [sb(f"yat{i}", [128, 512], F32) for i in range(2)]
        yans = [sb(f"yan{i}", [128, 512], BF16) for i in range(2)]
        r2b = sb("r2b", [128, 512], F32)
        st = [sb(f"stt{i}", [128, 32], F32) for i in range(4)]
        st2 = sb("st2", [128, 8], F32)
        rtk = sb("rtk", [128, 8], F32)
        xs = [sb(f"xs{i}", [128, D], F32) for i in range(2)]
        ctab = sb("ctab_s", [128, 512], F32)
        stab = sb("stab_s", [128, 512], F32)
        gtab = sb("gtab_s", [128, NG], F32)
        gattn = sb("gattn_s", [128, 2, 512], F32)
        sinkb = sb("sinkb_s", [128, 16], F32)
        negsink = sb("negsink", [128, 16], F32)
        epsb = sb("epsb", [128, 1], F32)
        maskb = sb("maskb", [128, 4, 256], BF16)
        cmat = sb("cmat_s", [128, 3, 128], BF16)
        identf = sb("identf", [128, 128], F32)
        diag = sb("diag", [128, 24, 128], BF16)
        ident = cmat[:, 0, :]
        perm = cmat[:, 1, :]
        ones = cmat[:, 2, :]

        psG = [ps(f"psG{i}", [128, 512], F32) for i in range(2)]
        psS = [ps(f"psS{i}", [128, 1024], F32) for i in range(2)]
        psPT = ps("psPT", [128, 1024], BF16)
        psO = ps("psO", [128, 512], F32)

        sem_names = [PE, ACT, DVE, POOL]
        sems = {e: es.enter_context(nc.semaphore(f"s_{e}")) for e in sem_names}
        dsems = [es.enter_context(nc.semaphore(f"d{i}")) for i in range(DMA_SLOTS)]
        engs = {PE: nc.tensor, ACT: nc.scalar, DVE: nc.vector, POOL: nc.gpsimd, SP: nc.sync}

        def program(P):
            gctr = [0]

            G2 = [(psG[0], "psG0"), (psG[1], "psG1")]
            G6 = G2 + [(psS[0][:, 0:512], "psS0a"), (psS[0][:, 512:1024], "psS0b"),
                       (psS[1][:, 0:512], "psS1a"), (psS[1][:, 512:1024], "psS1b")]
            gpool = [G6]

            def psg():
                pool = gpool[0]
                i = gctr[0] % len(pool)
                gctr[0] += 1
                return pool[i]

            sqj = sq[:, 7, :]
            P.add(SP, lambda e: e.dma_start(out=gtab[:], in_=gtab_d[:, :]), writes=["gtab"], dma=True)
            P.add(SP, lambda e: e.dma_start(out=gattn[:], in_=gattn_d.rearrange("l p f -> p l f")), writes=["gattn"], dma=True)
            P.add(SP, lambda e: e.dma_start(out=sinkb[:], in_=sink_d[:, :]), writes=["sinkb"], dma=True)
            P.add(SP, lambda e: e.dma_start(out=identf[:], in_=cmat_d[:, 0:128]), writes=["identf"], dma=True)
            P.add(POOL, lambda e: e.dma_start(out=cmat[:], in_=cmat_d.rearrange("p (a n) -> p a n", a=3)), writes=["cmat"], dma=True)
            P.add(POOL, lambda e: e.dma_start(out=maskb[:], in_=mask_d.rearrange("p (a n) -> p a n", a=4)), writes=["maskb"], dma=True)
            P.add(POOL, lambda e: e.memset(epsb[:], EPS), writes=["epsb"])
            P.add(POOL, lambda e: e.tensor_scalar(out=negsink[:], in0=sinkb[:], scalar1=-1.0, scalar2=None, op0=ALU.mult),
                  reads=["sinkb"], writes=["negsink"])
            for l in range(2):
                P.add(POOL, lambda e, l=l: e.memset(kA[l][:], 0.0), writes=[f"kA{l}"])
                P.add(POOL, lambda e, l=l: e.memset(kB[l][:], 0.0), writes=[f"kB{l}"])
                P.add(POOL, lambda e, l=l: e.memset(Vt[l][:], 0.0), writes=[f"Vt{l}"])
                P.add(POOL, lambda e, l=l: e.memset(Ut[l][:], 0.0), writes=[f"Ut{l}"])
                for i in range(4):
                    for k in range(3):
                        col = l * NG_L + G_CW + k * 4 + i
                        P.add(POOL, lambda e, l=l, i=i, k=k, col=col: e.tensor_scalar(
                            out=diag[:, l * 12 + i * 3 + k, :], in0=identf[:], scalar1=gtab[:, col:col + 1],
                            scalar2=None, op0=ALU.mult),
                            reads=["identf", "gtab"], writes=[f"diag{l}"])

            ring_ctr = [0]
            PIECES = ["in0", "in1", "in2", "out", "up0", "up1", "up2", "up3", "dn0", "dn1", "dn2", "dn3"]

            def piece_src(l, name):
                if name.startswith("in"):
                    j = int(name[2:])
                    src = w_in[l, :, j * 768:(j + 1) * 768].rearrange("(kc p) n -> p kc n", p=128)
                    return src, (8, 768)
                if name == "out":
                    return w_out[l].rearrange("(kc p) n -> p kc n", p=128), (8, 1024)
                if name.startswith("up"):
                    j = int(name[2:])
                    return w_up[l, :, j * 1024:(j + 1) * 1024].rearrange("(kc p) n -> p kc n", p=128), (8, 1024)
                j = int(name[2:])
                return w_down[l, :, j * 256:(j + 1) * 256].rearrange("(kc p) n -> p kc n", p=128), (32, 256)

            SEQ = [(ti_, l_, nm_) for ti_ in range(len(tile_blocks)) for l_ in range(2) for nm_ in PIECES]
            issued = [0]
            LA = 2

            def emit_load(n):
                ti, l, name = SEQ[n]
                s = n % NSLOT
                src, (a, n_) = piece_src(l, name)
                view = ring[s][:, 0:a * n_].rearrange("p (a n) -> p a n", a=a)
                pi = l * 12 + PIECES.index(name)
                key = f"ring{s}"
                skey = f"scr{pi}"
                if ti == 0:
                    q = a // 4

                    def f(e):
                        return [e.dma_start(out=view[:, i * q:(i + 1) * q, :], in_=src[:, i * q:(i + 1) * q, :])
                                for i in range(4)]
                    P.add(POOL, f, writes=[key], dma=True, ndma=4)
                    P.add(SP, lambda e: e.dma_start(out=scr[pi, :, 0:a * n_], in_=ring[s][:, 0:a * n_]),
                          reads=[key], writes=[skey], dma=True)
                else:
                    def f(e):
                        h = (a * n_) // 2
                        return [e.dma_start(out=ring[s][:, i * h:(i + 1) * h], in_=scr[pi, :, i * h:(i + 1) * h])
                                for i in range(2)]
                    P.add(SP, f, reads=[skey], writes=[key], dma=True, ndma=2)

            def load_piece(ti, l, name):
                n = ring_ctr[0]
                ring_ctr[0] += 1
                assert SEQ[n] == (ti, l, name), (SEQ[n], ti, l, name)
                while issued[0] < min(n + 1 + LA, len(SEQ)):
                    emit_load(issued[0])
                    issued[0] += 1
                s = n % NSLOT
                _, (a, n_) = piece_src(l, name)
                view = ring[s][:, 0:a * n_].rearrange("p (a n) -> p a n", a=a)
                return view, f"ring{s}"

            def rms_stats(T, nchunks, inv_n, sqkeys):
                pt, pk = psg()
                for c in range(nchunks):
                    P.add(PE, lambda e, c=c, pt=pt: e.matmul(pt[:, 0:T], lhsT=ones, rhs=sq[:, c, 0:T],
                                                             start=(c == 0), stop=(c == nchunks - 1)),
                          reads=["cmat", sqkeys[c]], writes=[pk])
                P.add(ACT, lambda e, pt=pt: e.activation(out=lnv[:, 0:T], in_=pt[:, 0:T], func=AF.Ln,
                                                         bias=epsb[:, 0:1], scale=inv_n),
                      reads=[pk, "epsb"], writes=["lnv"])
                P.add(ACT, lambda e: e.activation(out=rstd[:, 0:T], in_=lnv[:, 0:T], func=AF.Exp, scale=-0.5),
                      reads=["lnv"], writes=["rstd"])

            def pre_norm(T, l, gcol):
                for c in range(NCH):
                    if c % 2 == 0:
                        P.add(ACT, lambda e, c=c: e.activation(out=sq[:, c, 0:T], in_=hT[:, c, 0:T], func=AF.Square),
                              reads=[f"h{c}"], writes=[f"sq{c}"])
                    else:
                        P.add(DVE, lambda e, c=c: e.tensor_tensor(out=sq[:, c, 0:T], in0=hT[:, c, 0:T], in1=hT[:, c, 0:T],
                                                                   op=ALU.mult),
                              reads=[f"h{c}"], writes=[f"sq{c}"])
                rms_stats(T, NCH, 1.0 / D, [f"sq{c}" for c in range(NCH)])
                for c in range(NCH):
                    col = l * NG_L + gcol + c
                    eng = DVE
                    P.add(eng, lambda e, c=c, col=col: e.scalar_tensor_tensor(
                        out=aT[:, c, 0:T], in0=hT[:, c, 0:T], scalar=gtab[:, col:col + 1], in1=rstd[:, 0:T],
                        op0=ALU.mult, op1=ALU.mult),
                        reads=[f"h{c}", "gtab", "rstd"], writes=[f"a{c}"])

            def stat_mm(T, c):
                P.add(PE, lambda e: e.matmul(psO[:, 0:T], lhsT=ones, rhs=sq[:, c, 0:T], start=(c == 0), stop=(c == NCH - 1)),
                      reads=["cmat", f"sq{c}"], writes=["psO"])

            def pre_norm_deferred(T, l, gcol):
                for c in range(NCH):
                    col = l * NG_L + gcol + c
                    P.add(ACT, lambda e, c=c, col=col: e.activation(out=aT[:, c, 0:T], in_=hT[:, c, 0:T], func=AF.Copy,
                                                                    scale=gtab[:, col:col + 1]),
                          reads=[f"h{c}", "gtab"], writes=[f"a{c}"])
                for c in range(NCH):
                    P.add(ACT, lambda e, c=c: e.activation(out=sq[:, c, 0:T], in_=hT[:, c, 0:T], func=AF.Square),
                          reads=[f"h{c}"], writes=[f"sq{c}"])
                pt, pk = psg()
                for c in range(NCH):
                    P.add(PE, lambda e, c=c, pt=pt: e.matmul(pt[:, 0:T], lhsT=ones, rhs=sq[:, c, 0:T],
                                                             start=(c == 0), stop=(c == NCH - 1)),
                          reads=["cmat", f"sq{c}"], writes=[pk])
                P.add(ACT, lambda e, pt=pt: e.activation(out=r2b[:, 0:T], in_=pt[:, 0:T], func=AF.Ln, bias=epsb[:, 0:1], scale=1.0 / D),
                      reads=[pk, "epsb"], writes=["r2b"])
                P.add(ACT, lambda e: e.activation(out=r2b[:, 0:T], in_=r2b[:, 0:T], func=AF.Exp, scale=-1.0),
                      reads=["r2b"], writes=["r2b"])

            def post_norm_update(T, l, deferred=False):
                if not deferred:
                    P.add(ACT, lambda e: e.activation(out=lnv[:, 0:T], in_=psO[:, 0:T], func=AF.Ln, bias=epsb[:, 0:1], scale=1.0 / D),
                          reads=["psO", "epsb"], writes=["lnv"])
                    P.add(ACT, lambda e: e.activation(out=rstd[:, 0:T], in_=lnv[:, 0:T], func=AF.Exp, scale=-0.5),
                          reads=["lnv"], writes=["rstd"])
                else:
                    P.add(DVE, lambda e: e.tensor_tensor(out=lnv[:, 0:T], in0=psO[:, 0:T], in1=r2b[:, 0:T], op=ALU.mult),
                          reads=["psO", "r2b"], writes=["lnv"])
                    P.add(DVE, lambda e: e.tensor_tensor(out=lnv[:, 0:T], in0=lnv[:, 0:T], in1=r2b[:, 0:T], op=ALU.mult),
                          reads=["lnv", "r2b"], writes=["lnv"])
                    P.add(ACT, lambda e: e.activation(out=lnv[:, 0:T], in_=lnv[:, 0:T], func=AF.Ln, bias=epsb[:, 0:1], scale=1.0 / D),
                          reads=["lnv", "epsb"], writes=["lnv"])
                    P.add(ACT, lambda e: e.activation(out=lnv[:, 0:T], in_=lnv[:, 0:T], func=AF.Exp, scale=-0.5),
                          reads=["lnv"], writes=["lnv"])
                    P.add(DVE, lambda e: e.tensor_tensor(out=rstd[:, 0:T], in0=lnv[:, 0:T], in1=r2b[:, 0:T], op=ALU.mult),
                          reads=["lnv", "r2b"], writes=["rstd"])
                for c in range(NCH):
                    P.add(DVE, lambda e, c=c: e.tensor_tensor(out=z[:, c, 0:T], in0=z[:, c, 0:T], in1=rstd[:, 0:T], op=ALU.mult),
                          reads=[f"z{c}", "rstd"], writes=[f"z{c}"])
                    P.add(DVE, lambda e, c=c: e.tensor_tensor(out=hT[:, c, 0:T], in0=hT[:, c, 0:T], in1=z[:, c, 0:T], op=ALU.add),
                          reads=[f"z{c}", f"h{c}"], writes=[f"h{c}"])

            def branch_evac(T, l, pt, pk, d, gcol):
                col = l * NG_L + gcol + d
                P.add(ACT, lambda e, d=d, pt=pt: e.activation(out=sq[:, d, 0:T], in_=pt[:, 0:T], func=AF.Square),
                      reads=[pk], writes=[f"sq{d}"])
                P.add(DVE, lambda e, d=d, pt=pt, col=col: e.tensor_scalar(out=z[:, d, 0:T], in0=pt[:, 0:T],
                                                                          scalar1=gtab[:, col:col + 1], scalar2=None, op0=ALU.mult),
                      reads=[pk, "gtab"], writes=[f"z{d}"])

            blk0 = 0
            for ti, nb in enumerate(tile_blocks):
                T = nb * 128
                tok0 = blk0 * 128
                P.add(SP, lambda e, tok0=tok0, T=T: e.dma_start(out=ctab[:, 0:T], in_=ctab_d[:, tok0:tok0 + T]),
                      writes=["ctab"], dma=True)
                P.add(SP, lambda e, tok0=tok0, T=T: e.dma_start(out=stab[:, 0:T], in_=stab_d[:, tok0:tok0 + T]),
                      writes=["stab"], dma=True)
                def x_load(tok_base, b):
                    i = b % 2
                    r0 = tok_base + b * 128

                    def f(e):
                        return [e.dma_start(out=rt1[i][:], in_=xin[r0:r0 + 128, 0:512]),
                                e.dma_start(out=rt2[i][:], in_=xin[r0:r0 + 128, 512:1024])]
                    P.add(SP, f, writes=[f"rt1_{i}", f"rt2_{i}"], dma=True, ndma=2)

                def x_transpose(b):
                    i = b % 2
                    pS = psS[b % 2]
                    pSk = f"psS{b % 2}"
                    for c in range(NCH):
                        src = rt1[i] if c < 4 else rt2[i]
                        sk = f"rt1_{i}" if c < 4 else f"rt2_{i}"
                        cc = c % 4
                        P.add(PE, lambda e, c=c, cc=cc, src=src: e.transpose(out=pS[:, c * 128:(c + 1) * 128],
                                                                             in_=src[:, cc * 128:(cc + 1) * 128], identity=identf[:]),
                              reads=[sk, "identf"], writes=[pSk + ("a" if c < 4 else "b")])
                    for hlf in range(2):
                        eng = ACT if hlf == 0 else DVE
                        if eng == ACT:
                            fn = lambda e, hlf=hlf: e.activation(
                                out=hT[:, hlf * 4:(hlf + 1) * 4, b * 128:(b + 1) * 128],
                                in_=pS[:, hlf * 512:(hlf + 1) * 512].rearrange("p (c t) -> p c t", c=4), func=AF.Copy)
                        else:
                            fn = lambda e, hlf=hlf: e.tensor_copy(
                                out=hT[:, hlf * 4:(hlf + 1) * 4, b * 128:(b + 1) * 128],
                                in_=pS[:, hlf * 512:(hlf + 1) * 512].rearrange("p (c t) -> p c t", c=4))
                        P.add(eng, fn, reads=[pSk + "ab"[hlf]], writes=[f"h{c}" for c in range(hlf * 4, hlf * 4 + 4)])

                if ti == 0:
                    for b in range(min(2, nb)):
                        x_load(tok0, b)
                for b in range(nb):
                    x_transpose(b)
                    if b + 2 < nb:
                        x_load(tok0, b + 2)

                for l in range(2):
                    for c in range(NCH):
                        col = l * NG_L + G_PRE + c
                        P.add(ACT, lambda e, c=c, col=col: e.activation(out=aT[:, c, 0:T], in_=hT[:, c, 0:T], func=AF.Copy,
                                                                        scale=gtab[:, col:col + 1]),
                              reads=[f"h{c}", "gtab"], writes=[f"a{c}"])
                    for c in range(NCH):
                        P.add(ACT, lambda e, c=c: e.activation(out=sq[:, c, 0:T], in_=hT[:, c, 0:T], func=AF.Square),
                              reads=[f"h{c}"], writes=[f"sq{c}"])
                    akeys = [f"a{c}" for c in range(NCH)]
                    wv0, wk0 = load_piece(ti, l, "in0")
                    for j in range(5):
                        pt, pk = psg()
                        for kc in range(NCH):
                            P.add(PE, lambda e, j=j, kc=kc, pt=pt: e.matmul(
                                pt[:, 0:T], lhsT=wv0[:, kc, j * 128:(j + 1) * 128], rhs=aT[:, kc, 0:T],
                                start=(kc == 0), stop=(kc == NCH - 1)),
                                reads=[wk0, f"a{kc}"], writes=[pk])
                        P.add(ACT, lambda e, j=j, pt=pt: e.activation(out=qpre[:, j, 0:T], in_=pt[:, 0:T], func=AF.Copy),
                              reads=[pk], writes=[f"qpre{j}", "Umix"])
                    pt, pk = psg()
                    for b in range(nb):
                        for kc in range(NCH):
                            P.add(PE, lambda e, b=b, kc=kc, pt=pt: e.matmul(
                                pt[:, b * 128:(b + 1) * 128], lhsT=aT[:, kc, b * 128:(b + 1) * 128], rhs=wv0[:, kc, 640:768],
                                start=(kc == 0), stop=(kc == NCH - 1)),
                                reads=[wk0, f"a{kc}"], writes=[pk])
                    ptv, pkv = pt, pk
                    pt, pk = psg()
                    for c in range(NCH):
                        P.add(PE, lambda e, c=c, pt=pt: e.matmul(pt[:, 0:T], lhsT=ones, rhs=sq[:, c, 0:T],
                                                                 start=(c == 0), stop=(c == NCH - 1)),
                              reads=["cmat", f"sq{c}"], writes=[pk])
                    P.add(ACT, lambda e, pt=pt: e.activation(out=r2b[:, 0:T], in_=pt[:, 0:T], func=AF.Ln, bias=epsb[:, 0:1], scale=1.0 / D),
                          reads=[pk, "epsb"], writes=["r2b"])
                    P.add(ACT, lambda e: e.activation(out=rstd[:, 0:T], in_=r2b[:, 0:T], func=AF.Exp, scale=-0.5),
                          reads=["r2b"], writes=["rstd"])
                    P.add(ACT, lambda e: e.activation(out=r2b[:, 0:T], in_=r2b[:, 0:T], func=AF.Exp, scale=-1.0),
                          reads=["r2b"], writes=["r2b"])
                    pt2, pk2 = psg()
                    for b in range(nb):
                        for c in range(NCH):
                            P.add(PE, lambda e, b=b, c=c, pt2=pt2: e.matmul(pt2[:, b:b + 1], lhsT=sq[:, c, b * 128:(b + 1) * 128], rhs=ones[:, 0:1],
                                                                         start=(c == 0), stop=(c == NCH - 1)),
                                  reads=["cmat", f"sq{c}"], writes=[pk2])
                    P.add(ACT, lambda e, pt2=pt2: e.activation(out=rtk[:, 0:nb], in_=pt2[:, 0:nb], func=AF.Ln, bias=epsb[:, 0:1], scale=1.0 / D),
                          reads=[pk2, "epsb"], writes=["rtok"])
                    P.add(ACT, lambda e: e.activation(out=rtk[:, 0:nb], in_=rtk[:, 0:nb], func=AF.Exp, scale=-0.5),
                          reads=["rtok"], writes=["rtok"])
                    for b in range(nb):
                        P.add(ACT, lambda e, b=b: e.activation(out=Vt[l][:, 1 + b, :], in_=ptv[:, b * 128:(b + 1) * 128], func=AF.Copy,
                                                               scale=rtk[:, b:b + 1]),
                              reads=[pkv, "rtok"], writes=[f"Vt{l}"])
                    Cs = yconv[:, 0, :]
                    Ss = yconv[:, 1, :]
                    P.add(DVE, lambda e: e.tensor_tensor(out=Cs[:, 0:T], in0=ctab[:, 0:T], in1=rstd[:, 0:T], op=ALU.mult),
                          reads=["ctab", "rstd"], writes=["yconv0"])
                    P.add(DVE, lambda e: e.tensor_tensor(out=Ss[:, 0:T], in0=stab[:, 0:T], in1=rstd[:, 0:T], op=ALU.mult),
                          reads=["stab", "rstd"], writes=["yconv1"])
                    for j in (4, 0, 1, 2, 3):
                        pr, prk = psg()
                        P.add(PE, lambda e, j=j, pr=pr: e.matmul(pr[:, 0:T], lhsT=perm, rhs=qpre[:, j, 0:T], start=True, stop=True),
                              reads=["cmat", f"qpre{j}"], writes=[prk])
                        r1 = rt1[j % 2]
                        r2 = rt2[j % 2]
                        P.add(DVE, lambda e, j=j, r1=r1: e.tensor_tensor(out=r1[:, 0:T], in0=qpre[:, j, 0:T], in1=Cs[:, 0:T], op=ALU.mult),
                              reads=[f"qpre{j}", "yconv0"], writes=[f"rt1_{j % 2}"])
                        P.add(DVE, lambda e, pr=pr, r2=r2: e.tensor_tensor(out=r2[:, 0:T], in0=pr[:, 0:T], in1=Ss[:, 0:T], op=ALU.mult),
                              reads=[prk, "yconv1"], writes=[f"rt2_{j % 2}"])
                        if j < 4:
                            P.add(DVE, lambda e, j=j, r1=r1, r2=r2: e.tensor_tensor(out=qT[:, j, 0:T], in0=r1[:, 0:T], in1=r2[:, 0:T], op=ALU.add),
                                  reads=[f"rt1_{j % 2}", f"rt2_{j % 2}"], writes=[f"qT{j}", "Umix"])
                        else:
                            P.add(DVE, lambda e, r1=r1, r2=r2: e.tensor_tensor(out=kA[l][0:64, 128:128 + T], in0=r1[0:64, 0:T], in1=r2[0:64, 0:T], op=ALU.add),
                                  reads=[f"rt1_{j % 2}", f"rt2_{j % 2}"], writes=[f"kA{l}"])
                            P.add(DVE, lambda e, r1=r1, r2=r2: e.tensor_tensor(out=kB[l][64:128, 128:128 + T], in0=r1[64:128, 0:T], in1=r2[64:128, 0:T], op=ALU.add),
                                  reads=[f"rt1_{j % 2}", f"rt2_{j % 2}"], writes=[f"kB{l}"])

                    def conv_gen():
                        wv = wk = None
                        for i in range(4):
                            if i % 2 == 0:
                                wv, wk = load_piece(ti, l, f"in{1 + i // 2}")
                            base = (i % 2) * 384
                            ct = ctmp[i % 2]
                            bt = btmp[i % 2]
                            pt, pk = psg()
                            for kc in range(NCH):
                                P.add(PE, lambda e, kc=kc, pt=pt, wv=wv, base=base: e.matmul(
                                    pt[:, 0:T], lhsT=wv[:, kc, base:base + 128], rhs=aT[:, kc, 0:T], start=(kc == 0), stop=(kc == NCH - 1)),
                                    reads=[wk, f"a{kc}"], writes=[pk])
                            P.add(DVE, lambda e, pt=pt, ct=ct: e.tensor_tensor(out=ct[:, 0:T], in0=pt[:, 0:T], in1=r2b[:, 0:T], op=ALU.mult),
                                  reads=[pk, "r2b"], writes=[f"rt1_{i % 2}"])
                            yield
                            pt, pk = psg()
                            for kc in range(NCH):
                                P.add(PE, lambda e, kc=kc, pt=pt, wv=wv, base=base: e.matmul(
                                    pt[:, 0:T], lhsT=wv[:, kc, base + 128:base + 256], rhs=aT[:, kc, 0:T], start=(kc == 0), stop=(kc == NCH - 1)),
                                    reads=[wk, f"a{kc}"], writes=[pk])
                            P.add(DVE, lambda e, pt=pt, ct=ct, i=i: e.tensor_tensor(out=Ut[l][:, i, 2:2 + T], in0=pt[:, 0:T], in1=ct[:, 0:T], op=ALU.mult),
                                  reads=[pk, f"rt1_{i % 2}"], writes=[f"Ut{l}_{i}"])
                            yield
                            pt, pk = psg()
                            for kc in range(NCH):
                                P.add(PE, lambda e, kc=kc, pt=pt, wv=wv, base=base: e.matmul(
                                    pt[:, 0:T], lhsT=wv[:, kc, base + 256:base + 384], rhs=aT[:, kc, 0:T], start=(kc == 0), stop=(kc == NCH - 1)),
                                    reads=[wk, f"a{kc}"], writes=[pk])
                            P.add(DVE, lambda e, pt=pt, bt=bt: e.tensor_tensor(out=bt[:, 0:T], in0=pt[:, 0:T], in1=rstd[:, 0:T], op=ALU.mult),
                                  reads=[pk, "rstd"], writes=[f"rt2_{i % 2}"])
                            pt, pk = psg()
                            for k in range(3):
                                P.add(PE, lambda e, k=k, pt=pt, i=i: e.matmul(
                                    pt[:, 0:T], lhsT=diag[:, l * 12 + i * 3 + k, :], rhs=Ut[l][:, i, k:k + T], start=(k == 0), stop=(k == 2)),
                                    reads=[f"diag{l}", f"Ut{l}_{i}"], writes=[pk])
                            P.add(DVE, lambda e, pt=pt, bt=bt, i=i: e.tensor_tensor(out=yconv[:, i, 0:T], in0=pt[:, 0:T], in1=bt[:, 0:T], op=ALU.mult),
                                  reads=[pk, f"rt2_{i % 2}"], writes=[f"yconv{i}"])
                            P.add(ACT, lambda e, i=i: e.activation(out=sq[:, i, 0:T], in_=yconv[:, i, 0:T], func=AF.Square),
                                  reads=[f"yconv{i}"], writes=[f"sq{i}"])
                            P.add(POOL, lambda e, i=i: e.tensor_copy(out=Ut[l][:, i, 0:2], in_=Ut[l][:, i, T:T + 2]),
                                  reads=[f"Ut{l}_{i}"], writes=[f"Ut{l}_{i}"])
                            yield
                        rms_stats(T, 4, 1.0 / 512, [f"sq{i}" for i in range(4)])
                        for i in range(4):
                            col = l * NG_L + G_CONV + i
                            P.add(DVE, lambda e, i=i, col=col: e.scalar_tensor_tensor(
                                out=yT[:, 4 + i, 0:T], in0=yconv[:, i, 0:T], scalar=gtab[:, col:col + 1], in1=rstd[:, 0:T],
                                op0=ALU.mult, op1=ALU.mult),
                                reads=[f"yconv{i}", "gtab", "rstd"], writes=[f"yT{4 + i}", "Umix"])
                        yield

                    def unit(u):
                        b, g = u // 2, u % 2
                        return b, g, psS[u % 2], f"psS{u % 2}", st[u % 4], f"st{u % 4}", Pm[u % 2], f"Pm{u % 2}", PTs[u % 2], f"PTs{u % 2}"

                    def stage_A(u):
                        b, g, pS, pSk, sg, sgk, Pg, Pk, PTg, PTk = unit(u)
                        mv = min(blk0 + b, 3)
                        k0 = b * 128
                        kbuf = kA[l] if g == 0 else kB[l]
                        kkey = f"kA{l}" if g == 0 else f"kB{l}"
                        for j in range(4):
                            P.add(PE, lambda e, j=j: e.matmul(
                                pS[:, j * 256:(j + 1) * 256], lhsT=qT[:, j, b * 128:(b + 1) * 128], rhs=kbuf[:, k0:k0 + 256],
                                start=True, stop=False),
                                reads=[f"qT{j}", kkey], writes=[pSk + "ab"[j // 2]])
                            P.add(PE, lambda e, j=j: e.matmul(
                                pS[:, j * 256:(j + 1) * 256], lhsT=ident, rhs=maskb[:, mv, :], start=False, stop=True),
                                reads=["cmat", "maskb"], writes=[pSk + "ab"[j // 2]])
                        P.add(DVE, lambda e: e.reduce_max(out=sg[:, 0:4], in_=pS[:, :].rearrange("p (h k) -> p h k", h=4), axis=AX.X),
                              reads=[pSk + "a", pSk + "b"], writes=[sgk])
                        P.add(DVE, lambda e: e.scalar_tensor_tensor(
                            out=sg[:, 4:8], in0=sg[:, 0:4], scalar=-0.125, in1=negsink[:, l * 8 + g * 4:l * 8 + g * 4 + 4],
                            op0=ALU.mult, op1=ALU.min),
                            reads=[sgk, "negsink"], writes=[sgk])
                        P.add(POOL, lambda e: e.memset(sg[:, 8:12], 0.0), writes=[sgk + f"s{j}" for j in range(4)])
                        for j in range(4):
                            P.add(ACT, lambda e, j=j: e.activation(
                                out=Pg[:, j, :], in_=pS[:, j * 256:(j + 1) * 256], func=AF.Exp,
                                bias=sg[:, 4 + j:5 + j], scale=0.125, accum_out=sg[:, 8 + j:9 + j]),
                                reads=[pSk + "ab"[j // 2], sgk, sgk + f"s{j}"], writes=[Pk + f"_{j}", sgk + f"s{j}"])

                    def stage_B2(u):
                        b, g, pS, pSk, sg, sgk, Pg, Pk, PTg, PTk = unit(u)
                        P.add(DVE, lambda e: e.tensor_tensor(out=sg[:, 12:16], in0=sg[:, 4:8],
                                                              in1=sinkb[:, l * 8 + g * 4:l * 8 + g * 4 + 4], op=ALU.add),
                              reads=[sgk, "sinkb"], writes=[sgk + "t"])
                        P.add(ACT, lambda e: e.activation(out=sg[:, 16:20], in_=sg[:, 12:16], func=AF.Exp),
                              reads=[sgk + "t"], writes=[sgk + "e"])

                    def stage_B2b(u):
                        b, g, pS, pSk, sg, sgk, Pg, Pk, PTg, PTk = unit(u)
                        P.add(DVE, lambda e: e.tensor_tensor(out=sg[:, 20:24], in0=sg[:, 8:12], in1=sg[:, 16:20], op=ALU.add),
                              reads=[sgk + f"s{j}" for j in range(4)] + [sgk + "e"], writes=[sgk + "d"])
                        P.add(DVE, lambda e: e.reciprocal(out=sg[:, 24:28], in_=sg[:, 20:24]),
                              reads=[sgk + "d"], writes=[sgk + "r"])

                    def stage_C(u):
                        b, g, pS, pSk, sg, sgk, Pg, Pk, PTg, PTk = unit(u)
                        for j in range(4):
                            for kb in range(2):
                                P.add(PE, lambda e, j=j, kb=kb: e.transpose(
                                    out=psPT[:, (j * 2 + kb) * 128:(j * 2 + kb + 1) * 128], in_=Pg[:, j, kb * 128:(kb + 1) * 128], identity=ident),
                                    reads=[Pk + f"_{j}", "cmat"], writes=["psPT"])
                        if u % 2 == 0:
                            P.add(DVE, lambda e: e.tensor_copy(out=PTg[:, :, :], in_=psPT[:, :].rearrange("p (a q) -> p a q", a=8)),
                                  reads=["psPT"], writes=[PTk])
                        else:
                            P.add(ACT, lambda e: e.activation(out=PTg[:, :, :], in_=psPT[:, :].rearrange("p (a q) -> p a q", a=8), func=AF.Copy),
                                  reads=["psPT"], writes=[PTk])

                    def stage_D(u):
                        b, g, pS, pSk, sg, sgk, Pg, Pk, PTg, PTk = unit(u)
                        for j in range(4):
                            h = g * 4 + j
                            for kb in range(2):
                                P.add(PE, lambda e, j=j, kb=kb, h=h: e.matmul(
                                    psO[:, h * 64:(h + 1) * 64], lhsT=PTg[:, j * 2 + kb, :], rhs=Vt[l][:, b + kb, g * 64:(g + 1) * 64],
                                    start=(kb == 0), stop=(kb == 1)),
                                    reads=[PTk, f"Vt{l}"], writes=["psO"])
                        for j in range(4):
                            h = g * 4 + j
                            P.add(DVE, lambda e, j=j, h=h: e.tensor_scalar(
                                out=yats[b % 2][:, h * 64:(h + 1) * 64], in0=psO[:, h * 64:(h + 1) * 64], scalar1=sg[:, 24 + j:25 + j],
                                scalar2=None, op0=ALU.mult),
                                reads=["psO", sgk + "r"], writes=[f"yat{b % 2}"])
                        if g == 1:
                            stage_E1(b)

                    def stage_E1(b):
                        yat = yats[b % 2]
                        yk = f"yat{b % 2}"
                        yan = yans[b % 2]
                        ynk = f"yan{b % 2}"
                        s2 = st2[:, (b % 2) * 4:(b % 2) * 4 + 4]
                        s2k = f"st2_{b % 2}"
                        P.add(POOL, lambda e: e.memset(s2[:, 0:1], 0.0), writes=[s2k])
                        P.add(ACT, lambda e: e.activation(out=sqj[:], in_=yat[:], func=AF.Square, accum_out=s2[:, 0:1]),
                              reads=[yk, s2k], writes=["sq7", s2k])
                        P.add(ACT, lambda e: e.activation(out=s2[:, 1:2], in_=s2[:, 0:1], func=AF.Ln, bias=epsb[:, 0:1], scale=1.0 / 512),
                              reads=[s2k, "epsb"], writes=[s2k + "a"])
                        P.add(ACT, lambda e: e.activation(out=s2[:, 2:3], in_=s2[:, 1:2], func=AF.Exp, scale=-0.5),
                              reads=[s2k + "a"], writes=[s2k + "b"])
                        P.add(DVE, lambda e: e.scalar_tensor_tensor(out=yan[:], in0=yat[:], scalar=s2[:, 2:3], in1=gattn[:, l, :],
                                                                    op0=ALU.mult, op1=ALU.mult),
                              reads=[yk, s2k + "b", "gattn"], writes=[ynk])

                    def stage_E2(b):
                        yan = yans[b % 2]
                        ynk = f"yan{b % 2}"
                        pt, pk = psg()
                        ptb = pt[:, 0:256].bitcast(BF16)
                        for c in range(4):
                            P.add(PE, lambda e, c=c: e.transpose(out=ptb[:, c * 128:(c + 1) * 128], in_=yan[:, c * 128:(c + 1) * 128], identity=ident),
                                  reads=[ynk, "cmat"], writes=[pk])
                        P.add(ACT, lambda e: e.activation(out=yT[:, 0:4, b * 128:(b + 1) * 128],
                                                          in_=ptb[:, 0:512].rearrange("p (c q) -> p c q", c=4), func=AF.Copy),
                              reads=[pk], writes=[f"yT{c}" for c in range(4)] + ["Umix"])

                    cg = conv_gen()
                    nun = 2 * nb
                    gpool[0] = G2

                    def filler(n=1):
                        for _ in range(n):
                            next(cg, None)

                    for step in range(nun + 2):
                        if 1 <= step <= nun:
                            stage_B2(step - 1)
                        if step < nun:
                            stage_A(step)
                        if 1 <= step <= nun:
                            stage_B2b(step - 1)
                        filler(1)
                        if 1 <= step <= nun:
                            stage_C(step - 1)
                        if step >= 2:
                            stage_D(step - 2)
                        if step % 2 == 1:
                            filler(1)
                        if step >= 4 and step % 2 == 0:
                            stage_E2((step - 4) // 2)
                    stage_E2(nb - 1)
                    for _ in cg:
                        pass
                    gpool[0] = G6
                    P.add(POOL, lambda e: e.tensor_copy(out=kA[l][0:64, 0:128], in_=kA[l][0:64, T:T + 128]), reads=[f"kA{l}"], writes=[f"kA{l}"])
                    P.add(POOL, lambda e: e.tensor_copy(out=kB[l][64:128, 0:128], in_=kB[l][64:128, T:T + 128]), reads=[f"kB{l}"], writes=[f"kB{l}"])
                    P.add(POOL, lambda e: e.tensor_copy(out=Vt[l][:, 0, :], in_=Vt[l][:, nb, :]), reads=[f"Vt{l}"], writes=[f"Vt{l}"])


                    wv, wk = load_piece(ti, l, "out")
                    for d in range(NCH):
                        pt, pk = psg()
                        for ki, kc in enumerate((4, 5, 6, 7, 0, 1, 2, 3)):
                            P.add(PE, lambda e, d=d, kc=kc, ki=ki, pt=pt, wv=wv: e.matmul(
                                pt[:, 0:T], lhsT=wv[:, kc, d * 128:(d + 1) * 128], rhs=yT[:, kc, 0:T], start=(ki == 0), stop=(ki == NCH - 1)),
                                reads=[wk, f"yT{kc}"], writes=[pk])
                        branch_evac(T, l, pt, pk, d, G_POST)
                        if d >= 1:
                            stat_mm(T, d - 1)
                    stat_mm(T, NCH - 1)
                    post_norm_update(T, l)

                    pre_norm_deferred(T, l, G_MPRE)
                    for pj in range(4):
                        wv, wk = load_piece(ti, l, f"up{pj}")
                        for fi in range(8):
                            f = pj * 8 + fi
                            pt, pk = psg()
                            for kc in range(NCH):
                                P.add(PE, lambda e, fi=fi, kc=kc, pt=pt, wv=wv: e.matmul(
                                    pt[:, 0:T], lhsT=wv[:, kc, fi * 128:(fi + 1) * 128], rhs=aT[:, kc, 0:T], start=(kc == 0), stop=(kc == NCH - 1)),
                                    reads=[wk, f"a{kc}"], writes=[pk])
                            if f % 2 == 0:
                                P.add(ACT, lambda e, f=f, pt=pt: e.activation(out=hid[:, f, 0:T], in_=pt[:, 0:T], func=AF.Relu),
                                      reads=[pk], writes=[f"hid{f}", "Umlp"])
                                P.add(ACT, lambda e, f=f: e.activation(out=hid[:, f, 0:T], in_=hid[:, f, 0:T], func=AF.Square),
                                      reads=[f"hid{f}"], writes=[f"hid{f}", "Umlp"])
                            else:
                                P.add(DVE, lambda e, f=f, pt=pt: e.tensor_scalar(out=hid[:, f, 0:T], in0=pt[:, 0:T], scalar1=0.0, scalar2=None, op0=ALU.max),
                                      reads=[pk], writes=[f"hid{f}", "Umlp"])
                                P.add(DVE, lambda e, f=f: e.tensor_tensor(out=hid[:, f, 0:T], in0=hid[:, f, 0:T], in1=hid[:, f, 0:T], op=ALU.mult),
                                      reads=[f"hid{f}"], writes=[f"hid{f}", "Umlp"])
                    for pj in range(4):
                        wv, wk = load_piece(ti, l, f"dn{pj}")
                        for dd in range(2):
                            d = pj * 2 + dd
                            pt, pk = psg()
                            for kc in range(32):
                                P.add(PE, lambda e, dd=dd, kc=kc, pt=pt, wv=wv: e.matmul(
                                    pt[:, 0:T], lhsT=wv[:, kc, dd * 128:(dd + 1) * 128], rhs=hid[:, kc, 0:T], start=(kc == 0), stop=(kc == 31)),
                                    reads=[wk, f"hid{kc}"], writes=[pk])
                            branch_evac(T, l, pt, pk, d, G_MPOST)
                            if d >= 1:
                                stat_mm(T, d - 1)
                    stat_mm(T, NCH - 1)
                    if l == 1 and ti + 1 < len(tile_blocks):
                        for b_ in range(min(2, tile_blocks[ti + 1])):
                            x_load(tok0 + T, b_)
                    post_norm_update(T, l, deferred=True)

                for b in range(nb):
                    gb = blk0 + b
                    if gb < 2:
                        continue
                    xb = xs[b % 2]
                    xk = f"xs{b % 2}"
                    pS = psS[b % 2]
                    pSk = f"psS{b % 2}"
                    for c in range(NCH):
                        P.add(PE, lambda e, c=c, b=b, pS=pS: e.transpose(out=pS[:, c * 128:(c + 1) * 128],
                                                                         in_=hT[:, c, b * 128:(b + 1) * 128], identity=identf[:]),
                              reads=[f"h{c}", "identf"], writes=[pSk + ("a" if c < 4 else "b")])
                    P.add(ACT, lambda e, xb=xb, pS=pS: e.activation(out=xb[:, 0:512], in_=pS[:, 0:512], func=AF.Copy),
                          reads=[pSk + "a"], writes=[xk + "lo"])
                    P.add(DVE, lambda e, xb=xb, pS=pS: e.tensor_copy(out=xb[:, 512:1024], in_=pS[:, 512:1024]),
                          reads=[pSk + "b"], writes=[xk + "hi"])
                    r0 = (gb - 2) * 128
                    P.add(SP, lambda e, xb=xb, r0=r0: e.dma_start(out=out_d[r0:r0 + 128, :], in_=xb[:]), reads=[xk, xk + "lo", xk + "hi"],
                          writes=[xk], dma=True)
                blk0 += nb

        P = Prog()
        program(P)
        _alias_fix(P)
        P.schedule()
        P.start_emit(nc, engs, sems, dsems)
        program(P)
        P.finish_emit()
    return nc


def _alias_fix(P):
    mix_pref = ("qpre", "qT", "yT")
    for op in P.ops:
        keys = op.reads + op.writes
        is_mix = any(k.startswith(mix_pref) for k in keys)
        is_mlp = any(k.startswith("hid") for k in keys)
        op.reads = [k for k in op.reads if k not in ("Umix", "Umlp")]
        op.writes = [k for k in op.writes if k not in ("Umix", "Umlp")]
        if is_mix:
            op.writes.append("Ualias_mix")
        if is_mlp:
            op.writes.append("Ualias_mlp")
    side = None
    for op in P.ops:
        m = "Ualias_mix" in op.writes
        h = "Ualias_mlp" in op.writes
        op.writes = [k for k in op.writes if k not in ("Ualias_mix", "Ualias_mlp")]
        if not (m or h):
            continue
        s = "mix" if m else "mlp"
        if s != side:
            op.writes.append("Uphase")
            side = s
        else:
            op.reads.append("Uphase")


_NC_CACHE = {}


def _prep_shared(mix_pre_g, w_in, conv_w, sinks, attn_out_g, conv_out_g, w_out, mix_post_g,
                 mlp_pre_g, w_up, w_down, mlp_post_g):
    perm = _w_in_perm()
    w_in_p = np.ascontiguousarray(w_in[:, :, perm])
    gtab = np.zeros((128, NG), np.float32)
    for l in range(2):
        o = l * NG_L
        gtab[:, o + G_PRE:o + G_PRE + 8] = mix_pre_g[l].reshape(8, 128).T
        gtab[:, o + G_POST:o + G_POST + 8] = mix_post_g[l].reshape(8, 128).T
        gtab[:, o + G_MPRE:o + G_MPRE + 8] = mlp_pre_g[l].reshape(8, 128).T
        gtab[:, o + G_MPOST:o + G_MPOST + 8] = mlp_post_g[l].reshape(8, 128).T
        gtab[:, o + G_CONV:o + G_CONV + 4] = conv_out_g[l].reshape(4, 128).T
        for k in range(3):
            gtab[:, o + G_CW + k * 4:o + G_CW + k * 4 + 4] = conv_w[l, k].reshape(4, 128).T
    gattn = np.ascontiguousarray(np.broadcast_to(attn_out_g[:, None, :], (2, 128, 512))).astype(np.float32)
    sinkb = np.ascontiguousarray(np.broadcast_to(sinks.reshape(1, 16), (128, 16))).astype(np.float32)
    cmat = np.concatenate([np.eye(128, dtype=np.float32), _perm_matrix(), np.ones((128, 128), np.float32)], axis=1)
    return dict(w_in=w_in_p, w_out=np.ascontiguousarray(w_out), w_up=np.ascontiguousarray(w_up),
                w_down=np.ascontiguousarray(w_down), gtab=gtab, gattn=gattn, sinkb=sinkb, cmat=cmat)


def _core_inputs(x, meta_tokens, core):
    b, half = core // 2, core % 2
    xin = np.zeros((NBLK * 128, D), np.float32)
    if half == 0:
        xin[128 + 112:256] = meta_tokens
        xin[256:] = x[b, 0:4096]
        first_pos = -128 - 112
    else:
        xin[:] = x[b, 3840:8192]
        first_pos = N_META + 3840
    C, S = _rope_tables(first_pos)
    masks = _masks(half == 0)
    masks = np.ascontiguousarray(masks.transpose(1, 0, 2).reshape(128, 4 * 256))
    return dict(xin=xin, ctab=C, stab=S, masks=masks)


def kernel(x, meta_tokens, mix_pre_g, w_in, conv_w, sinks, attn_out_g, conv_out_g, w_out, mix_post_g,
           mlp_pre_g, w_up, w_down, mlp_post_g, _tile_blocks=None, _cores=None):
    x = np.asarray(x, np.float32)
    args = [np.asarray(a, np.float32) for a in (mix_pre_g, w_in, conv_w, sinks, attn_out_g, conv_out_g, w_out,
                                                mix_post_g, mlp_pre_g, w_up, w_down, mlp_post_g)]
    shared = _prep_shared(*args)
    key = tuple(_tile_blocks) if _tile_blocks is not None else None
    if key not in _NC_CACHE:
        _NC_CACHE[key] = build(_tile_blocks)
    nc = _NC_CACHE[key]
    cores = list(range(8)) if _cores is None else _cores
    in_maps = []
    for c in cores:
        m = dict(shared)
        m.update(_core_inputs(x, np.asarray(meta_tokens, np.float32), c))
        in_maps.append(m)
    res = run_bass_kernel_spmd(nc, in_maps, core_ids=list(range(len(cores))))
    if _tile_blocks is not None:
        return [r["out"] for r in res.results]
    out = np.zeros((4, 8192, D), np.float32)
    for i, c in enumerate(cores):
        b, half = c // 2, c % 2
        out[b, half * 4096:(half + 1) * 4096] = res.results[i]["out"]
    return out
```

```python
import contextlib
import numpy as np
import concourse.bass as bass
import concourse.mybir as mybir
from concourse.bass_utils import run_bass_kernel_spmd

F32 = mybir.dt.float32
BF16 = mybir.dt.bfloat16
ALU = mybir.AluOpType
AF = mybir.ActivationFunctionType
AX = mybir.AxisListType

D = 1024
NCH = 8
DFF = 4096
NBLK = 34
TILE_BLOCKS = [4] * 8 + [2]
EPS = 1e-6
NEG = -30000.0
N_META = 16
ROT = 16
THETA = 500000.0
NSLOT = 3
DMA_SLOTS = 20

PE, ACT, DVE, POOL, SP = "pe", "act", "dve", "pool", "sp"
LIMIT = [10 ** 9]


class Op:
    __slots__ = ("eng", "fn", "reads", "writes", "dma", "deps_eng", "deps_dma", "inc", "ticket",
                 "slot", "value", "ndma")

    def __init__(self, eng, fn, reads, writes, dma=False, ndma=1):
        self.eng = eng
        self.fn = fn
        self.reads = reads
        self.writes = writes
        self.dma = dma
        self.ndma = ndma
        self.deps_eng = {}
        self.deps_dma = set()
        self.inc = False
        self.ticket = 0
        self.slot = -1
        self.value = 0


class Prog:
    def __init__(self):
        self.ops = []
        self.mode = "plan"
        self.pos = 0
        self.n = 0

    def add(self, eng, fn, reads=(), writes=(), dma=False, ndma=1):
        self.n += 1
        if self.n > LIMIT[0]:
            return
        if self.mode == "plan":
            writes = list(writes) + [k for k in reads if k.startswith("ps")]
            self.ops.append(Op(eng, None, list(reads), list(writes), dma, ndma))
            import sys as _s
            self.ops[-1].fn = _s._getframe(1).f_lineno
        else:
            op = self.ops[self.pos]
            assert op.eng == eng and op.dma == dma and op.ndma == ndma, (self.pos, op.eng, eng)
            self.emit_one(self.pos, op, fn)
            self.pos += 1

    def schedule(self):
        ops = self.ops
        last_w = {}
        readers = {}
        dma_count = 0
        dma_hist = []
        for i, op in enumerate(ops):
            deps = set()
            for k in op.reads:
                w = last_w.get(k)
                if w is not None:
                    deps.add((w, "raw"))
            for k in op.writes:
                w = last_w.get(k)
                if w is not None:
                    deps.add((w, "waw"))
                for r in readers.get(k, ()):
                    deps.add((r, "war"))
            if op.dma:
                if dma_count >= DMA_SLOTS:
                    deps.add((dma_hist[dma_count - DMA_SLOTS], "raw"))
                dma_hist.append(i)
                dma_count += 1
            for (j, kind) in deps:
                if j == i:
                    continue
                p = ops[j]
                if p.dma:
                    op.deps_dma.add(j)
                    continue
                if p.eng == op.eng and not op.dma:
                    if op.eng == PE:
                        continue
                    if kind != "raw":
                        continue
                cur = op.deps_eng.get(p.eng, -1)
                if j > cur:
                    op.deps_eng[p.eng] = j
            for k in op.reads:
                readers.setdefault(k, []).append(i)
            for k in op.writes:
                last_w[k] = i
                readers[k] = []
        for op in ops:
            for e, j in op.deps_eng.items():
                ops[j].inc = True
        cnt = {}
        for op in ops:
            if op.dma:
                continue
            if op.inc:
                cnt[op.eng] = cnt.get(op.eng, 0) + 1
            op.ticket = cnt.get(op.eng, 0)
        slot_val = [0] * DMA_SLOTS
        k = 0
        for op in ops:
            if op.dma:
                s = k % DMA_SLOTS
                slot_val[s] += 16 * op.ndma
                op.slot = s
                op.value = slot_val[s]
                k += 1

    def start_emit(self, nc, engs, sems, dsems):
        self.mode = "emit"
        self.pos = 0
        self.n = 0
        self.engs = engs
        self.sems = sems
        self.dsems = dsems
        self.waited = {e: {} for e in engs}
        self.waited_dma = {e: {} for e in engs}
        self.dmas = []

    def emit_one(self, i, op, fn):
        ops = self.ops
        e = self.engs[op.eng]
        waited = self.waited[op.eng]
        waited_dma = self.waited_dma[op.eng]
        for pe_, j in op.deps_eng.items():
            t = ops[j].ticket
            if waited.get(pe_, 0) < t:
                e.wait_ge(self.sems[pe_], t)
                waited[pe_] = t
        for j in sorted(op.deps_dma):
            p = ops[j]
            if waited_dma.get(p.slot, 0) < p.value:
                e.wait_ge(self.dsems[p.slot], p.value)
                waited_dma[p.slot] = p.value
        if op.dma:
            insts = fn(e)
            if not isinstance(insts, (list, tuple)):
                insts = [insts]
            assert len(insts) == op.ndma
            for ins in insts:
                ins.then_inc(self.dsems[op.slot], 16)
            self.dmas.append(i)
        else:
            ins = fn(e)
            if op.inc:
                ins.then_inc(self.sems[op.eng], 1)

    def finish_emit(self):
        assert self.pos == len(self.ops), (self.pos, len(self.ops))
        for i in self.dmas:
            op = self.ops[i]
            wd = self.waited_dma[op.eng]
            if wd.get(op.slot, 0) < op.value:
                self.engs[op.eng].wait_ge(self.dsems[op.slot], op.value)
                wd[op.slot] = op.value


def _w_in_perm():
    s_q, s_k, s_v = 0, 512, 640
    s_b, s_c, s_h = 768, 1280, 1792
    perm = []
    for j in range(4):
        perm += list(range(s_q + j * 64, s_q + j * 64 + 64))
        perm += list(range(s_q + (4 + j) * 64, s_q + (4 + j) * 64 + 64))
    perm += list(range(s_k, s_k + 128))
    perm += list(range(s_v, s_v + 128))
    for i in range(4):
        perm += list(range(s_c + i * 128, s_c + (i + 1) * 128))
        perm += list(range(s_h + i * 128, s_h + (i + 1) * 128))
        perm += list(range(s_b + i * 128, s_b + (i + 1) * 128))
    return np.array(perm, dtype=np.int64)


def _rope_tables(first_pos):
    n = NBLK * 128
    pos = (first_pos + np.arange(n)).astype(np.float32)
    inv_freq = np.power(np.float32(THETA), -np.arange(0, ROT, 2, dtype=np.float32) / np.float32(ROT)).astype(np.float32)
    ang = (pos[:, None] * inv_freq[None, :]).astype(np.float32)
    cos = np.cos(ang).astype(np.float32).T
    sin = np.sin(ang).astype(np.float32).T
    C = np.ones((128, n), np.float32)
    S = np.zeros((128, n), np.float32)
    for a in range(2):
        b0 = 64 * a
        C[b0:b0 + 8] = cos
        C[b0 + 8:b0 + 16] = cos
        S[b0:b0 + 8] = -sin
        S[b0 + 8:b0 + 16] = sin
    return C, S


def _perm_matrix():
    P = np.zeros((128, 128), np.float32)
    for a in range(2):
        b0 = 64 * a
        for m in range(8):
            P[b0 + m + 8, b0 + m] = 1.0
            P[b0 + m, b0 + m + 8] = 1.0
    return P


def _masks(is_first_half):
    q = np.arange(128)[:, None]
    k = np.arange(128)[None, :]
    band_prev = (k > q)
    causal = (k <= q)
    allm = np.zeros((128, 128), bool)
    kvalid = (k >= 112) & np.ones((128, 1), bool)
    m = np.zeros((4, 128, 256), bool)
    if is_first_half:
        m[0, :, :128] = allm
        m[0, :, 128:] = allm
        m[1, :, :128] = allm
        m[1, :, 128:] = causal & kvalid
        m[2, :, :128] = band_prev & kvalid
        m[2, :, 128:] = causal
    else:
        m[0, :, :128] = allm
        m[0, :, 128:] = causal
        m[1, :, :128] = band_prev
        m[1, :, 128:] = causal
        m[2, :, :128] = band_prev
        m[2, :, 128:] = causal
    m[3, :, :128] = band_prev
    m[3, :, 128:] = causal
    return np.where(m, 0.0, NEG).astype(np.float32)


G_PRE, G_POST, G_MPRE, G_MPOST, G_CONV, G_CW = 0, 8, 16, 24, 32, 36
NG_L = 48
NG = 2 * NG_L


def build(tile_blocks=None, n_out_blocks=None):
    tile_blocks = TILE_BLOCKS if tile_blocks is None else tile_blocks
    nblk = sum(tile_blocks)
    n_out = nblk - 2
    nc = bass.Bass("TRN2", target_bir_lowering=False)

    def dram_in(name, shape, dt=F32):
        return nc.dram_tensor(name, list(shape), dt, kind="ExternalInput").ap()

    xin = dram_in("xin", [NBLK * 128, D])
    w_in = dram_in("w_in", [2, D, 2304])
    w_out = dram_in("w_out", [2, D, D])
    w_up = dram_in("w_up", [2, D, DFF])
    w_down = dram_in("w_down", [2, DFF, D])
    gtab_d = dram_in("gtab", [128, NG])
    gattn_d = dram_in("gattn", [2, 128, 512])
    sink_d = dram_in("sinkb", [128, 16])
    ctab_d = dram_in("ctab", [128, NBLK * 128])
    stab_d = dram_in("stab", [128, NBLK * 128])
    mask_d = dram_in("masks", [128, 4 * 256])
    cmat_d = dram_in("cmat", [128, 3 * 128])
    out_d = nc.dram_tensor("out", [max(n_out, 1) * 128, D], F32, kind="ExternalOutput").ap()
    scr = nc.dram_tensor("wscr", [2 * 12, 128, 8192], BF16, kind="Internal").ap()

    es = contextlib.ExitStack()
    with es:
        def sb(name, shape, dt):
            return es.enter_context(nc.sbuf_tensor(name, list(shape), dt))

        def ps(name, shape, dt):
            return es.enter_context(nc.psum_tensor(name, list(shape), dt))

        ring = [sb(f"ring{i}", [128, 8192], BF16) for i in range(NSLOT)]
        hT = sb("hT", [128, NCH, 512], F32)
        aT = sb("aT", [128, NCH, 512], BF16)
        sq = sb("sq", [128, NCH, 512], BF16)
        z = sb("z", [128, NCH, 512], F32)
        U = sb("U", [128, 16384], BF16)
        hid = U[:, :].rearrange("p (f t) -> p f t", f=32)
        qpre = U[:, 0:2560].rearrange("p (j t) -> p j t", j=5)
        qT = U[:, 2560:4608].rearrange("p (j t) -> p j t", j=4)
        yT = U[:, 4608:8704].rearrange("p (c t) -> p c t", c=8)
        yconv = sb("yconv", [128, 4, 512], F32)
        rt1 = [sb(f"rt1_{i}", [128, 512], F32) for i in range(2)]
        rt2 = [sb(f"rt2_{i}", [128, 512], F32) for i in range(2)]
        ctmp = rt1
        btmp = rt2
        lnv = sb("lnv", [128, 512], F32)
        rstd = sb("rstd", [128, 512], F32)
        kA = [sb(f"kA{l}", [128, 640], BF16) for l in range(2)]
        kB = [sb(f"kB{l}", [128, 640], BF16) for l in range(2)]
        Vt = [sb(f"Vt{l}", [128, 5, 128], BF16) for l in range(2)]
        Ut = [sb(f"Ut{l}", [128, 4, 520], BF16) for l in range(2)]
        Pm = [sb(f"Pm{i}", [128, 4, 256], BF16) for i in range(2)]
        PTs = [sb(f"PTs{i}", [128, 8, 128], BF16) for i in range(2)]
        yats = [sb(f"yat{i}", [128, 512], F32) for i in range(2)]
        yans = [sb(f"yan{i}", [128, 512], BF16) for i in range(2)]
        r2b = sb("r2b", [128, 512], F32)
        st = [sb(f"stt{i}", [128, 32], F32) for i in range(4)]
        st2 = sb("st2", [128, 8], F32)
        rtk = sb("rtk", [128, 8], F32)
        xs = [sb(f"xs{i}", [128, D], F32) for i in range(2)]
        ctab = sb("ctab_s", [128, 512], F32)
        stab = sb("stab_s", [128, 512], F32)
        gtab = sb("gtab_s", [128, NG], F32)
        gattn = sb("gattn_s", [128, 2, 512], F32)
        sinkb = sb("sinkb_s", [128, 16], F32)
        negsink = sb("negsink", [128, 16], F32)
        epsb = sb("epsb", [128, 1], F32)
        maskb = sb("maskb", [128, 4, 256], BF16)
        cmat = sb("cmat_s", [128, 3, 128], BF16)
        identf = sb("identf", [128, 128], F32)
        diag = sb("diag", [128, 24, 128], BF16)
        ident = cmat[:, 0, :]
        perm = cmat[:, 1, :]
        ones = cmat[:, 2, :]

        psG = [ps(f"psG{i}", [128, 512], F32) for i in range(2)]
        psS = [ps(f"psS{i}", [128, 1024], F32) for i in range(2)]
        psPT = ps("psPT", [128, 1024], BF16)
        psO = ps("psO", [128, 512], F32)

        sem_names = [PE, ACT, DVE, POOL]
        sems = {e: es.enter_context(nc.semaphore(f"s_{e}")) for e in sem_names}
        dsems = [es.enter_context(nc.semaphore(f"d{i}")) for i in range(DMA_SLOTS)]
        engs = {PE: nc.tensor, ACT: nc.scalar, DVE: nc.vector, POOL: nc.gpsimd, SP: nc.sync}

        def program(P):
            gctr = [0]

            G2 = [(psG[0], "psG0"), (psG[1], "psG1")]
            G6 = G2 + [(psS[0][:, 0:512], "psS0a"), (psS[0][:, 512:1024], "psS0b"),
                       (psS[1][:, 0:512], "psS1a"), (psS[1][:, 512:1024], "psS1b")]
            gpool = [G6]

            def psg():
                pool = gpool[0]
                i = gctr[0] % len(pool)
                gctr[0] += 1
                return pool[i]

            sqj = sq[:, 7, :]
            ring_ctr = [0]
            PIECES = ["in0", "in1", "in2", "out", "up0", "up1", "up2", "up3", "dn0", "dn1", "dn2", "dn3"]

            def piece_src(l, name):
                if name.startswith("in"):
                    j = int(name[2:])
                    src = w_in[l, :, j * 768:(j + 1) * 768].rearrange("(kc p) n -> p kc n", p=128)
                    return src, (8, 768)
                if name == "out":
                    return w_out[l].rearrange("(kc p) n -> p kc n", p=128), (8, 1024)
                if name.startswith("up"):
                    j = int(name[2:])
                    return w_up[l, :, j * 1024:(j + 1) * 1024].rearrange("(kc p) n -> p kc n", p=128), (8, 1024)
                j = int(name[2:])
                return w_down[l, :, j * 256:(j + 1) * 256].rearrange("(kc p) n -> p kc n", p=128), (32, 256)

            SEQ = [(ti_, l_, nm_) for ti_ in range(len(tile_blocks)) for l_ in range(2) for nm_ in PIECES]
            issued = [0]
            LA = 2

            def emit_load(n):
                ti, l, name = SEQ[n]
                s = n % NSLOT
                src, (a, n_) = piece_src(l, name)
                view = ring[s][:, 0:a * n_].rearrange("p (a n) -> p a n", a=a)
                pi = l * 12 + PIECES.index(name)
                key = f"ring{s}"
                skey = f"scr{pi}"
                if ti == 0:
                    q = a // 4

                    def f(e):
                        return [e.dma_start(out=view[:, i * q:(i + 1) * q, :], in_=src[:, i * q:(i + 1) * q, :])
                                for i in range(4)]
                    P.add(POOL, f, writes=[key], dma=True, ndma=4)
                    P.add(SP, lambda e: e.dma_start(out=scr[pi, :, 0:a * n_], in_=ring[s][:, 0:a * n_]),
                          reads=[key], writes=[skey], dma=True)
                else:
                    def f(e):
                        h = (a * n_) // 2
                        return [e.dma_start(out=ring[s][:, i * h:(i + 1) * h], in_=scr[pi, :, i * h:(i + 1) * h])
                                for i in range(2)]
                    P.add(SP, f, reads=[skey], writes=[key], dma=True, ndma=2)

            def load_piece(ti, l, name):
                n = ring_ctr[0]
                ring_ctr[0] += 1
                assert SEQ[n] == (ti, l, name), (SEQ[n], ti, l, name)
                while issued[0] < min(n + 1 + LA, len(SEQ)):
                    emit_load(issued[0])
                    issued[0] += 1
                s = n % NSLOT
                _, (a, n_) = piece_src(l, name)
                view = ring[s][:, 0:a * n_].rearrange("p (a n) -> p a n", a=a)
                return view, f"ring{s}"

            def rms_stats(T, nchunks, inv_n, sqkeys):
                pt, pk = psg()
                for c in range(nchunks):
                    P.add(PE, lambda e, c=c, pt=pt: e.matmul(pt[:, 0:T], lhsT=ones, rhs=sq[:, c, 0:T],
                                                             start=(c == 0), stop=(c == nchunks - 1)),
                          reads=["cmat", sqkeys[c]], writes=[pk])
                P.add(ACT, lambda e, pt=pt: e.activation(out=lnv[:, 0:T], in_=pt[:, 0:T], func=AF.Ln,
                                                         bias=epsb[:, 0:1], scale=inv_n),
                      reads=[pk, "epsb"], writes=["lnv"])
                P.add(ACT, lambda e: e.activation(out=rstd[:, 0:T], in_=lnv[:, 0:T], func=AF.Exp, scale=-0.5),
                      reads=["lnv"], writes=["rstd"])

            def pre_norm(T, l, gcol):
                for c in range(NCH):
                    if c % 2 == 0:
                        P.add(ACT, lambda e, c=c: e.activation(out=sq[:, c, 0:T], in_=hT[:, c, 0:T], func=AF.Square),
                              reads=[f"h{c}"], writes=[f"sq{c}"])
                    else:
                        P.add(DVE, lambda e, c=c: e.tensor_tensor(out=sq[:, c, 0:T], in0=hT[:, c, 0:T], in1=hT[:, c, 0:T],
                                                                   op=ALU.mult),
                              reads=[f"h{c}"], writes=[f"sq{c}"])
                rms_stats(T, NCH, 1.0 / D, [f"sq{c}" for c in range(NCH)])
                for c in range(NCH):
                    col = l * NG_L + gcol + c
                    eng = DVE
                    P.add(eng, lambda e, c=c, col=col: e.scalar_tensor_tensor(
                        out=aT[:, c, 0:T], in0=hT[:, c, 0:T], scalar=gtab[:, col:col + 1], in1=rstd[:, 0:T],
                        op0=ALU.mult, op1=ALU.mult),
                        reads=[f"h{c}", "gtab", "rstd"], writes=[f"a{c}"])

            def stat_mm(T, c):
                P.add(PE, lambda e: e.matmul(psO[:, 0:T], lhsT=ones, rhs=sq[:, c, 0:T], start=(c == 0), stop=(c == NCH - 1)),
                      reads=["cmat", f"sq{c}"], writes=["psO"])

            def pre_norm_deferred(T, l, gcol):
                for c in range(NCH):
                    col = l * NG_L + gcol + c
                    P.add(ACT, lambda e, c=c, col=col: e.activation(out=aT[:, c, 0:T], in_=hT[:, c, 0:T], func=AF.Copy,
                                                                    scale=gtab[:, col:col + 1]),
                          reads=[f"h{c}", "gtab"], writes=[f"a{c}"])
                for c in range(NCH):
                    P.add(ACT, lambda e, c=c: e.activation(out=sq[:, c, 0:T], in_=hT[:, c, 0:T], func=AF.Square),
                          reads=[f"h{c}"], writes=[f"sq{c}"])
                pt, pk = psg()
                for c in range(NCH):
                    P.add(PE, lambda e, c=c, pt=pt: e.matmul(pt[:, 0:T], lhsT=ones, rhs=sq[:, c, 0:T],
                                                             start=(c == 0), stop=(c == NCH - 1)),
                          reads=["cmat", f"sq{c}"], writes=[pk])
                P.add(ACT, lambda e, pt=pt: e.activation(out=r2b[:, 0:T], in_=pt[:, 0:T], func=AF.Ln, bias=epsb[:, 0:1], scale=1.0 / D),
                      reads=[pk, "epsb"], writes=["r2b"])
                P.add(ACT, lambda e: e.activation(out=r2b[:, 0:T], in_=r2b[:, 0:T], func=AF.Exp, scale=-1.0),
                      reads=["r2b"], writes=["r2b"])

            def post_norm_update(T, l, deferred=False):
                if not deferred:
                    P.add(ACT, lambda e: e.activation(out=lnv[:, 0:T], in_=psO[:, 0:T], func=AF.Ln, bias=epsb[:, 0:1], scale=1.0 / D),
                          reads=["psO", "epsb"], writes=["lnv"])
                    P.add(ACT, lambda e: e.activation(out=rstd[:, 0:T], in_=lnv[:, 0:T], func=AF.Exp, scale=-0.5),
                          reads=["lnv"], writes=["rstd"])
                else:
                    P.add(DVE, lambda e: e.tensor_tensor(out=lnv[:, 0:T], in0=psO[:, 0:T], in1=r2b[:, 0:T], op=ALU.mult),
                          reads=["psO", "r2b"], writes=["lnv"])
                    P.add(DVE, lambda e: e.tensor_tensor(out=lnv[:, 0:T], in0=lnv[:, 0:T], in1=r2b[:, 0:T], op=ALU.mult),
                          reads=["lnv", "r2b"], writes=["lnv"])
                    P.add(ACT, lambda e: e.activation(out=lnv[:, 0:T], in_=lnv[:, 0:T], func=AF.Ln, bias=epsb[:, 0:1], scale=1.0 / D),
                          reads=["lnv", "epsb"], writes=["lnv"])
                    P.add(ACT, lambda e: e.activation(out=lnv[:, 0:T], in_=lnv[:, 0:T], func=AF.Exp, scale=-0.5),
                          reads=["lnv"], writes=["lnv"])
                    P.add(DVE, lambda e: e.tensor_tensor(out=rstd[:, 0:T], in0=lnv[:, 0:T], in1=r2b[:, 0:T], op=ALU.mult),
                          reads=["lnv", "r2b"], writes=["rstd"])
                for c in range(NCH):
                    P.add(DVE, lambda e, c=c: e.tensor_tensor(out=z[:, c, 0:T], in0=z[:, c, 0:T], in1=rstd[:, 0:T], op=ALU.mult),
                          reads=[f"z{c}", "rstd"], writes=[f"z{c}"])
                    P.add(DVE, lambda e, c=c: e.tensor_tensor(out=hT[:, c, 0:T], in0=hT[:, c, 0:T], in1=z[:, c, 0:T], op=ALU.add),
                          reads=[f"z{c}", f"h{c}"], writes=[f"h{c}"])

            def branch_evac(T, l, pt, pk, d, gcol):
                col = l * NG_L + gcol + d
                P.add(ACT, lambda e, d=d, pt=pt: e.activation(out=sq[:, d, 0:T], in_=pt[:, 0:T], func=AF.Square),
                      reads=[pk], writes=[f"sq{d}"])
                P.add(DVE, lambda e, d=d, pt=pt, col=col: e.tensor_scalar(out=z[:, d, 0:T], in0=pt[:, 0:T],
                                                                          scalar1=gtab[:, col:col + 1], scalar2=None, op0=ALU.mult),
                      reads=[pk, "gtab"], writes=[f"z{d}"])

            while issued[0] < min(1 + LA, len(SEQ)):
                emit_load(issued[0])
                issued[0] += 1
            P.add(SP, lambda e: e.dma_start(out=gtab[:], in_=gtab_d[:, :]), writes=["gtab"], dma=True)
            P.add(SP, lambda e: e.dma_start(out=gattn[:], in_=gattn_d.rearrange("l p f -> p l f")), writes=["gattn"], dma=True)
            P.add(SP, lambda e: e.dma_start(out=sinkb[:], in_=sink_d[:, :]), writes=["sinkb"], dma=True)
            P.add(SP, lambda e: e.dma_start(out=identf[:], in_=cmat_d[:, 0:128]), writes=["identf"], dma=True)
            P.add(POOL, lambda e: e.dma_start(out=cmat[:], in_=cmat_d.rearrange("p (a n) -> p a n", a=3)), writes=["cmat"], dma=True)
            P.add(POOL, lambda e: e.dma_start(out=maskb[:], in_=mask_d.rearrange("p (a n) -> p a n", a=4)), writes=["maskb"], dma=True)
            P.add(POOL, lambda e: e.memset(epsb[:], EPS), writes=["epsb"])
            P.add(POOL, lambda e: e.tensor_scalar(out=negsink[:], in0=sinkb[:], scalar1=-1.0, scalar2=None, op0=ALU.mult),
                  reads=["sinkb"], writes=["negsink"])
            for l in range(2):
                P.add(POOL, lambda e, l=l: e.memset(kA[l][:], 0.0), writes=[f"kA{l}"])
                P.add(POOL, lambda e, l=l: e.memset(kB[l][:], 0.0), writes=[f"kB{l}"])
                P.add(POOL, lambda e, l=l: e.memset(Vt[l][:], 0.0), writes=[f"Vt{l}"])
                P.add(POOL, lambda e, l=l: e.memset(Ut[l][:], 0.0), writes=[f"Ut{l}"])
                for i in range(4):
                    for k in range(3):
                        col = l * NG_L + G_CW + k * 4 + i
                        P.add(POOL, lambda e, l=l, i=i, k=k, col=col: e.tensor_scalar(
                            out=diag[:, l * 12 + i * 3 + k, :], in0=identf[:], scalar1=gtab[:, col:col + 1],
                            scalar2=None, op0=ALU.mult),
                            reads=["identf", "gtab"], writes=[f"diag{l}"])

            blk0 = 0
            for ti, nb in enumerate(tile_blocks):
                T = nb * 128
                tok0 = blk0 * 128
                P.add(SP, lambda e, tok0=tok0, T=T: e.dma_start(out=ctab[:, 0:T], in_=ctab_d[:, tok0:tok0 + T]),
                      writes=["ctab"], dma=True)
                P.add(SP, lambda e, tok0=tok0, T=T: e.dma_start(out=stab[:, 0:T], in_=stab_d[:, tok0:tok0 + T]),
                      writes=["stab"], dma=True)
                def x_load(tok_base, b):
                    i = b % 2
                    r0 = tok_base + b * 128

                    def f(e):
                        return [e.dma_start(out=rt1[i][:], in_=xin[r0:r0 + 128, 0:512]),
                                e.dma_start(out=rt2[i][:], in_=xin[r0:r0 + 128, 512:1024])]
                    P.add(SP, f, writes=[f"rt1_{i}", f"rt2_{i}"], dma=True, ndma=2)

                def x_transpose(b):
                    i = b % 2
                    pS = psS[b % 2]
                    pSk = f"psS{b % 2}"
                    for c in range(NCH):
                        src = rt1[i] if c < 4 else rt2[i]
                        sk = f"rt1_{i}" if c < 4 else f"rt2_{i}"
                        cc = c % 4
                        P.add(PE, lambda e, c=c, cc=cc, src=src: e.transpose(out=pS[:, c * 128:(c + 1) * 128],
                                                                             in_=src[:, cc * 128:(cc + 1) * 128], identity=identf[:]),
                              reads=[sk, "identf"], writes=[pSk + ("a" if c < 4 else "b")])
                    for hlf in range(2):
                        eng = ACT if hlf == 0 else DVE
                        if eng == ACT:
                            fn = lambda e, hlf=hlf: e.activation(
                                out=hT[:, hlf * 4:(hlf + 1) * 4, b * 128:(b + 1) * 128],
                                in_=pS[:, hlf * 512:(hlf + 1) * 512].rearrange("p (c t) -> p c t", c=4), func=AF.Copy)
                        else:
                            fn = lambda e, hlf=hlf: e.tensor_copy(
                                out=hT[:, hlf * 4:(hlf + 1) * 4, b * 128:(b + 1) * 128],
                                in_=pS[:, hlf * 512:(hlf + 1) * 512].rearrange("p (c t) -> p c t", c=4))
                        P.add(eng, fn, reads=[pSk + "ab"[hlf]], writes=[f"h{c}" for c in range(hlf * 4, hlf * 4 + 4)])

                if ti == 0:
                    for b in range(min(2, nb)):
                        x_load(tok0, b)
                for b in range(nb):
                    x_transpose(b)
                    if b + 2 < nb:
                        x_load(tok0, b + 2)

                for l in range(2):
                    for c in range(NCH):
                        col = l * NG_L + G_PRE + c
                        P.add(ACT, lambda e, c=c, col=col: e.activation(out=aT[:, c, 0:T], in_=hT[:, c, 0:T], func=AF.Copy,
                                                                        scale=gtab[:, col:col + 1]),
                              reads=[f"h{c}", "gtab"], writes=[f"a{c}"])
                    for c in range(NCH):
                        P.add(ACT, lambda e, c=c: e.activation(out=sq[:, c, 0:T], in_=hT[:, c, 0:T], func=AF.Square),
                              reads=[f"h{c}"], writes=[f"sq{c}"])
                    akeys = [f"a{c}" for c in range(NCH)]
                    wv0, wk0 = load_piece(ti, l, "in0")
                    for j in range(5):
                        pt, pk = psg()
                        for kc in range(NCH):
                            P.add(PE, lambda e, j=j, kc=kc, pt=pt: e.matmul(
                                pt[:, 0:T], lhsT=wv0[:, kc, j * 128:(j + 1) * 128], rhs=aT[:, kc, 0:T],
                                start=(kc == 0), stop=(kc == NCH - 1)),
                                reads=[wk0, f"a{kc}"], writes=[pk])
                        P.add(ACT, lambda e, j=j, pt=pt: e.activation(out=qpre[:, j, 0:T], in_=pt[:, 0:T], func=AF.Copy),
                              reads=[pk], writes=[f"qpre{j}", "Umix"])
                    pt, pk = psg()
                    for b in range(nb):
                        for kc in range(NCH):
                            P.add(PE, lambda e, b=b, kc=kc, pt=pt: e.matmul(
                                pt[:, b * 128:(b + 1) * 128], lhsT=aT[:, kc, b * 128:(b + 1) * 128], rhs=wv0[:, kc, 640:768],
                                start=(kc == 0), stop=(kc == NCH - 1)),
                                reads=[wk0, f"a{kc}"], writes=[pk])
                    ptv, pkv = pt, pk
                    pt, pk = psg()
                    for c in range(NCH):
                        P.add(PE, lambda e, c=c, pt=pt: e.matmul(pt[:, 0:T], lhsT=ones, rhs=sq[:, c, 0:T],
                                                                 start=(c == 0), stop=(c == NCH - 1)),
                              reads=["cmat", f"sq{c}"], writes=[pk])
                    P.add(ACT, lambda e, pt=pt: e.activation(out=r2b[:, 0:T], in_=pt[:, 0:T], func=AF.Ln, bias=epsb[:, 0:1], scale=1.0 / D),
                          reads=[pk, "epsb"], writes=["r2b"])
                    P.add(ACT, lambda e: e.activation(out=rstd[:, 0:T], in_=r2b[:, 0:T], func=AF.Exp, scale=-0.5),
                          reads=["r2b"], writes=["rstd"])
                    P.add(ACT, lambda e: e.activation(out=r2b[:, 0:T], in_=r2b[:, 0:T], func=AF.Exp, scale=-1.0),
                          reads=["r2b"], writes=["r2b"])
                    pt2, pk2 = psg()
                    for b in range(nb):
                        for c in range(NCH):
                            P.add(PE, lambda e, b=b, c=c, pt2=pt2: e.matmul(pt2[:, b:b + 1], lhsT=sq[:, c, b * 128:(b + 1) * 128], rhs=ones[:, 0:1],
                                                                         start=(c == 0), stop=(c == NCH - 1)),
                                  reads=["cmat", f"sq{c}"], writes=[pk2])
                    P.add(ACT, lambda e, pt2=pt2: e.activation(out=rtk[:, 0:nb], in_=pt2[:, 0:nb], func=AF.Ln, bias=epsb[:, 0:1], scale=1.0 / D),
                          reads=[pk2, "epsb"], writes=["rtok"])
                    P.add(ACT, lambda e: e.activation(out=rtk[:, 0:nb], in_=rtk[:, 0:nb], func=AF.Exp, scale=-0.5),
                          reads=["rtok"], writes=["rtok"])
                    for b in range(nb):
                        P.add(ACT, lambda e, b=b: e.activation(out=Vt[l][:, 1 + b, :], in_=ptv[:, b * 128:(b + 1) * 128], func=AF.Copy,
                                                               scale=rtk[:, b:b + 1]),
                              reads=[pkv, "rtok"], writes=[f"Vt{l}"])
                    Cs = yconv[:, 0, :]
                    Ss = yconv[:, 1, :]
                    P.add(DVE, lambda e: e.tensor_tensor(out=Cs[:, 0:T], in0=ctab[:, 0:T], in1=rstd[:, 0:T], op=ALU.mult),
                          reads=["ctab", "rstd"], writes=["yconv0"])
                    P.add(DVE, lambda e: e.tensor_tensor(out=Ss[:, 0:T], in0=stab[:, 0:T], in1=rstd[:, 0:T], op=ALU.mult),
                          reads=["stab", "rstd"], writes=["yconv1"])
                    for j in (4, 0, 1, 2, 3):
                        pr, prk = psg()
                        P.add(PE, lambda e, j=j, pr=pr: e.matmul(pr[:, 0:T], lhsT=perm, rhs=qpre[:, j, 0:T], start=True, stop=True),
                              reads=["cmat", f"qpre{j}"], writes=[prk])
                        r1 = rt1[j % 2]
                        r2 = rt2[j % 2]
                        P.add(DVE, lambda e, j=j, r1=r1: e.tensor_tensor(out=r1[:, 0:T], in0=qpre[:, j, 0:T], in1=Cs[:, 0:T], op=ALU.mult),
                              reads=[f"qpre{j}", "yconv0"], writes=[f"rt1_{j % 2}"])
                        P.add(DVE, lambda e, pr=pr, r2=r2: e.tensor_tensor(out=r2[:, 0:T], in0=pr[:, 0:T], in1=Ss[:, 0:T], op=ALU.mult),
                              reads=[prk, "yconv1"], writes=[f"rt2_{j % 2}"])
                        if j < 4:
                            P.add(DVE, lambda e, j=j, r1=r1, r2=r2: e.tensor_tensor(out=qT[:, j, 0:T], in0=r1[:, 0:T], in1=r2[:, 0:T], op=ALU.add),
                                  reads=[f"rt1_{j % 2}", f"rt2_{j % 2}"], writes=[f"qT{j}", "Umix"])
                        else:
                            P.add(DVE, lambda e, r1=r1, r2=r2: e.tensor_tensor(out=kA[l][0:64, 128:128 + T], in0=r1[0:64, 0:T], in1=r2[0:64, 0:T], op=ALU.add),
                                  reads=[f"rt1_{j % 2}", f"rt2_{j % 2}"], writes=[f"kA{l}"])
                            P.add(DVE, lambda e, r1=r1, r2=r2: e.tensor_tensor(out=kB[l][64:128, 128:128 + T], in0=r1[64:128, 0:T], in1=r2[64:128, 0:T], op=ALU.add),
                                  reads=[f"rt1_{j % 2}", f"rt2_{j % 2}"], writes=[f"kB{l}"])

                    def conv_gen():
                        wv = wk = None
                        for i in range(4):
                            if i % 2 == 0:
                                wv, wk = load_piece(ti, l, f"in{1 + i // 2}")
                            base = (i % 2) * 384
                            ct = ctmp[i % 2]
                            bt = btmp[i % 2]
                            pt, pk = psg()
                            for kc in range(NCH):
                                P.add(PE, lambda e, kc=kc, pt=pt, wv=wv, base=base: e.matmul(
                                    pt[:, 0:T], lhsT=wv[:, kc, base:base + 128], rhs=aT[:, kc, 0:T], start=(kc == 0), stop=(kc == NCH - 1)),
                                    reads=[wk, f"a{kc}"], writes=[pk])
                            P.add(DVE, lambda e, pt=pt, ct=ct: e.tensor_tensor(out=ct[:, 0:T], in0=pt[:, 0:T], in1=r2b[:, 0:T], op=ALU.mult),
                                  reads=[pk, "r2b"], writes=[f"rt1_{i % 2}"])
                            yield
                            pt, pk = psg()
                            for kc in range(NCH):
                                P.add(PE, lambda e, kc=kc, pt=pt, wv=wv, base=base: e.matmul(
                                    pt[:, 0:T], lhsT=wv[:, kc, base + 128:base + 256], rhs=aT[:, kc, 0:T], start=(kc == 0), stop=(kc == NCH - 1)),
                                    reads=[wk, f"a{kc}"], writes=[pk])
                            P.add(DVE, lambda e, pt=pt, ct=ct, i=i: e.tensor_tensor(out=Ut[l][:, i, 2:2 + T], in0=pt[:, 0:T], in1=ct[:, 0:T], op=ALU.mult),
                                  reads=[pk, f"rt1_{i % 2}"], writes=[f"Ut{l}_{i}"])
                            yield
                            pt, pk = psg()
                            for kc in range(NCH):
                                P.add(PE, lambda e, kc=kc, pt=pt, wv=wv, base=base: e.matmul(
                                    pt[:, 0:T], lhsT=wv[:, kc, base + 256:base + 384], rhs=aT[:, kc, 0:T], start=(kc == 0), stop=(kc == NCH - 1)),
                                    reads=[wk, f"a{kc}"], writes=[pk])
                            P.add(DVE, lambda e, pt=pt, bt=bt: e.tensor_tensor(out=bt[:, 0:T], in0=pt[:, 0:T], in1=rstd[:, 0:T], op=ALU.mult),
                                  reads=[pk, "rstd"], writes=[f"rt2_{i % 2}"])
                            pt, pk = psg()
                            for k in range(3):
                                P.add(PE, lambda e, k=k, pt=pt, i=i: e.matmul(
                                    pt[:, 0:T], lhsT=diag[:, l * 12 + i * 3 + k, :], rhs=Ut[l][:, i, k:k + T], start=(k == 0), stop=(k == 2)),
                                    reads=[f"diag{l}", f"Ut{l}_{i}"], writes=[pk])
                            P.add(DVE, lambda e, pt=pt, bt=bt, i=i: e.tensor_tensor(out=yconv[:, i, 0:T], in0=pt[:, 0:T], in1=bt[:, 0:T], op=ALU.mult),
                                  reads=[pk, f"rt2_{i % 2}"], writes=[f"yconv{i}"])
                            P.add(ACT, lambda e, i=i: e.activation(out=sq[:, i, 0:T], in_=yconv[:, i, 0:T], func=AF.Square),
                                  reads=[f"yconv{i}"], writes=[f"sq{i}"])
                            P.add(POOL, lambda e, i=i: e.tensor_copy(out=Ut[l][:, i, 0:2], in_=Ut[l][:, i, T:T + 2]),
                                  reads=[f"Ut{l}_{i}"], writes=[f"Ut{l}_{i}"])
                            yield
                        rms_stats(T, 4, 1.0 / 512, [f"sq{i}" for i in range(4)])
                        for i in range(4):
                            col = l * NG_L + G_CONV + i
                            P.add(DVE, lambda e, i=i, col=col: e.scalar_tensor_tensor(
                                out=yT[:, 4 + i, 0:T], in0=yconv[:, i, 0:T], scalar=gtab[:, col:col + 1], in1=rstd[:, 0:T],
                                op0=ALU.mult, op1=ALU.mult),
                                reads=[f"yconv{i}", "gtab", "rstd"], writes=[f"yT{4 + i}", "Umix"])
                        yield

                    def unit(u):
                        b, g = u // 2, u % 2
                        return b, g, psS[u % 2], f"psS{u % 2}", st[u % 4], f"st{u % 4}", Pm[u % 2], f"Pm{u % 2}", PTs[u % 2], f"PTs{u % 2}"

                    def stage_A(u):
                        b, g, pS, pSk, sg, sgk, Pg, Pk, PTg, PTk = unit(u)
                        mv = min(blk0 + b, 3)
                        k0 = b * 128
                        kbuf = kA[l] if g == 0 else kB[l]
                        kkey = f"kA{l}" if g == 0 else f"kB{l}"
                        for j in range(4):
                            P.add(PE, lambda e, j=j: e.matmul(
                                pS[:, j * 256:(j + 1) * 256], lhsT=qT[:, j, b * 128:(b + 1) * 128], rhs=kbuf[:, k0:k0 + 256],
                                start=True, stop=False),
                                reads=[f"qT{j}", kkey], writes=[pSk + "ab"[j // 2]])
                            P.add(PE, lambda e, j=j: e.matmul(
                                pS[:, j * 256:(j + 1) * 256], lhsT=ident, rhs=maskb[:, mv, :], start=False, stop=True),
                                reads=["cmat", "maskb"], writes=[pSk + "ab"[j // 2]])
                        P.add(DVE, lambda e: e.reduce_max(out=sg[:, 0:4], in_=pS[:, :].rearrange("p (h k) -> p h k", h=4), axis=AX.X),
                              reads=[pSk + "a", pSk + "b"], writes=[sgk])
                        P.add(DVE, lambda e: e.scalar_tensor_tensor(
                            out=sg[:, 4:8], in0=sg[:, 0:4], scalar=-0.125, in1=negsink[:, l * 8 + g * 4:l * 8 + g * 4 + 4],
                            op0=ALU.mult, op1=ALU.min),
                            reads=[sgk, "negsink"], writes=[sgk])
                        P.add(POOL, lambda e: e.memset(sg[:, 8:12], 0.0), writes=[sgk + f"s{j}" for j in range(4)])
                        for j in range(4):
                            P.add(ACT, lambda e, j=j: e.activation(
                                out=Pg[:, j, :], in_=pS[:, j * 256:(j + 1) * 256], func=AF.Exp,
                                bias=sg[:, 4 + j:5 + j], scale=0.125, accum_out=sg[:, 8 + j:9 + j]),
                                reads=[pSk + "ab"[j // 2], sgk, sgk + f"s{j}"], writes=[Pk + f"_{j}", sgk + f"s{j}"])

                    def stage_B2(u):
                        b, g, pS, pSk, sg, sgk, Pg, Pk, PTg, PTk = unit(u)
                        P.add(DVE, lambda e: e.tensor_tensor(out=sg[:, 12:16], in0=sg[:, 4:8],
                                                              in1=sinkb[:, l * 8 + g * 4:l * 8 + g * 4 + 4], op=ALU.add),
                              reads=[sgk, "sinkb"], writes=[sgk + "t"])
                        P.add(ACT, lambda e: e.activation(out=sg[:, 16:20], in_=sg[:, 12:16], func=AF.Exp),
                              reads=[sgk + "t"], writes=[sgk + "e"])

                    def stage_B2b(u):
                        b, g, pS, pSk, sg, sgk, Pg, Pk, PTg, PTk = unit(u)
                        P.add(DVE, lambda e: e.tensor_tensor(out=sg[:, 20:24], in0=sg[:, 8:12], in1=sg[:, 16:20], op=ALU.add),
                              reads=[sgk + f"s{j}" for j in range(4)] + [sgk + "e"], writes=[sgk + "d"])
                        P.add(DVE, lambda e: e.reciprocal(out=sg[:, 24:28], in_=sg[:, 20:24]),
                              reads=[sgk + "d"], writes=[sgk + "r"])

                    def stage_C(u):
                        b, g, pS, pSk, sg, sgk, Pg, Pk, PTg, PTk = unit(u)
                        for j in range(4):
                            for kb in range(2):
                                P.add(PE, lambda e, j=j, kb=kb: e.transpose(
                                    out=psPT[:, (j * 2 + kb) * 128:(j * 2 + kb + 1) * 128], in_=Pg[:, j, kb * 128:(kb + 1) * 128], identity=ident),
                                    reads=[Pk + f"_{j}", "cmat"], writes=["psPT"])
                        if u % 2 == 0:
                            P.add(DVE, lambda e: e.tensor_copy(out=PTg[:, :, :], in_=psPT[:, :].rearrange("p (a q) -> p a q", a=8)),
                                  reads=["psPT"], writes=[PTk])
                        else:
                            P.add(ACT, lambda e: e.activation(out=PTg[:, :, :], in_=psPT[:, :].rearrange("p (a q) -> p a q", a=8), func=AF.Copy),
                                  reads=["psPT"], writes=[PTk])

                    def stage_D(u):
                        b, g, pS, pSk, sg, sgk, Pg, Pk, PTg, PTk = unit(u)
                        for j in range(4):
                            h = g * 4 + j
                            for kb in range(2):
                                P.add(PE, lambda e, j=j, kb=kb, h=h: e.matmul(
                                    psO[:, h * 64:(h + 1) * 64], lhsT=PTg[:, j * 2 + kb, :], rhs=Vt[l][:, b + kb, g * 64:(g + 1) * 64],
                                    start=(kb == 0), stop=(kb == 1)),
                                    reads=[PTk, f"Vt{l}"], writes=["psO"])
                        for j in range(4):
                            h = g * 4 + j
                            P.add(DVE, lambda e, j=j, h=h: e.tensor_scalar(
                                out=yats[b % 2][:, h * 64:(h + 1) * 64], in0=psO[:, h * 64:(h + 1) * 64], scalar1=sg[:, 24 + j:25 + j],
                                scalar2=None, op0=ALU.mult),
                                reads=["psO", sgk + "r"], writes=[f"yat{b % 2}"])
                        if g == 1:
                            stage_E1(b)

                    def stage_E1(b):
                        yat = yats[b % 2]
                        yk = f"yat{b % 2}"
                        yan = yans[b % 2]
                        ynk = f"yan{b % 2}"
                        s2 = st2[:, (b % 2) * 4:(b % 2) * 4 + 4]
                        s2k = f"st2_{b % 2}"
                        P.add(POOL, lambda e: e.memset(s2[:, 0:1], 0.0), writes=[s2k])
                        P.add(ACT, lambda e: e.activation(out=sqj[:], in_=yat[:], func=AF.Square, accum_out=s2[:, 0:1]),
                              reads=[yk, s2k], writes=["sq7", s2k])
                        P.add(ACT, lambda e: e.activation(out=s2[:, 1:2], in_=s2[:, 0:1], func=AF.Ln, bias=epsb[:, 0:1], scale=1.0 / 512),
                              reads=[s2k, "epsb"], writes=[s2k + "a"])
                        P.add(ACT, lambda e: e.activation(out=s2[:, 2:3], in_=s2[:, 1:2], func=AF.Exp, scale=-0.5),
                              reads=[s2k + "a"], writes=[s2k + "b"])
                        P.add(DVE, lambda e: e.scalar_tensor_tensor(out=yan[:], in0=yat[:], scalar=s2[:, 2:3], in1=gattn[:, l, :],
                                                                    op0=ALU.mult, op1=ALU.mult),
                              reads=[yk, s2k + "b", "gattn"], writes=[ynk])

                    def stage_E2(b):
                        yan = yans[b % 2]
                        ynk = f"yan{b % 2}"
                        pt, pk = psg()
                        ptb = pt[:, 0:256].bitcast(BF16)
                        for c in range(4):
                            P.add(PE, lambda e, c=c: e.transpose(out=ptb[:, c * 128:(c + 1) * 128], in_=yan[:, c * 128:(c + 1) * 128], identity=ident),
                                  reads=[ynk, "cmat"], writes=[pk])
                        P.add(ACT, lambda e: e.activation(out=yT[:, 0:4, b * 128:(b + 1) * 128],
                                                          in_=ptb[:, 0:512].rearrange("p (c q) -> p c q", c=4), func=AF.Copy),
                              reads=[pk], writes=[f"yT{c}" for c in range(4)] + ["Umix"])

                    cg = conv_gen()
                    nun = 2 * nb
                    gpool[0] = G2

                    def filler(n=1):
                        for _ in range(n):
                            next(cg, None)

                    for step in range(nun + 2):
                        if 1 <= step <= nun:
                            stage_B2(step - 1)
                        if step < nun:
                            stage_A(step)
                        if 1 <= step <= nun:
                            stage_B2b(step - 1)
                        filler(1)
                        if 1 <= step <= nun:
                            stage_C(step - 1)
                        if step >= 2:
                            stage_D(step - 2)
                        if step % 2 == 1:
                            filler(1)
                        if step >= 4 and step % 2 == 0:
                            stage_E2((step - 4) // 2)
                    stage_E2(nb - 1)
                    for _ in cg:
                        pass
                    gpool[0] = G6
                    P.add(POOL, lambda e: e.tensor_copy(out=kA[l][0:64, 0:128], in_=kA[l][0:64, T:T + 128]), reads=[f"kA{l}"], writes=[f"kA{l}"])
                    P.add(POOL, lambda e: e.tensor_copy(out=kB[l][64:128, 0:128], in_=kB[l][64:128, T:T + 128]), reads=[f"kB{l}"], writes=[f"kB{l}"])
                    P.add(POOL, lambda e: e.tensor_copy(out=Vt[l][:, 0, :], in_=Vt[l][:, nb, :]), reads=[f"Vt{l}"], writes=[f"Vt{l}"])


                    wv, wk = load_piece(ti, l, "out")
                    for d in range(NCH):
                        pt, pk = psg()
                        for ki, kc in enumerate((4, 5, 6, 7, 0, 1, 2, 3)):
                            P.add(PE, lambda e, d=d, kc=kc, ki=ki, pt=pt, wv=wv: e.matmul(
                                pt[:, 0:T], lhsT=wv[:, kc, d * 128:(d + 1) * 128], rhs=yT[:, kc, 0:T], start=(ki == 0), stop=(ki == NCH - 1)),
                                reads=[wk, f"yT{kc}"], writes=[pk])
                        branch_evac(T, l, pt, pk, d, G_POST)
                        if d >= 1:
                            stat_mm(T, d - 1)
                    stat_mm(T, NCH - 1)
                    post_norm_update(T, l)

                    pre_norm_deferred(T, l, G_MPRE)
                    for pj in range(4):
                        wv, wk = load_piece(ti, l, f"up{pj}")
                        for fi in range(8):
                            f = pj * 8 + fi
                            pt, pk = psg()
                            for kc in range(NCH):
                                P.add(PE, lambda e, fi=fi, kc=kc, pt=pt, wv=wv: e.matmul(
                                    pt[:, 0:T], lhsT=wv[:, kc, fi * 128:(fi + 1) * 128], rhs=aT[:, kc, 0:T], start=(kc == 0), stop=(kc == NCH - 1)),
                                    reads=[wk, f"a{kc}"], writes=[pk])
                            if f % 2 == 0:
                                P.add(ACT, lambda e, f=f, pt=pt: e.activation(out=hid[:, f, 0:T], in_=pt[:, 0:T], func=AF.Relu),
                                      reads=[pk], writes=[f"hid{f}", "Umlp"])
                                P.add(ACT, lambda e, f=f: e.activation(out=hid[:, f, 0:T], in_=hid[:, f, 0:T], func=AF.Square),
                                      reads=[f"hid{f}"], writes=[f"hid{f}", "Umlp"])
                            else:
                                P.add(DVE, lambda e, f=f, pt=pt: e.tensor_scalar(out=hid[:, f, 0:T], in0=pt[:, 0:T], scalar1=0.0, scalar2=None, op0=ALU.max),
                                      reads=[pk], writes=[f"hid{f}", "Umlp"])
                                P.add(DVE, lambda e, f=f: e.tensor_tensor(out=hid[:, f, 0:T], in0=hid[:, f, 0:T], in1=hid[:, f, 0:T], op=ALU.mult),
                                      reads=[f"hid{f}"], writes=[f"hid{f}", "Umlp"])
                    for pj in range(4):
                        wv, wk = load_piece(ti, l, f"dn{pj}")
                        for dd in range(2):
                            d = pj * 2 + dd
                            pt, pk = psg()
                            for kc in range(32):
                                P.add(PE, lambda e, dd=dd, kc=kc, pt=pt, wv=wv: e.matmul(
                                    pt[:, 0:T], lhsT=wv[:, kc, dd * 128:(dd + 1) * 128], rhs=hid[:, kc, 0:T], start=(kc == 0), stop=(kc == 31)),
                                    reads=[wk, f"hid{kc}"], writes=[pk])
                            branch_evac(T, l, pt, pk, d, G_MPOST)
                            if d >= 1:
                                stat_mm(T, d - 1)
                    stat_mm(T, NCH - 1)
                    if l == 1 and ti + 1 < len(tile_blocks):
                        for b_ in range(min(2, tile_blocks[ti + 1])):
                            x_load(tok0 + T, b_)
                    post_norm_update(T, l, deferred=True)

                for b in range(nb):
                    gb = blk0 + b
                    if gb < 2:
                        continue
                    xb = xs[b % 2]
                    xk = f"xs{b % 2}"
                    pS = psS[b % 2]
                    pSk = f"psS{b % 2}"
                    for c in range(NCH):
                        P.add(PE, lambda e, c=c, b=b, pS=pS: e.transpose(out=pS[:, c * 128:(c + 1) * 128],
                                                                         in_=hT[:, c, b * 128:(b + 1) * 128], identity=identf[:]),
                              reads=[f"h{c}", "identf"], writes=[pSk + ("a" if c < 4 else "b")])
                    P.add(ACT, lambda e, xb=xb, pS=pS: e.activation(out=xb[:, 0:512], in_=pS[:, 0:512], func=AF.Copy),
                          reads=[pSk + "a"], writes=[xk + "lo"])
                    P.add(DVE, lambda e, xb=xb, pS=pS: e.tensor_copy(out=xb[:, 512:1024], in_=pS[:, 512:1024]),
                          reads=[pSk + "b"], writes=[xk + "hi"])
                    r0 = (gb - 2) * 128
                    P.add(SP, lambda e, xb=xb, r0=r0: e.dma_start(out=out_d[r0:r0 + 128, :], in_=xb[:]), reads=[xk, xk + "lo", xk + "hi"],
                          writes=[xk], dma=True)
                blk0 += nb

        P = Prog()
        program(P)
        _alias_fix(P)
        P.schedule()
        P.start_emit(nc, engs, sems, dsems)
        program(P)
        P.finish_emit()
    return nc


def _alias_fix(P):
    mix_pref = ("qpre", "qT", "yT")
    for op in P.ops:
        keys = op.reads + op.writes
        is_mix = any(k.startswith(mix_pref) for k in keys)
        is_mlp = any(k.startswith("hid") for k in keys)
        op.reads = [k for k in op.reads if k not in ("Umix", "Umlp")]
        op.writes = [k for k in op.writes if k not in ("Umix", "Umlp")]
        if is_mix:
            op.writes.append("Ualias_mix")
        if is_mlp:
            op.writes.append("Ualias_mlp")
    side = None
    for op in P.ops:
        m = "Ualias_mix" in op.writes
        h = "Ualias_mlp" in op.writes
        op.writes = [k for k in op.writes if k not in ("Ualias_mix", "Ualias_mlp")]
        if not (m or h):
            continue
        s = "mix" if m else "mlp"
        if s != side:
            op.writes.append("Uphase")
            side = s
        else:
            op.reads.append("Uphase")


_NC_CACHE = {}


def _prep_shared(mix_pre_g, w_in, conv_w, sinks, attn_out_g, conv_out_g, w_out, mix_post_g,
                 mlp_pre_g, w_up, w_down, mlp_post_g):
    perm = _w_in_perm()
    w_in_p = np.ascontiguousarray(w_in[:, :, perm])
    gtab = np.zeros((128, NG), np.float32)
    for l in range(2):
        o = l * NG_L
        gtab[:, o + G_PRE:o + G_PRE + 8] = mix_pre_g[l].reshape(8, 128).T
        gtab[:, o + G_POST:o + G_POST + 8] = mix_post_g[l].reshape(8, 128).T
        gtab[:, o + G_MPRE:o + G_MPRE + 8] = mlp_pre_g[l].reshape(8, 128).T
        gtab[:, o + G_MPOST:o + G_MPOST + 8] = mlp_post_g[l].reshape(8, 128).T
        gtab[:, o + G_CONV:o + G_CONV + 4] = conv_out_g[l].reshape(4, 128).T
        for k in range(3):
            gtab[:, o + G_CW + k * 4:o + G_CW + k * 4 + 4] = conv_w[l, k].reshape(4, 128).T
    gattn = np.ascontiguousarray(np.broadcast_to(attn_out_g[:, None, :], (2, 128, 512))).astype(np.float32)
    sinkb = np.ascontiguousarray(np.broadcast_to(sinks.reshape(1, 16), (128, 16))).astype(np.float32)
    cmat = np.concatenate([np.eye(128, dtype=np.float32), _perm_matrix(), np.ones((128, 128), np.float32)], axis=1)
    return dict(w_in=w_in_p, w_out=np.ascontiguousarray(w_out), w_up=np.ascontiguousarray(w_up),
                w_down=np.ascontiguousarray(w_down), gtab=gtab, gattn=gattn, sinkb=sinkb, cmat=cmat)


def _core_inputs(x, meta_tokens, core):
    b, half = core // 2, core % 2
    xin = np.zeros((NBLK * 128, D), np.float32)
    if half == 0:
        xin[128 + 112:256] = meta_tokens
        xin[256:] = x[b, 0:4096]
        first_pos = -128 - 112
    else:
        xin[:] = x[b, 3840:8192]
        first_pos = N_META + 3840
    C, S = _rope_tables(first_pos)
    masks = _masks(half == 0)
    masks = np.ascontiguousarray(masks.transpose(1, 0, 2).reshape(128, 4 * 256))
    return dict(xin=xin, ctab=C, stab=S, masks=masks)


def kernel(x, meta_tokens, mix_pre_g, w_in, conv_w, sinks, attn_out_g, conv_out_g, w_out, mix_post_g,
           mlp_pre_g, w_up, w_down, mlp_post_g, _tile_blocks=None, _cores=None):
    x = np.asarray(x, np.float32)
    args = [np.asarray(a, np.float32) for a in (mix_pre_g, w_in, conv_w, sinks, attn_out_g, conv_out_g, w_out,
                                                mix_post_g, mlp_pre_g, w_up, w_down, mlp_post_g)]
    shared = _prep_shared(*args)
    key = tuple(_tile_blocks) if _tile_blocks is not None else None
    if key not in _NC_CACHE:
        _NC_CACHE[key] = build(_tile_blocks)
    nc = _NC_CACHE[key]
    cores = list(range(8)) if _cores is None else _cores
    in_maps = []
    for c in cores:
        m = dict(shared)
        m.update(_core_inputs(x, np.asarray(meta_tokens, np.float32), c))
        in_maps.append(m)
    res = run_bass_kernel_spmd(nc, in_maps, core_ids=list(range(len(cores))))
    if _tile_blocks is not None:
        return [r["out"] for r in res.results]
    out = np.zeros((4, 8192, D), np.float32)
    for i, c in enumerate(cores):
        b, half = c // 2, c % 2
        out[b, half * 4096:(half + 1) * 4096] = res.results[i]["out"]
    return out
```

```python
import contextlib
import numpy as np
import concourse.bass as bass
import concourse.mybir as mybir
from concourse.bass_utils import run_bass_kernel_spmd

F32 = mybir.dt.float32
BF16 = mybir.dt.bfloat16
ALU = mybir.AluOpType
AF = mybir.ActivationFunctionType
AX = mybir.AxisListType

D = 1024
NCH = 8
DFF = 4096
NBLK = 34
TILE_BLOCKS = [4] * 8 + [2]
EPS = 1e-6
NEG = -30000.0
N_META = 16
ROT = 16
THETA = 500000.0
NSLOT = 3
DMA_SLOTS = 20

PE, ACT, DVE, POOL, SP = "pe", "act", "dve", "pool", "sp"
LIMIT = [10 ** 9]


class Op:
    __slots__ = ("eng", "fn", "reads", "writes", "dma", "deps_eng", "deps_dma", "inc", "ticket",
                 "slot", "value", "ndma")

    def __init__(self, eng, fn, reads, writes, dma=False, ndma=1):
        self.eng = eng
        self.fn = fn
        self.reads = reads
        self.writes = writes
        self.dma = dma
        self.ndma = ndma
        self.deps_eng = {}
        self.deps_dma = set()
        self.inc = False
        self.ticket = 0
        self.slot = -1
        self.value = 0


class Prog:
    def __init__(self):
        self.ops = []
        self.mode = "plan"
        self.pos = 0
        self.n = 0

    def add(self, eng, fn, reads=(), writes=(), dma=False, ndma=1):
        self.n += 1
        if self.n > LIMIT[0]:
            return
        if self.mode == "plan":
            writes = list(writes) + [k for k in reads if k.startswith("ps")]
            self.ops.append(Op(eng, None, list(reads), list(writes), dma, ndma))
            import sys as _s
            self.ops[-1].fn = _s._getframe(1).f_lineno
        else:
            op = self.ops[self.pos]
            assert op.eng == eng and op.dma == dma and op.ndma == ndma, (self.pos, op.eng, eng)
            self.emit_one(self.pos, op, fn)
            self.pos += 1

    def schedule(self):
        ops = self.ops
        last_w = {}
        readers = {}
        dma_count = 0
        dma_hist = []
        for i, op in enumerate(ops):
            deps = set()
            for k in op.reads:
                w = last_w.get(k)
                if w is not None:
                    deps.add((w, "raw"))
            for k in op.writes:
                w = last_w.get(k)
                if w is not None:
                    deps.add((w, "waw"))
                for r in readers.get(k, ()):
                    deps.add((r, "war"))
            if op.dma:
                if dma_count >= DMA_SLOTS:
                    deps.add((dma_hist[dma_count - DMA_SLOTS], "raw"))
                dma_hist.append(i)
                dma_count += 1
            for (j, kind) in deps:
                if j == i:
                    continue
                p = ops[j]
                if p.dma:
                    op.deps_dma.add(j)
                    continue
                if p.eng == op.eng and not op.dma:
                    if op.eng == PE:
                        continue
                    if kind != "raw":
                        continue
                cur = op.deps_eng.get(p.eng, -1)
                if j > cur:
                    op.deps_eng[p.eng] = j
            for k in op.reads:
                readers.setdefault(k, []).append(i)
            for k in op.writes:
                last_w[k] = i
                readers[k] = []
        for op in ops:
            for e, j in op.deps_eng.items():
                ops[j].inc = True
        cnt = {}
        for op in ops:
            if op.dma:
                continue
            if op.inc:
                cnt[op.eng] = cnt.get(op.eng, 0) + 1
            op.ticket = cnt.get(op.eng, 0)
        slot_val = [0] * DMA_SLOTS
        k = 0
        for op in ops:
            if op.dma:
                s = k % DMA_SLOTS
                slot_val[s] += 16 * op.ndma
                op.slot = s
                op.value = slot_val[s]
                k += 1

    def start_emit(self, nc, engs, sems, dsems):
        self.mode = "emit"
        self.pos = 0
        self.n = 0
        self.engs = engs
        self.sems = sems
        self.dsems = dsems
        self.waited = {e: {} for e in engs}
        self.waited_dma = {e: {} for e in engs}
        self.dmas = []

    def emit_one(self, i, op, fn):
        ops = self.ops
        e = self.engs[op.eng]
        waited = self.waited[op.eng]
        waited_dma = self.waited_dma[op.eng]
        for pe_, j in op.deps_eng.items():
            t = ops[j].ticket
            if waited.get(pe_, 0) < t:
                e.wait_ge(self.sems[pe_], t)
                waited[pe_] = t
        for j in sorted(op.deps_dma):
            p = ops[j]
            if waited_dma.get(p.slot, 0) < p.value:
                e.wait_ge(self.dsems[p.slot], p.value)
                waited_dma[p.slot] = p.value
        if op.dma:
            insts = fn(e)
            if not isinstance(insts, (list, tuple)):
                insts = [insts]
            assert len(insts) == op.ndma
            for ins in insts:
                ins.then_inc(self.dsems[op.slot], 16)
            self.dmas.append(i)
        else:
            ins = fn(e)
            if op.inc:
                ins.then_inc(self.sems[op.eng], 1)

    def finish_emit(self):
        assert self.pos == len(self.ops), (self.pos, len(self.ops))
        for i in self.dmas:
            op = self.ops[i]
            wd = self.waited_dma[op.eng]
            if wd.get(op.slot, 0) < op.value:
                self.engs[op.eng].wait_ge(self.dsems[op.slot], op.value)
                wd[op.slot] = op.value


def _w_in_perm():
    s_q, s_k, s_v = 0, 512, 640
    s_b, s_c, s_h = 768, 1280, 1792
    perm = []
    for j in range(4):
        perm += list(range(s_q + j * 64, s_q + j * 64 + 64))
        perm += list(range(s_q + (4 + j) * 64, s_q + (4 + j) * 64 + 64))
    perm += list(range(s_k, s_k + 128))
    perm += list(range(s_v, s_v + 128))
    for i in range(4):
        perm += list(range(s_c + i * 128, s_c + (i + 1) * 128))
        perm += list(range(s_h + i * 128, s_h + (i + 1) * 128))
        perm += list(range(s_b + i * 128, s_b + (i + 1) * 128))
    return np.array(perm, dtype=np.int64)


def _rope_tables(first_pos):
    n = NBLK * 128
    pos = (first_pos + np.arange(n)).astype(np.float32)
    inv_freq = np.power(np.float32(THETA), -np.arange(0, ROT, 2, dtype=np.float32) / np.float32(ROT)).astype(np.float32)
    ang = (pos[:, None] * inv_freq[None, :]).astype(np.float32)
    cos = np.cos(ang).astype(np.float32).T
    sin = np.sin(ang).astype(np.float32).T
    C = np.ones((128, n), np.float32)
    S = np.zeros((128, n), np.float32)
    for a in range(2):
        b0 = 64 * a
        C[b0:b0 + 8] = cos
        C[b0 + 8:b0 + 16] = cos
        S[b0:b0 + 8] = -sin
        S[b0 + 8:b0 + 16] = sin
    return C, S


def _perm_matrix():
    P = np.zeros((128, 128), np.float32)
    for a in range(2):
        b0 = 64 * a
        for m in range(8):
            P[b0 + m + 8, b0 + m] = -1.0
            P[b0 + m, b0 + m + 8] = -1.0
    return P


def _masks(is_first_half):
    q = np.arange(128)[:, None]
    k = np.arange(128)[None, :]
    band_prev = (k > q)
    causal = (k <= q)
    allm = np.zeros((128, 128), bool)
    kvalid = (k >= 112) & np.ones((128, 1), bool)
    m = np.zeros((4, 128, 256), bool)
    if is_first_half:
        m[0, :, :128] = allm
        m[0, :, 128:] = allm
        m[1, :, :128] = allm
        m[1, :, 128:] = causal & kvalid
        m[2, :, :128] = band_prev & kvalid
        m[2, :, 128:] = causal
    else:
        m[0, :, :128] = allm
        m[0, :, 128:] = causal
        m[1, :, :128] = band_prev
        m[1, :, 128:] = causal
        m[2, :, :128] = band_prev
        m[2, :, 128:] = causal
    m[3, :, :128] = band_prev
    m[3, :, 128:] = causal
    return np.where(m, 0.0, NEG).astype(np.float32)


G_PRE, G_POST, G_MPRE, G_MPOST, G_CONV, G_CW = 0, 8, 16, 24, 32, 36
NG_L = 48
NG = 2 * NG_L


def build(tile_blocks=None, n_out_blocks=None):
    tile_blocks = TILE_BLOCKS if tile_blocks is None else tile_blocks
    nblk = sum(tile_blocks)
    n_out = nblk - 2
    nc = bass.Bass("TRN2", target_bir_lowering=False)

    def dram_in(name, shape, dt=F32):
        return nc.dram_tensor(name, list(shape), dt, kind="ExternalInput").ap()

    xin = dram_in("xin", [NBLK * 128, D])
    w_in = dram_in("w_in", [2, D, 2304])
    w_out = dram_in("w_out", [2, D, D])
    w_up = dram_in("w_up", [2, D, DFF])
    w_down = dram_in("w_down", [2, DFF, D])
    gtab_d = dram_in("gtab", [128, NG])
    gattn_d = dram_in("gattn", [2, 128, 512])
    sink_d = dram_in("sinkb", [128, 16])
    ctab_d = dram_in("ctab", [128, NBLK * 128])
    stab_d = dram_in("stab", [128, NBLK * 128])
    mask_d = dram_in("masks", [128, 4 * 256])
    cmat_d = dram_in("cmat", [128, 3 * 128])
    out_d = nc.dram_tensor("out", [max(n_out, 1) * 128, D], F32, kind="ExternalOutput").ap()
    scr = nc.dram_tensor("wscr", [2 * 12, 128, 8192], BF16, kind="Internal").ap()

    es = contextlib.ExitStack()
    with es:
        def sb(name, shape, dt):
            return es.enter_context(nc.sbuf_tensor(name, list(shape), dt))

        def ps(name, shape, dt):
            return es.enter_context(nc.psum_tensor(name, list(shape), dt))

        ring = [sb(f"ring{i}", [128, 8192], BF16) for i in range(NSLOT)]
        hT = sb("hT", [128, NCH, 512], F32)
        aT = sb("aT", [128, NCH, 512], BF16)
        sq = sb("sq", [128, NCH, 512], BF16)
        z = sb("z", [128, NCH, 512], F32)
        U = sb("U", [128, 16384], BF16)
        hid = U[:, :].rearrange("p (f t) -> p f t", f=32)
        qpre = U[:, 0:2560].rearrange("p (j t) -> p j t", j=5)
        qT = U[:, 2560:4608].rearrange("p (j t) -> p j t", j=4)
        yT = U[:, 4608:8704].rearrange("p (c t) -> p c t", c=8)
        yconv = sb("yconv", [128, 4, 512], F32)
        rt1 = [sb(f"rt1_{i}", [128, 512], F32) for i in range(2)]
        rt2 = [sb(f"rt2_{i}", [128, 512], F32) for i in range(2)]
        ctmp = rt1
        btmp = rt2
        lnv = sb("lnv", [128, 512], F32)
        rstd = sb("rstd", [128, 512], F32)
        kA = [sb(f"kA{l}", [128, 640], BF16) for l in range(2)]
        kB = [sb(f"kB{l}", [128, 640], BF16) for l in range(2)]
        Vt = [sb(f"Vt{l}", [128, 5, 128], BF16) for l in range(2)]
        Ut = [sb(f"Ut{l}", [128, 4, 520], BF16) for l in range(2)]
        Pm = [sb(f"Pm{i}", [128, 4, 256], BF16) for i in range(2)]
        PTs = [sb(f"PTs{i}", [128, 8, 128], BF16) for i in range(2)]
        yats = [sb(f"yat{i}", [128, 512], F32) for i in range(2)]
        yans = [sb(f"yan{i}", [128, 512], BF16) for i in range(2)]
        r2b = sb("r2b", [128, 512], F32)
        st = [sb(f"stt{i}", [128, 32], F32) for i in range(4)]
        st2 = sb("st2", [128, 8], F32)
        rtk = sb("rtk", [128, 8], F32)
        xs = [sb(f"xs{i}", [128, D], F32) for i in range(2)]
        ctab = sb("ctab_s", [128, 512], F32)
        stab = sb("stab_s", [128, 512], F32)
        gtab = sb("gtab_s", [128, NG], F32)
        gattn = sb("gattn_s", [128, 2, 512], F32)
        sinkb = sb("sinkb_s", [128, 16], F32)
        negsink = sb("negsink", [128, 16], F32)
        epsb = sb("epsb", [128, 1], F32)
        maskb = sb("maskb", [128, 4, 256], BF16)
        cmat = sb("cmat_s", [128, 3, 128], BF16)
        identf = sb("identf", [128, 128], F32)
        diag = sb("diag", [128, 24, 128], BF16)
        ident = cmat[:, 0, :]
        perm = cmat[:, 1, :]
        ones = cmat[:, 2, :]

        psG = [ps(f"psG{i}", [128, 512], F32) for i in range(2)]
        psS = [ps(f"psS{i}", [128, 1024], F32) for i in range(2)]
        psPT = ps("psPT", [128, 1024], BF16)
        psO = ps("psO", [128, 512], F32)

        sem_names = [PE, ACT, DVE, POOL]
        sems = {e: es.enter_context(nc.semaphore(f"s_{e}")) for e in sem_names}
        dsems = [es.enter_context(nc.semaphore(f"d{i}")) for i in range(DMA_SLOTS)]
        engs = {PE: nc.tensor, ACT: nc.scalar, DVE: nc.vector, POOL: nc.gpsimd, SP: nc.sync}

        def program(P):
            gctr = [0]

            G2 = [(psG[0], "psG0"), (psG[1], "psG1")]
            G6 = G2 + [(psS[0][:, 0:512], "psS0a"), (psS[0][:, 512:1024], "psS0b"),
                       (psS[1][:, 0:512], "psS1a"), (psS[1][:, 512:1024], "psS1b")]
            gpool = [G6]

            def psg():
                pool = gpool[0]
                i = gctr[0] % len(pool)
                gctr[0] += 1
                return pool[i]

            sqj = sq[:, 7, :]
            P.add(SP, lambda e: e.dma_start(out=gtab[:], in_=gtab_d[:, :]), writes=["gtab"], dma=True)
            P.add(SP, lambda e: e.dma_start(out=gattn[:], in_=gattn_d.rearrange("l p f -> p l f")), writes=["gattn"], dma=True)
            P.add(SP, lambda e: e.dma_start(out=sinkb[:], in_=sink_d[:, :]), writes=["sinkb"], dma=True)
            P.add(SP, lambda e: e.dma_start(out=identf[:], in_=cmat_d[:, 0:128]), writes=["identf"], dma=True)
            P.add(POOL, lambda e: e.dma_start(out=cmat[:], in_=cmat_d.rearrange("p (a n) -> p a n", a=3)), writes=["cmat"], dma=True)
            P.add(POOL, lambda e: e.dma_start(out=maskb[:], in_=mask_d.rearrange("p (a n) -> p a n", a=4)), writes=["maskb"], dma=True)
            P.add(POOL, lambda e: e.memset(epsb[:], EPS), writes=["epsb"])
            P.add(POOL, lambda e: e.tensor_scalar(out=negsink[:], in0=sinkb[:], scalar1=-1.0, scalar2=None, op0=ALU.mult),
                  reads=["sinkb"], writes=["negsink"])
            for l in range(2):
                P.add(POOL, lambda e, l=l: e.memset(kA[l][:], 0.0), writes=[f"kA{l}"])
                P.add(POOL, lambda e, l=l: e.memset(kB[l][:], 0.0), writes=[f"kB{l}"])
                P.add(POOL, lambda e, l=l: e.memset(Vt[l][:], 0.0), writes=[f"Vt{l}"])
                P.add(POOL, lambda e, l=l: e.memset(Ut[l][:], 0.0), writes=[f"Ut{l}"])
                for i in range(4):
                    for k in range(3):
                        col = l * NG_L + G_CW + k * 4 + i
                        P.add(POOL, lambda e, l=l, i=i, k=k, col=col: e.tensor_scalar(
                            out=diag[:, l * 12 + i * 3 + k, :], in0=identf[:], scalar1=gtab[:, col:col + 1],
                            scalar2=None, op0=ALU.mult),
                            reads=["identf", "gtab"], writes=[f"diag{l}"])

            ring_ctr = [0]
            PIECES = ["in0", "in1", "in2", "out", "up0", "up1", "up2", "up3", "dn0", "dn1", "dn2", "dn3"]

            def piece_src(l, name):
                if name.startswith("in"):
                    j = int(name[2:])
                    src = w_in[l, :, j * 768:(j + 1) * 768].rearrange("(kc p) n -> p kc n", p=128)
                    return src, (8, 768)
                if name == "out":
                    return w_out[l].rearrange("(kc p) n -> p kc n", p=128), (8, 1024)
                if name.startswith("up"):
                    j = int(name[2:])
                    return w_up[l, :, j * 1024:(j + 1) * 1024].rearrange("(kc p) n -> p kc n", p=128), (8, 1024)
                j = int(name[2:])
                return w_down[l, :, j * 256:(j + 1) * 256].rearrange("(kc p) n -> p kc n", p=128), (32, 256)

            SEQ = [(ti_, l_, nm_) for ti_ in range(len(tile_blocks)) for l_ in range(2) for nm_ in PIECES]
            issued = [0]
            LA = 2

            def emit_load(n):
                ti, l, name = SEQ[n]
                s = n % NSLOT
                src, (a, n_) = piece_src(l, name)
                view = ring[s][:, 0:a * n_].rearrange("p (a n) -> p a n", a=a)
                pi = l * 12 + PIECES.index(name)
                key = f"ring{s}"
                skey = f"scr{pi}"
                if ti == 0:
                    q = a // 4

                    def f(e):
                        return [e.dma_start(out=view[:, i * q:(i + 1) * q, :], in_=src[:, i * q:(i + 1) * q, :])
                                for i in range(4)]
                    P.add(POOL, f, writes=[key], dma=True, ndma=4)
                    P.add(SP, lambda e: e.dma_start(out=scr[pi, :, 0:a * n_], in_=ring[s][:, 0:a * n_]),
                          reads=[key], writes=[skey], dma=True)
                else:
                    def f(e):
                        h = (a * n_) // 2
                        return [e.dma_start(out=ring[s][:, i * h:(i + 1) * h], in_=scr[pi, :, i * h:(i + 1) * h])
                                for i in range(2)]
                    P.add(SP, f, reads=[skey], writes=[key], dma=True, ndma=2)

            def load_piece(ti, l, name):
                n = ring_ctr[0]
                ring_ctr[0] += 1
                assert SEQ[n] == (ti, l, name), (SEQ[n], ti, l, name)
                while issued[0] < min(n + 1 + LA, len(SEQ)):
                    emit_load(issued[0])
                    issued[0] += 1
                s = n % NSLOT
                _, (a, n_) = piece_src(l, name)
                view = ring[s][:, 0:a * n_].rearrange("p (a n) -> p a n", a=a)
                return view, f"ring{s}"

            def rms_stats(T, nchunks, inv_n, sqkeys):
                pt, pk = psg()
                for c in range(nchunks):
                    P.add(PE, lambda e, c=c, pt=pt: e.matmul(pt[:, 0:T], lhsT=ones, rhs=sq[:, c, 0:T],
                                                             start=(c == 0), stop=(c == nchunks - 1)),
                          reads=["cmat", sqkeys[c]], writes=[pk])
                P.add(ACT, lambda e, pt=pt: e.activation(out=lnv[:, 0:T], in_=pt[:, 0:T], func=AF.Ln,
                                                         bias=epsb[:, 0:1], scale=inv_n),
                      reads=[pk, "epsb"], writes=["lnv"])
                P.add(ACT, lambda e: e.activation(out=rstd[:, 0:T], in_=lnv[:, 0:T], func=AF.Exp, scale=-0.5),
                      reads=["lnv"], writes=["rstd"])

            def pre_norm(T, l, gcol):
                for c in range(NCH):
                    if c % 2 == 0:
                        P.add(ACT, lambda e, c=c: e.activation(out=sq[:, c, 0:T], in_=hT[:, c, 0:T], func=AF.Square),
                              reads=[f"h{c}"], writes=[f"sq{c}"])
                    else:
                        P.add(DVE, lambda e, c=c: e.tensor_tensor(out=sq[:, c, 0:T], in0=hT[:, c, 0:T], in1=hT[:, c, 0:T],
                                                                   op=ALU.mult),
                              reads=[f"h{c}"], writes=[f"sq{c}"])
                rms_stats(T, NCH, 1.0 / D, [f"sq{c}" for c in range(NCH)])
                for c in range(NCH):
                    col = l * NG_L + gcol + c
                    eng = DVE
                    P.add(eng, lambda e, c=c, col=col: e.scalar_tensor_tensor(
                        out=aT[:, c, 0:T], in0=hT[:, c, 0:T], scalar=gtab[:, col:col + 1], in1=rstd[:, 0:T],
                        op0=ALU.mult, op1=ALU.mult),
                        reads=[f"h{c}", "gtab", "rstd"], writes=[f"a{c}"])

            def stat_mm(T, c):
                P.add(PE, lambda e: e.matmul(psO[:, 0:T], lhsT=ones, rhs=sq[:, c, 0:T], start=(c == 0), stop=(c == NCH - 1)),
                      reads=["cmat", f"sq{c}"], writes=["psO"])

            def pre_norm_deferred(T, l, gcol):
                for c in range(NCH):
                    col = l * NG_L + gcol + c
                    P.add(ACT, lambda e, c=c, col=col: e.activation(out=aT[:, c, 0:T], in_=hT[:, c, 0:T], func=AF.Copy,
                                                                    scale=gtab[:, col:col + 1]),
                          reads=[f"h{c}", "gtab"], writes=[f"a{c}"])
                for c in range(NCH):
                    P.add(ACT, lambda e, c=c: e.activation(out=sq[:, c, 0:T], in_=hT[:, c, 0:T], func=AF.Square),
                          reads=[f"h{c}"], writes=[f"sq{c}"])
                pt, pk = psg()
                for c in range(NCH):
                    P.add(PE, lambda e, c=c, pt=pt: e.matmul(pt[:, 0:T], lhsT=ones, rhs=sq[:, c, 0:T],
                                                             start=(c == 0), stop=(c == NCH - 1)),
                          reads=["cmat", f"sq{c}"], writes=[pk])
                P.add(ACT, lambda e, pt=pt: e.activation(out=r2b[:, 0:T], in_=pt[:, 0:T], func=AF.Ln, bias=epsb[:, 0:1], scale=1.0 / D),
                      reads=[pk, "epsb"], writes=["r2b"])
                P.add(ACT, lambda e: e.activation(out=r2b[:, 0:T], in_=r2b[:, 0:T], func=AF.Exp, scale=-1.0),
                      reads=["r2b"], writes=["r2b"])

            def post_norm_update(T, l, deferred=False):
                if not deferred:
                    P.add(ACT, lambda e: e.activation(out=lnv[:, 0:T], in_=psO[:, 0:T], func=AF.Ln, bias=epsb[:, 0:1], scale=1.0 / D),
                          reads=["psO", "epsb"], writes=["lnv"])
                    P.add(ACT, lambda e: e.activation(out=rstd[:, 0:T], in_=lnv[:, 0:T], func=AF.Exp, scale=-0.5),
                          reads=["lnv"], writes=["rstd"])
                else:
                    P.add(DVE, lambda e: e.tensor_tensor(out=lnv[:, 0:T], in0=psO[:, 0:T], in1=r2b[:, 0:T], op=ALU.mult),
                          reads=["psO", "r2b"], writes=["lnv"])
                    P.add(DVE, lambda e: e.tensor_tensor(out=lnv[:, 0:T], in0=lnv[:, 0:T], in1=r2b[:, 0:T], op=ALU.mult),
                          reads=["lnv", "r2b"], writes=["lnv"])
                    P.add(ACT, lambda e: e.activation(out=lnv[:, 0:T], in_=lnv[:, 0:T], func=AF.Ln, bias=epsb[:, 0:1], scale=1.0 / D),
                          reads=["lnv", "epsb"], writes=["lnv"])
                    P.add(ACT, lambda e: e.activation(out=lnv[:, 0:T], in_=lnv[:, 0:T], func=AF.Exp, scale=-0.5),
                          reads=["lnv"], writes=["lnv"])
                    P.add(DVE, lambda e: e.tensor_tensor(out=rstd[:, 0:T], in0=lnv[:, 0:T], in1=r2b[:, 0:T], op=ALU.mult),
                          reads=["lnv", "r2b"], writes=["rstd"])
                for c in range(NCH):
                    P.add(DVE, lambda e, c=c: e.tensor_tensor(out=z[:, c, 0:T], in0=z[:, c, 0:T], in1=rstd[:, 0:T], op=ALU.mult),
                          reads=[f"z{c}", "rstd"], writes=[f"z{c}"])
                    P.add(DVE, lambda e, c=c: e.tensor_tensor(out=hT[:, c, 0:T], in0=hT[:, c, 0:T], in1=z[:, c, 0:T], op=ALU.add),
                          reads=[f"z{c}", f"h{c}"], writes=[f"h{c}"])

            def branch_evac(T, l, pt, pk, d, gcol):
                col = l * NG_L + gcol + d
                P.add(ACT, lambda e, d=d, pt=pt: e.activation(out=sq[:, d, 0:T], in_=pt[:, 0:T], func=AF.Square),
                      reads=[pk], writes=[f"sq{d}"])
                P.add(DVE, lambda e, d=d, pt=pt, col=col: e.tensor_scalar(out=z[:, d, 0:T], in0=pt[:, 0:T],
                                                                          scalar1=gtab[:, col:col + 1], scalar2=None, op0=ALU.mult),
                      reads=[pk, "gtab"], writes=[f"z{d}"])

            blk0 = 0
            for ti, nb in enumerate(tile_blocks):
                T = nb * 128
                tok0 = blk0 * 128
                P.add(SP, lambda e, tok0=tok0, T=T: e.dma_start(out=ctab[:, 0:T], in_=ctab_d[:, tok0:tok0 + T]),
                      writes=["ctab"], dma=True)
                P.add(SP, lambda e, tok0=tok0, T=T: e.dma_start(out=stab[:, 0:T], in_=stab_d[:, tok0:tok0 + T]),
                      writes=["stab"], dma=True)
                def x_load(tok_base, b):
                    i = b % 2
                    r0 = tok_base + b * 128

                    def f(e):
                        return [e.dma_start(out=rt1[i][:], in_=xin[r0:r0 + 128, 0:512]),
                                e.dma_start(out=rt2[i][:], in_=xin[r0:r0 + 128, 512:1024])]
                    P.add(SP, f, writes=[f"rt1_{i}", f"rt2_{i}"], dma=True, ndma=2)

                def x_transpose(b):
                    i = b % 2
                    pS = psS[b % 2]
                    pSk = f"psS{b % 2}"
                    for c in range(NCH):
                        src = rt1[i] if c < 4 else rt2[i]
                        sk = f"rt1_{i}" if c < 4 else f"rt2_{i}"
                        cc = c % 4
                        P.add(PE, lambda e, c=c, cc=cc, src=src: e.transpose(out=pS[:, c * 128:(c + 1) * 128],
                                                                             in_=src[:, cc * 128:(cc + 1) * 128], identity=identf[:]),
                              reads=[sk, "identf"], writes=[pSk + ("a" if c < 4 else "b")])
                    for hlf in range(2):
                        eng = ACT if hlf == 0 else DVE
                        if eng == ACT:
                            fn = lambda e, hlf=hlf: e.activation(
                                out=hT[:, hlf * 4:(hlf + 1) * 4, b * 128:(b + 1) * 128],
                                in_=pS[:, hlf * 512:(hlf + 1) * 512].rearrange("p (c t) -> p c t", c=4), func=AF.Copy)
                        else:
                            fn = lambda e, hlf=hlf: e.tensor_copy(
                                out=hT[:, hlf * 4:(hlf + 1) * 4, b * 128:(b + 1) * 128],
                                in_=pS[:, hlf * 512:(hlf + 1) * 512].rearrange("p (c t) -> p c t", c=4))
                        P.add(eng, fn, reads=[pSk + "ab"[hlf]], writes=[f"h{c}" for c in range(hlf * 4, hlf * 4 + 4)])

                if ti == 0:
                    for b in range(min(2, nb)):
                        x_load(tok0, b)
                for b in range(nb):
                    x_transpose(b)
                    if b + 2 < nb:
                        x_load(tok0, b + 2)

                for l in range(2):
                    for c in range(NCH):
                        col = l * NG_L + G_PRE + c
                        P.add(ACT, lambda e, c=c, col=col: e.activation(out=aT[:, c, 0:T], in_=hT[:, c, 0:T], func=AF.Copy,
                                                                        scale=gtab[:, col:col + 1]),
                              reads=[f"h{c}", "gtab"], writes=[f"a{c}"])
                    for c in range(NCH):
                        P.add(ACT, lambda e, c=c: e.activation(out=sq[:, c, 0:T], in_=hT[:, c, 0:T], func=AF.Square),
                              reads=[f"h{c}"], writes=[f"sq{c}"])
                    akeys = [f"a{c}" for c in range(NCH)]
                    wv0, wk0 = load_piece(ti, l, "in0")
                    for j in range(5):
                        pt, pk = psg()
                        for kc in range(NCH):
                            P.add(PE, lambda e, j=j, kc=kc, pt=pt: e.matmul(
                                pt[:, 0:T], lhsT=wv0[:, kc, j * 128:(j + 1) * 128], rhs=aT[:, kc, 0:T],
                                start=(kc == 0), stop=(kc == NCH - 1)),
                                reads=[wk0, f"a{kc}"], writes=[pk])
                        P.add(ACT, lambda e, j=j, pt=pt: e.activation(out=qpre[:, j, 0:T], in_=pt[:, 0:T], func=AF.Copy),
                              reads=[pk], writes=[f"qpre{j}", "Umix"])
                    pt, pk = psg()
                    for b in range(nb):
                        for kc in range(NCH):
                            P.add(PE, lambda e, b=b, kc=kc, pt=pt: e.matmul(
                                pt[:, b * 128:(b + 1) * 128], lhsT=aT[:, kc, b * 128:(b + 1) * 128], rhs=wv0[:, kc, 640:768],
                                start=(kc == 0), stop=(kc == NCH - 1)),
                                reads=[wk0, f"a{kc}"], writes=[pk])
                    ptv, pkv = pt, pk
                    pt, pk = psg()
                    for c in range(NCH):
                        P.add(PE, lambda e, c=c, pt=pt: e.matmul(pt[:, 0:T], lhsT=ones, rhs=sq[:, c, 0:T],
                                                                 start=(c == 0), stop=(c == NCH - 1)),
                              reads=["cmat", f"sq{c}"], writes=[pk])
                    P.add(ACT, lambda e, pt=pt: e.activation(out=r2b[:, 0:T], in_=pt[:, 0:T], func=AF.Ln, bias=epsb[:, 0:1], scale=1.0 / D),
                          reads=[pk, "epsb"], writes=["r2b"])
                    P.add(ACT, lambda e: e.activation(out=rstd[:, 0:T], in_=r2b[:, 0:T], func=AF.Exp, scale=-0.5),
                          reads=["r2b"], writes=["rstd"])
                    P.add(ACT, lambda e: e.activation(out=r2b[:, 0:T], in_=r2b[:, 0:T], func=AF.Exp, scale=-1.0),
                          reads=["r2b"], writes=["r2b"])
                    pt2, pk2 = psg()
                    for b in range(nb):
                        for c in range(NCH):
                            P.add(PE, lambda e, b=b, c=c, pt2=pt2: e.matmul(pt2[:, b:b + 1], lhsT=sq[:, c, b * 128:(b + 1) * 128], rhs=ones[:, 0:1],
                                                                         start=(c == 0), stop=(c == NCH - 1)),
                                  reads=["cmat", f"sq{c}"], writes=[pk2])
                    P.add(ACT, lambda e, pt2=pt2: e.activation(out=rtk[:, 0:nb], in_=pt2[:, 0:nb], func=AF.Ln, bias=epsb[:, 0:1], scale=1.0 / D),
                          reads=[pk2, "epsb"], writes=["rtok"])
                    P.add(ACT, lambda e: e.activation(out=rtk[:, 0:nb], in_=rtk[:, 0:nb], func=AF.Exp, scale=-0.5),
                          reads=["rtok"], writes=["rtok"])
                    for b in range(nb):
                        P.add(ACT, lambda e, b=b: e.activation(out=Vt[l][:, 1 + b, :], in_=ptv[:, b * 128:(b + 1) * 128], func=AF.Copy,
                                                               scale=rtk[:, b:b + 1]),
                              reads=[pkv, "rtok"], writes=[f"Vt{l}"])
                    Cs = yconv[:, 0, :]
                    Ss = yconv[:, 1, :]
                    P.add(DVE, lambda e: e.tensor_tensor(out=Cs[:, 0:T], in0=ctab[:, 0:T], in1=rstd[:, 0:T], op=ALU.mult),
                          reads=["ctab", "rstd"], writes=["yconv0"])
                    P.add(DVE, lambda e: e.tensor_tensor(out=Ss[:, 0:T], in0=stab[:, 0:T], in1=rstd[:, 0:T], op=ALU.mult),
                          reads=["stab", "rstd"], writes=["yconv1"])
                    for j in (4, 0, 1, 2, 3):
                        qc = rt1[j % 2][:, 0:256].bitcast(BF16)
                        qs = rt2[j % 2][:, 0:256].bitcast(BF16)
                        P.add(DVE, lambda e, j=j, qc=qc: e.tensor_tensor(out=qc[:, 0:T], in0=qpre[:, j, 0:T], in1=Cs[:, 0:T], op=ALU.mult),
                              reads=[f"qpre{j}", "yconv0"], writes=[f"rt1_{j % 2}"])
                        P.add(DVE, lambda e, j=j, qs=qs: e.tensor_tensor(out=qs[:, 0:T], in0=qpre[:, j, 0:T], in1=Ss[:, 0:T], op=ALU.mult),
                              reads=[f"qpre{j}", "yconv1"], writes=[f"rt2_{j % 2}"])
                        pr, prk = psg()
                        P.add(PE, lambda e, pr=pr, qc=qc: e.matmul(pr[:, 0:T], lhsT=ident, rhs=qc[:, 0:T], start=True, stop=False),
                              reads=["cmat", f"rt1_{j % 2}"], writes=[prk])
                        P.add(PE, lambda e, pr=pr, qs=qs: e.matmul(pr[:, 0:T], lhsT=perm, rhs=qs[:, 0:T], start=False, stop=True),
                              reads=["cmat", f"rt2_{j % 2}"], writes=[prk])
                        if j < 4:
                            P.add(ACT, lambda e, j=j, pr=pr: e.activation(out=qT[:, j, 0:T], in_=pr[:, 0:T], func=AF.Copy),
                                  reads=[prk], writes=[f"qT{j}", "Umix"])
                        else:
                            P.add(ACT, lambda e, pr=pr: e.activation(out=kA[l][0:64, 128:128 + T], in_=pr[0:64, 0:T], func=AF.Copy),
                                  reads=[prk], writes=[f"kA{l}"])
                            P.add(ACT, lambda e, pr=pr: e.activation(out=kB[l][64:128, 128:128 + T], in_=pr[64:128, 0:T], func=AF.Copy),
                                  reads=[prk], writes=[f"kB{l}"])

                    def conv_gen():
                        wv = wk = None
                        for i in range(4):
                            if i % 2 == 0:
                                wv, wk = load_piece(ti, l, f"in{1 + i // 2}")
                            base = (i % 2) * 384
                            ct = ctmp[i % 2]
                            bt = btmp[i % 2]
                            pt, pk = psg()
                            for kc in range(NCH):
                                P.add(PE, lambda e, kc=kc, pt=pt, wv=wv, base=base: e.matmul(
                                    pt[:, 0:T], lhsT=wv[:, kc, base:base + 128], rhs=aT[:, kc, 0:T], start=(kc == 0), stop=(kc == NCH - 1)),
                                    reads=[wk, f"a{kc}"], writes=[pk])
                            P.add(DVE, lambda e, pt=pt, ct=ct: e.tensor_tensor(out=ct[:, 0:T], in0=pt[:, 0:T], in1=r2b[:, 0:T], op=ALU.mult),
                                  reads=[pk, "r2b"], writes=[f"rt1_{i % 2}"])
                            yield
                            pt, pk = psg()
                            for kc in range(NCH):
                                P.add(PE, lambda e, kc=kc, pt=pt, wv=wv, base=base: e.matmul(
                                    pt[:, 0:T], lhsT=wv[:, kc, base + 128:base + 256], rhs=aT[:, kc, 0:T], start=(kc == 0), stop=(kc == NCH - 1)),
                                    reads=[wk, f"a{kc}"], writes=[pk])
                            P.add(DVE, lambda e, pt=pt, ct=ct, i=i: e.tensor_tensor(out=Ut[l][:, i, 2:2 + T], in0=pt[:, 0:T], in1=ct[:, 0:T], op=ALU.mult),
                                  reads=[pk, f"rt1_{i % 2}"], writes=[f"Ut{l}_{i}"])
                            yield
                            pt, pk = psg()
                            for kc in range(NCH):
                                P.add(PE, lambda e, kc=kc, pt=pt, wv=wv, base=base: e.matmul(
                                    pt[:, 0:T], lhsT=wv[:, kc, base + 256:base + 384], rhs=aT[:, kc, 0:T], start=(kc == 0), stop=(kc == NCH - 1)),
                                    reads=[wk, f"a{kc}"], writes=[pk])
                            P.add(DVE, lambda e, pt=pt, bt=bt: e.tensor_tensor(out=bt[:, 0:T], in0=pt[:, 0:T], in1=rstd[:, 0:T], op=ALU.mult),
                                  reads=[pk, "rstd"], writes=[f"rt2_{i % 2}"])
                            pt, pk = psg()
                            for k in range(3):
                                P.add(PE, lambda e, k=k, pt=pt, i=i: e.matmul(
                                    pt[:, 0:T], lhsT=diag[:, l * 12 + i * 3 + k, :], rhs=Ut[l][:, i, k:k + T], start=(k == 0), stop=(k == 2)),
                                    reads=[f"diag{l}", f"Ut{l}_{i}"], writes=[pk])
                            P.add(DVE, lambda e, pt=pt, bt=bt, i=i: e.tensor_tensor(out=yconv[:, i, 0:T], in0=pt[:, 0:T], in1=bt[:, 0:T], op=ALU.mult),
                                  reads=[pk, f"rt2_{i % 2}"], writes=[f"yconv{i}"])
                            P.add(ACT, lambda e, i=i: e.activation(out=sq[:, i, 0:T], in_=yconv[:, i, 0:T], func=AF.Square),
                                  reads=[f"yconv{i}"], writes=[f"sq{i}"])
                            P.add(POOL, lambda e, i=i: e.tensor_copy(out=Ut[l][:, i, 0:2], in_=Ut[l][:, i, T:T + 2]),
                                  reads=[f"Ut{l}_{i}"], writes=[f"Ut{l}_{i}"])
                            yield
                        rms_stats(T, 4, 1.0 / 512, [f"sq{i}" for i in range(4)])
                        for i in range(4):
                            col = l * NG_L + G_CONV + i
                            P.add(DVE, lambda e, i=i, col=col: e.scalar_tensor_tensor(
                                out=yT[:, 4 + i, 0:T], in0=yconv[:, i, 0:T], scalar=gtab[:, col:col + 1], in1=rstd[:, 0:T],
                                op0=ALU.mult, op1=ALU.mult),
                                reads=[f"yconv{i}", "gtab", "rstd"], writes=[f"yT{4 + i}", "Umix"])
                        yield

                    def unit(u):
                        b, g = u // 2, u % 2
                        return b, g, psS[u % 2], f"psS{u % 2}", st[u % 4], f"st{u % 4}", Pm[u % 2], f"Pm{u % 2}", PTs[u % 2], f"PTs{u % 2}"

                    def stage_A(u):
                        b, g, pS, pSk, sg, sgk, Pg, Pk, PTg, PTk = unit(u)
                        mv = min(blk0 + b, 3)
                        k0 = b * 128
                        kbuf = kA[l] if g == 0 else kB[l]
                        kkey = f"kA{l}" if g == 0 else f"kB{l}"
                        for j in range(4):
                            P.add(PE, lambda e, j=j: e.matmul(
                                pS[:, j * 256:(j + 1) * 256], lhsT=qT[:, j, b * 128:(b + 1) * 128], rhs=kbuf[:, k0:k0 + 256],
                                start=True, stop=False),
                                reads=[f"qT{j}", kkey], writes=[pSk + "ab"[j // 2]])
                            P.add(PE, lambda e, j=j: e.matmul(
                                pS[:, j * 256:(j + 1) * 256], lhsT=ident, rhs=maskb[:, mv, :], start=False, stop=True),
                                reads=["cmat", "maskb"], writes=[pSk + "ab"[j // 2]])
                        P.add(DVE, lambda e: e.reduce_max(out=sg[:, 0:4], in_=pS[:, :].rearrange("p (h k) -> p h k", h=4), axis=AX.X),
                              reads=[pSk + "a", pSk + "b"], writes=[sgk])
                        P.add(DVE, lambda e: e.scalar_tensor_tensor(
                            out=sg[:, 4:8], in0=sg[:, 0:4], scalar=-0.125, in1=negsink[:, l * 8 + g * 4:l * 8 + g * 4 + 4],
                            op0=ALU.mult, op1=ALU.min),
                            reads=[sgk, "negsink"], writes=[sgk])
                        P.add(POOL, lambda e: e.memset(sg[:, 8:12], 0.0), writes=[sgk + f"s{j}" for j in range(4)])
                        for j in range(4):
                            P.add(ACT, lambda e, j=j: e.activation(
                                out=Pg[:, j, :], in_=pS[:, j * 256:(j + 1) * 256], func=AF.Exp,
                                bias=sg[:, 4 + j:5 + j], scale=0.125, accum_out=sg[:, 8 + j:9 + j]),
                                reads=[pSk + "ab"[j // 2], sgk, sgk + f"s{j}"], writes=[Pk + f"_{j}", sgk + f"s{j}"])

                    def stage_B2(u):
                        b, g, pS, pSk, sg, sgk, Pg, Pk, PTg, PTk = unit(u)
                        P.add(DVE, lambda e: e.tensor_tensor(out=sg[:, 12:16], in0=sg[:, 4:8],
                                                              in1=sinkb[:, l * 8 + g * 4:l * 8 + g * 4 + 4], op=ALU.add),
                              reads=[sgk, "sinkb"], writes=[sgk + "t"])
                        P.add(ACT, lambda e: e.activation(out=sg[:, 16:20], in_=sg[:, 12:16], func=AF.Exp),
                              reads=[sgk + "t"], writes=[sgk + "e"])

                    def stage_B2b(u):
                        b, g, pS, pSk, sg, sgk, Pg, Pk, PTg, PTk = unit(u)
                        P.add(DVE, lambda e: e.tensor_tensor(out=sg[:, 20:24], in0=sg[:, 8:12], in1=sg[:, 16:20], op=ALU.add),
                              reads=[sgk + f"s{j}" for j in range(4)] + [sgk + "e"], writes=[sgk + "d"])
                        P.add(DVE, lambda e: e.reciprocal(out=sg[:, 24:28], in_=sg[:, 20:24]),
                              reads=[sgk + "d"], writes=[sgk + "r"])

                    def stage_C(u):
                        b, g, pS, pSk, sg, sgk, Pg, Pk, PTg, PTk = unit(u)
                        for j in range(4):
                            for kb in range(2):
                                P.add(PE, lambda e, j=j, kb=kb: e.transpose(
                                    out=psPT[:, (j * 2 + kb) * 128:(j * 2 + kb + 1) * 128], in_=Pg[:, j, kb * 128:(kb + 1) * 128], identity=ident),
                                    reads=[Pk + f"_{j}", "cmat"], writes=["psPT"])
                        if u % 2 == 0:
                            P.add(DVE, lambda e: e.tensor_copy(out=PTg[:, :, :], in_=psPT[:, :].rearrange("p (a q) -> p a q", a=8)),
                                  reads=["psPT"], writes=[PTk])
                        else:
                            P.add(ACT, lambda e: e.activation(out=PTg[:, :, :], in_=psPT[:, :].rearrange("p (a q) -> p a q", a=8), func=AF.Copy),
                                  reads=["psPT"], writes=[PTk])

                    def stage_D(u):
                        b, g, pS, pSk, sg, sgk, Pg, Pk, PTg, PTk = unit(u)
                        for j in range(4):
                            h = g * 4 + j
                            for kb in range(2):
                                P.add(PE, lambda e, j=j, kb=kb, h=h: e.matmul(
                                    psO[:, h * 64:(h + 1) * 64], lhsT=PTg[:, j * 2 + kb, :], rhs=Vt[l][:, b + kb, g * 64:(g + 1) * 64],
                                    start=(kb == 0), stop=(kb == 1)),
                                    reads=[PTk, f"Vt{l}"], writes=["psO"])
                        for j in range(4):
                            h = g * 4 + j
                            P.add(DVE, lambda e, j=j, h=h: e.tensor_scalar(
                                out=yats[b % 2][:, h * 64:(h + 1) * 64], in0=psO[:, h * 64:(h + 1) * 64], scalar1=sg[:, 24 + j:25 + j],
                                scalar2=None, op0=ALU.mult),
                                reads=["psO", sgk + "r"], writes=[f"yat{b % 2}"])
                        if g == 1:
                            stage_E1(b)

                    def stage_E1(b):
                        yat = yats[b % 2]
                        yk = f"yat{b % 2}"
                        yan = yans[b % 2]
                        ynk = f"yan{b % 2}"
                        s2 = st2[:, (b % 2) * 4:(b % 2) * 4 + 4]
                        s2k = f"st2_{b % 2}"
                        P.add(POOL, lambda e: e.memset(s2[:, 0:1], 0.0), writes=[s2k])
                        P.add(ACT, lambda e: e.activation(out=sqj[:], in_=yat[:], func=AF.Square, accum_out=s2[:, 0:1]),
                              reads=[yk, s2k], writes=["sq7", s2k])
                        P.add(ACT, lambda e: e.activation(out=s2[:, 1:2], in_=s2[:, 0:1], func=AF.Ln, bias=epsb[:, 0:1], scale=1.0 / 512),
                              reads=[s2k, "epsb"], writes=[s2k + "a"])
                        P.add(ACT, lambda e: e.activation(out=s2[:, 2:3], in_=s2[:, 1:2], func=AF.Exp, scale=-0.5),
                              reads=[s2k + "a"], writes=[s2k + "b"])
                        P.add(DVE, lambda e: e.scalar_tensor_tensor(out=yan[:], in0=yat[:], scalar=s2[:, 2:3], in1=gattn[:, l, :],
                                                                    op0=ALU.mult, op1=ALU.mult),
                              reads=[yk, s2k + "b", "gattn"], writes=[ynk])

                    def stage_E2(b):
                        yan = yans[b % 2]
                        ynk = f"yan{b % 2}"
                        pt, pk = psg()
                        ptb = pt[:, 0:256].bitcast(BF16)
                        for c in range(4):
                            P.add(PE, lambda e, c=c: e.transpose(out=ptb[:, c * 128:(c + 1) * 128], in_=yan[:, c * 128:(c + 1) * 128], identity=ident),
                                  reads=[ynk, "cmat"], writes=[pk])
                        P.add(ACT, lambda e: e.activation(out=yT[:, 0:4, b * 128:(b + 1) * 128],
                                                          in_=ptb[:, 0:512].rearrange("p (c q) -> p c q", c=4), func=AF.Copy),
                              reads=[pk], writes=[f"yT{c}" for c in range(4)] + ["Umix"])

                    cg = conv_gen()
                    nun = 2 * nb
                    gpool[0] = G2

                    def filler(n=1):
                        for _ in range(n):
                            next(cg, None)

                    for step in range(nun + 2):
                        if 1 <= step <= nun:
                            stage_B2(step - 1)
                        if step < nun:
                            stage_A(step)
                        if 1 <= step <= nun:
                            stage_B2b(step - 1)
                        filler(1)
                        if 1 <= step <= nun:
                            stage_C(step - 1)
                        if step >= 2:
                            stage_D(step - 2)
                        if step % 2 == 1:
                            filler(1)
                        if step >= 4 and step % 2 == 0:
                            stage_E2((step - 4) // 2)
                    stage_E2(nb - 1)
                    for _ in cg:
                        pass
                    gpool[0] = G6
                    P.add(POOL, lambda e: e.tensor_copy(out=kA[l][0:64, 0:128], in_=kA[l][0:64, T:T + 128]), reads=[f"kA{l}"], writes=[f"kA{l}"])
                    P.add(POOL, lambda e: e.tensor_copy(out=kB[l][64:128, 0:128], in_=kB[l][64:128, T:T + 128]), reads=[f"kB{l}"], writes=[f"kB{l}"])
                    P.add(POOL, lambda e: e.tensor_copy(out=Vt[l][:, 0, :], in_=Vt[l][:, nb, :]), reads=[f"Vt{l}"], writes=[f"Vt{l}"])


                    wv, wk = load_piece(ti, l, "out")
                    for d in range(NCH):
                        pt, pk = psg()
                        for ki, kc in enumerate((4, 5, 6, 7, 0, 1, 2, 3)):
                            P.add(PE, lambda e, d=d, kc=kc, ki=ki, pt=pt, wv=wv: e.matmul(
                                pt[:, 0:T], lhsT=wv[:, kc, d * 128:(d + 1) * 128], rhs=yT[:, kc, 0:T], start=(ki == 0), stop=(ki == NCH - 1)),
                                reads=[wk, f"yT{kc}"], writes=[pk])
                        branch_evac(T, l, pt, pk, d, G_POST)
                        if d >= 1:
                            stat_mm(T, d - 1)
                    stat_mm(T, NCH - 1)
                    post_norm_update(T, l)

                    pre_norm_deferred(T, l, G_MPRE)
                    for pj in range(4):
                        wv, wk = load_piece(ti, l, f"up{pj}")
                        for fi in range(8):
                            f = pj * 8 + fi
                            pt, pk = psg()
                            for kc in range(NCH):
                                P.add(PE, lambda e, fi=fi, kc=kc, pt=pt, wv=wv: e.matmul(
                                    pt[:, 0:T], lhsT=wv[:, kc, fi * 128:(fi + 1) * 128], rhs=aT[:, kc, 0:T], start=(kc == 0), stop=(kc == NCH - 1)),
                                    reads=[wk, f"a{kc}"], writes=[pk])
                            if f % 2 == 0:
                                P.add(ACT, lambda e, f=f, pt=pt: e.activation(out=hid[:, f, 0:T], in_=pt[:, 0:T], func=AF.Relu),
                                      reads=[pk], writes=[f"hid{f}", "Umlp"])
                                P.add(ACT, lambda e, f=f: e.activation(out=hid[:, f, 0:T], in_=hid[:, f, 0:T], func=AF.Square),
                                      reads=[f"hid{f}"], writes=[f"hid{f}", "Umlp"])
                            else:
                                P.add(DVE, lambda e, f=f, pt=pt: e.tensor_scalar(out=hid[:, f, 0:T], in0=pt[:, 0:T], scalar1=0.0, scalar2=None, op0=ALU.max),
                                      reads=[pk], writes=[f"hid{f}", "Umlp"])
                                P.add(DVE, lambda e, f=f: e.tensor_tensor(out=hid[:, f, 0:T], in0=hid[:, f, 0:T], in1=hid[:, f, 0:T], op=ALU.mult),
                                      reads=[f"hid{f}"], writes=[f"hid{f}", "Umlp"])
                    for pj in range(4):
                        wv, wk = load_piece(ti, l, f"dn{pj}")
                        for dd in range(2):
                            d = pj * 2 + dd
                            pt, pk = psg()
                            for kc in range(32):
                                P.add(PE, lambda e, dd=dd, kc=kc, pt=pt, wv=wv: e.matmul(
                                    pt[:, 0:T], lhsT=wv[:, kc, dd * 128:(dd + 1) * 128], rhs=hid[:, kc, 0:T], start=(kc == 0), stop=(kc == 31)),
                                    reads=[wk, f"hid{kc}"], writes=[pk])
                            branch_evac(T, l, pt, pk, d, G_MPOST)
                            if d >= 1:
                                stat_mm(T, d - 1)
                    stat_mm(T, NCH - 1)
                    if l == 1 and ti + 1 < len(tile_blocks):
                        for b_ in range(min(2, tile_blocks[ti + 1])):
                            x_load(tok0 + T, b_)
                    post_norm_update(T, l, deferred=True)

                for b in range(nb):
                    gb = blk0 + b
                    if gb < 2:
                        continue
                    xb = xs[b % 2]
                    xk = f"xs{b % 2}"
                    pS = psS[b % 2]
                    pSk = f"psS{b % 2}"
                    for c in range(NCH):
                        P.add(PE, lambda e, c=c, b=b, pS=pS: e.transpose(out=pS[:, c * 128:(c + 1) * 128],
                                                                         in_=hT[:, c, b * 128:(b + 1) * 128], identity=identf[:]),
                              reads=[f"h{c}", "identf"], writes=[pSk + ("a" if c < 4 else "b")])
                    P.add(ACT, lambda e, xb=xb, pS=pS: e.activation(out=xb[:, 0:512], in_=pS[:, 0:512], func=AF.Copy),
                          reads=[pSk + "a"], writes=[xk + "lo"])
                    P.add(DVE, lambda e, xb=xb, pS=pS: e.tensor_copy(out=xb[:, 512:1024], in_=pS[:, 512:1024]),
                          reads=[pSk + "b"], writes=[xk + "hi"])
                    r0 = (gb - 2) * 128
                    P.add(SP, lambda e, xb=xb, r0=r0: e.dma_start(out=out_d[r0:r0 + 128, :], in_=xb[:]), reads=[xk, xk + "lo", xk + "hi"],
                          writes=[xk], dma=True)
                blk0 += nb

        P = Prog()
        program(P)
        _alias_fix(P)
        P.schedule()
        P.start_emit(nc, engs, sems, dsems)
        program(P)
        P.finish_emit()
    return nc


def _alias_fix(P):
    mix_pref = ("qpre", "qT", "yT")
    for op in P.ops:
        keys = op.reads + op.writes
        is_mix = any(k.startswith(mix_pref) for k in keys)
        is_mlp = any(k.startswith("hid") for k in keys)
        op.reads = [k for k in op.reads if k not in ("Umix", "Umlp")]
        op.writes = [k for k in op.writes if k not in ("Umix", "Umlp")]
        if is_mix:
            op.writes.append("Ualias_mix")
        if is_mlp:
            op.writes.append("Ualias_mlp")
    side = None
    for op in P.ops:
        m = "Ualias_mix" in op.writes
        h = "Ualias_mlp" in op.writes
        op.writes = [k for k in op.writes if k not in ("Ualias_mix", "Ualias_mlp")]
        if not (m or h):
            continue
        s = "mix" if m else "mlp"
        if s != side:
            op.writes.append("Uphase")
            side = s
        else:
            op.reads.append("Uphase")


_NC_CACHE = {}


def _prep_shared(mix_pre_g, w_in, conv_w, sinks, attn_out_g, conv_out_g, w_out, mix_post_g,
                 mlp_pre_g, w_up, w_down, mlp_post_g):
    perm = _w_in_perm()
    w_in_p = np.ascontiguousarray(w_in[:, :, perm])
    gtab = np.zeros((128, NG), np.float32)
    for l in range(2):
        o = l * NG_L
        gtab[:, o + G_PRE:o + G_PRE + 8] = mix_pre_g[l].reshape(8, 128).T
        gtab[:, o + G_POST:o + G_POST + 8] = mix_post_g[l].reshape(8, 128).T
        gtab[:, o + G_MPRE:o + G_MPRE + 8] = mlp_pre_g[l].reshape(8, 128).T
        gtab[:, o + G_MPOST:o + G_MPOST + 8] = mlp_post_g[l].reshape(8, 128).T
        gtab[:, o + G_CONV:o + G_CONV + 4] = conv_out_g[l].reshape(4, 128).T
        for k in range(3):
            gtab[:, o + G_CW + k * 4:o + G_CW + k * 4 + 4] = conv_w[l, k].reshape(4, 128).T
    gattn = np.ascontiguousarray(np.broadcast_to(attn_out_g[:, None, :], (2, 128, 512))).astype(np.float32)
    sinkb = np.ascontiguousarray(np.broadcast_to(sinks.reshape(1, 16), (128, 16))).astype(np.float32)
    cmat = np.concatenate([np.eye(128, dtype=np.float32), _perm_matrix(), np.ones((128, 128), np.float32)], axis=1)
    return dict(w_in=w_in_p, w_out=np.ascontiguousarray(w_out), w_up=np.ascontiguousarray(w_up),
                w_down=np.ascontiguousarray(w_down), gtab=gtab, gattn=gattn, sinkb=sinkb, cmat=cmat)


def _core_inputs(x, meta_tokens, core):
    b, half = core // 2, core % 2
    xin = np.zeros((NBLK * 128, D), np.float32)
    if half == 0:
        xin[128 + 112:256] = meta_tokens
        xin[256:] = x[b, 0:4096]
        first_pos = -128 - 112
    else:
        xin[:] = x[b, 3840:8192]
        first_pos = N_META + 3840
    C, S = _rope_tables(first_pos)
    masks = _masks(half == 0)
    masks = np.ascontiguousarray(masks.transpose(1, 0, 2).reshape(128, 4 * 256))
    return dict(xin=xin, ctab=C, stab=S, masks=masks)


def kernel(x, meta_tokens, mix_pre_g, w_in, conv_w, sinks, attn_out_g, conv_out_g, w_out, mix_post_g,
           mlp_pre_g, w_up, w_down, mlp_post_g, _tile_blocks=None, _cores=None):
    x = np.asarray(x, np.float32)
    args = [np.asarray(a, np.float32) for a in (mix_pre_g, w_in, conv_w, sinks, attn_out_g, conv_out_g, w_out,
                                                mix_post_g, mlp_pre_g, w_up, w_down, mlp_post_g)]
    shared = _prep_shared(*args)
    key = tuple(_tile_blocks) if _tile_blocks is not None else None
    if key not in _NC_CACHE:
        _NC_CACHE[key] = build(_tile_blocks)
    nc = _NC_CACHE[key]
    cores = list(range(8)) if _cores is None else _cores
    in_maps = []
    for c in cores:
        m = dict(shared)
        m.update(_core_inputs(x, np.asarray(meta_tokens, np.float32), c))
        in_maps.append(m)
    res = run_bass_kernel_spmd(nc, in_maps, core_ids=list(range(len(cores))))
    if _tile_blocks is not None:
        return [r["out"] for r in res.results]
    out = np.zeros((4, 8192, D), np.float32)
    for i, c in enumerate(cores):
        b, half = c // 2, c % 2
        out[b, half * 4096:(half + 1) * 4096] = res.results[i]["out"]
    return out
```

```python
import contextlib
import numpy as np
import concourse.bass as bass
import concourse.mybir as mybir
from concourse.bass_utils import run_bass_kernel_spmd

F32 = mybir.dt.float32
BF16 = mybir.dt.bfloat16
ALU = mybir.AluOpType
AF = mybir.ActivationFunctionType
AX = mybir.AxisListType

D = 1024
NCH = 8
DFF = 4096
NBLK = 34
TILE_BLOCKS = [4] * 8 + [2]
EPS = 1e-6
NEG = -30000.0
N_META = 16
ROT = 16
THETA = 500000.0
NSLOT = 3
DMA_SLOTS = 20

PE, ACT, DVE, POOL, SP = "pe", "act", "dve", "pool", "sp"
LIMIT = [10 ** 9]


class Op:
    __slots__ = ("eng", "fn", "reads", "writes", "dma", "deps_eng", "deps_dma", "inc", "ticket",
                 "slot", "value", "ndma")

    def __init__(self, eng, fn, reads, writes, dma=False, ndma=1):
        self.eng = eng
        self.fn = fn
        self.reads = reads
        self.writes = writes
        self.dma = dma
        self.ndma = ndma
        self.deps_eng = {}
        self.deps_dma = set()
        self.inc = False
        self.ticket = 0
        self.slot = -1
        self.value = 0


class Prog:
    def __init__(self):
        self.ops = []
        self.mode = "plan"
        self.pos = 0
        self.n = 0

    def add(self, eng, fn, reads=(), writes=(), dma=False, ndma=1):
        self.n += 1
        if self.n > LIMIT[0]:
            return
        if self.mode == "plan":
            writes = list(writes) + [k for k in reads if k.startswith("ps")]
            self.ops.append(Op(eng, None, list(reads), list(writes), dma, ndma))
            import sys as _s
            self.ops[-1].fn = _s._getframe(1).f_lineno
        else:
            op = self.ops[self.pos]
            assert op.eng == eng and op.dma == dma and op.ndma == ndma, (self.pos, op.eng, eng)
            self.emit_one(self.pos, op, fn)
            self.pos += 1

    def schedule(self):
        ops = self.ops
        last_w = {}
        readers = {}
        dma_count = 0
        dma_hist = []
        for i, op in enumerate(ops):
            deps = set()
            for k in op.reads:
                w = last_w.get(k)
                if w is not None:
                    deps.add((w, "raw"))
            for k in op.writes:
                w = last_w.get(k)
                if w is not None:
                    deps.add((w, "waw"))
                for r in readers.get(k, ()):
                    deps.add((r, "war"))
            if op.dma:
                if dma_count >= DMA_SLOTS:
                    deps.add((dma_hist[dma_count - DMA_SLOTS], "raw"))
                dma_hist.append(i)
                dma_count += 1
            for (j, kind) in deps:
                if j == i:
                    continue
                p = ops[j]
                if p.dma:
                    op.deps_dma.add(j)
                    continue
                if p.eng == op.eng and not op.dma:
                    if op.eng == PE:
                        continue
                    if kind != "raw":
                        continue
                cur = op.deps_eng.get(p.eng, -1)
                if j > cur:
                    op.deps_eng[p.eng] = j
            for k in op.reads:
                readers.setdefault(k, []).append(i)
            for k in op.writes:
                last_w[k] = i
                readers[k] = []
        for op in ops:
            for e, j in op.deps_eng.items():
                ops[j].inc = True
        cnt = {}
        for op in ops:
            if op.dma:
                continue
            if op.inc:
                cnt[op.eng] = cnt.get(op.eng, 0) + 1
            op.ticket = cnt.get(op.eng, 0)
        slot_val = [0] * DMA_SLOTS
        k = 0
        for op in ops:
            if op.dma:
                s = k % DMA_SLOTS
                slot_val[s] += 16 * op.ndma
                op.slot = s
                op.value = slot_val[s]
                k += 1

    def start_emit(self, nc, engs, sems, dsems):
        self.mode = "emit"
        self.pos = 0
        self.n = 0
        self.engs = engs
        self.sems = sems
        self.dsems = dsems
        self.waited = {e: {} for e in engs}
        self.waited_dma = {e: {} for e in engs}
        self.dmas = []

    def emit_one(self, i, op, fn):
        ops = self.ops
        e = self.engs[op.eng]
        waited = self.waited[op.eng]
        waited_dma = self.waited_dma[op.eng]
        for pe_, j in op.deps_eng.items():
            t = ops[j].ticket
            if waited.get(pe_, 0) < t:
                e.wait_ge(self.sems[pe_], t)
                waited[pe_] = t
        for j in sorted(op.deps_dma):
            p = ops[j]
            if waited_dma.get(p.slot, 0) < p.value:
                e.wait_ge(self.dsems[p.slot], p.value)
                waited_dma[p.slot] = p.value
        if op.dma:
            insts = fn(e)
            if not isinstance(insts, (list, tuple)):
                insts = [insts]
            assert len(insts) == op.ndma
            for ins in insts:
                ins.then_inc(self.dsems[op.slot], 16)
            self.dmas.append(i)
        else:
            ins = fn(e)
            if op.inc:
                ins.then_inc(self.sems[op.eng], 1)

    def finish_emit(self):
        assert self.pos == len(self.ops), (self.pos, len(self.ops))
        for i in self.dmas:
            op = self.ops[i]
            wd = self.waited_dma[op.eng]
            if wd.get(op.slot, 0) < op.value:
                self.engs[op.eng].wait_ge(self.dsems[op.slot], op.value)
                wd[op.slot] = op.value


def _w_in_perm():
    s_q, s_k, s_v = 0, 512, 640
    s_b, s_c, s_h = 768, 1280, 1792
    perm = []
    for j in range(4):
        perm += list(range(s_q + j * 64, s_q + j * 64 + 64))
        perm += list(range(s_q + (4 + j) * 64, s_q + (4 + j) * 64 + 64))
    perm += list(range(s_k, s_k + 128))
    perm += list(range(s_v, s_v + 128))
    for i in range(4):
        perm += list(range(s_c + i * 128, s_c + (i + 1) * 128))
        perm += list(range(s_h + i * 128, s_h + (i + 1) * 128))
        perm += list(range(s_b + i * 128, s_b + (i + 1) * 128))
    return np.array(perm, dtype=np.int64)


def _rope_tables(first_pos):
    n = NBLK * 128
    pos = (first_pos + np.arange(n)).astype(np.float32)
    inv_freq = np.power(np.float32(THETA), -np.arange(0, ROT, 2, dtype=np.float32) / np.float32(ROT)).astype(np.float32)
    ang = (pos[:, None] * inv_freq[None, :]).astype(np.float32)
    cos = np.cos(ang).astype(np.float32).T
    sin = np.sin(ang).astype(np.float32).T
    C = np.ones((128, n), np.float32)
    S = np.zeros((128, n), np.float32)
    for a in range(2):
        b0 = 64 * a
        C[b0:b0 + 8] = cos
        C[b0 + 8:b0 + 16] = cos
        S[b0:b0 + 8] = -sin
        S[b0 + 8:b0 + 16] = sin
    return C, S


def _perm_matrix():
    P = np.zeros((128, 128), np.float32)
    for a in range(2):
        b0 = 64 * a
        for m in range(8):
            P[b0 + m + 8, b0 + m] = -1.0
            P[b0 + m, b0 + m + 8] = -1.0
    return P


def _masks(is_first_half):
    q = np.arange(128)[:, None]
    k = np.arange(128)[None, :]
    band_prev = (k > q)
    causal = (k <= q)
    allm = np.zeros((128, 128), bool)
    kvalid = (k >= 112) & np.ones((128, 1), bool)
    m = np.zeros((4, 128, 256), bool)
    if is_first_half:
        m[0, :, :128] = allm
        m[0, :, 128:] = allm
        m[1, :, :128] = allm
        m[1, :, 128:] = causal & kvalid
        m[2, :, :128] = band_prev & kvalid
        m[2, :, 128:] = causal
    else:
        m[0, :, :128] = allm
        m[0, :, 128:] = causal
        m[1, :, :128] = band_prev
        m[1, :, 128:] = causal
        m[2, :, :128] = band_prev
        m[2, :, 128:] = causal
    m[3, :, :128] = band_prev
    m[3, :, 128:] = causal
    return np.where(m, 0.0, NEG).astype(np.float32)


G_PRE, G_POST, G_MPRE, G_MPOST, G_CONV, G_CW = 0, 8, 16, 24, 32, 36
NG_L = 48
NG = 2 * NG_L


def build(tile_blocks=None, n_out_blocks=None):
    tile_blocks = TILE_BLOCKS if tile_blocks is None else tile_blocks
    nblk = sum(tile_blocks)
    n_out = nblk - 2
    nc = bass.Bass("TRN2", target_bir_lowering=False)

    def dram_in(name, shape, dt=F32):
        return nc.dram_tensor(name, list(shape), dt, kind="ExternalInput").ap()

    xin = dram_in("xin", [NBLK * 128, D])
    w_in = dram_in("w_in", [2, D, 2304])
    w_out = dram_in("w_out", [2, D, D])
    w_up = dram_in("w_up", [2, D, DFF])
    w_down = dram_in("w_down", [2, DFF, D])
    gtab_d = dram_in("gtab", [128, NG])
    gattn_d = dram_in("gattn", [2, 128, 512])
    sink_d = dram_in("sinkb", [128, 16])
    ctab_d = dram_in("ctab", [128, NBLK * 128])
    stab_d = dram_in("stab", [128, NBLK * 128])
    mask_d = dram_in("masks", [128, 4 * 256])
    cmat_d = dram_in("cmat", [128, 3 * 128])
    out_d = nc.dram_tensor("out", [max(n_out, 1) * 128, D], F32, kind="ExternalOutput").ap()
    scr = nc.dram_tensor("wscr", [2 * 12, 128, 8192], BF16, kind="Internal").ap()

    es = contextlib.ExitStack()
    with es:
        def sb(name, shape, dt):
            return es.enter_context(nc.sbuf_tensor(name, list(shape), dt))

        def ps(name, shape, dt):
            return es.enter_context(nc.psum_tensor(name, list(shape), dt))

        ring = [sb(f"ring{i}", [128, 8192], BF16) for i in range(NSLOT)]
        hT = sb("hT", [128, NCH, 512], F32)
        aT = sb("aT", [128, NCH, 512], BF16)
        sq = sb("sq", [128, NCH, 512], BF16)
        z = sb("z", [128, NCH, 512], F32)
        U = sb("U", [128, 16384], BF16)
        hid = U[:, :].rearrange("p (f t) -> p f t", f=32)
        qpre = U[:, 0:2560].rearrange("p (j t) -> p j t", j=5)
        qT = U[:, 2560:4608].rearrange("p (j t) -> p j t", j=4)
        yT = U[:, 4608:8704].rearrange("p (c t) -> p c t", c=8)
        yconv = sb("yconv", [128, 4, 512], F32)
        rt1 = [sb(f"rt1_{i}", [128, 512], F32) for i in range(2)]
        rt2 = [sb(f"rt2_{i}", [128, 512], F32) for i in range(2)]
        ctmp = rt1
        btmp = rt2
        lnv = sb("lnv", [128, 512], F32)
        rstd = sb("rstd", [128, 512], F32)
        kA = [sb(f"kA{l}", [128, 640], BF16) for l in range(2)]
        kB = [sb(f"kB{l}", [128, 640], BF16) for l in range(2)]
        Vt = [sb(f"Vt{l}", [128, 5, 128], BF16) for l in range(2)]
        Ut = [sb(f"Ut{l}", [128, 4, 520], BF16) for l in range(2)]
        Pm = [sb(f"Pm{i}", [128, 4, 256], BF16) for i in range(2)]
        PTs = [sb(f"PTs{i}", [128, 8, 128], BF16) for i in range(2)]
        yats = [sb(f"yat{i}", [128, 512], F32) for i in range(2)]
        yans = [sb(f"yan{i}", [128, 512], BF16) for i in range(2)]
        r2b = sb("r2b", [128, 512], F32)
        st = [sb(f"stt{i}", [128, 32], F32) for i in range(4)]
        st2 = sb("st2", [128, 8], F32)
        rtk = sb("rtk", [128, 8], F32)
        xs = [sb(f"xs{i}", [128, D], F32) for i in range(2)]
        ctab = sb("ctab_s", [128, 512], F32)
        stab = sb("stab_s", [128, 512], F32)
        gtab = sb("gtab_s", [128, NG], F32)
        gattn = sb("gattn_s", [128, 2, 512], F32)
        sinkb = sb("sinkb_s", [128, 16], F32)
        negsink = sb("negsink", [128, 16], F32)
        epsb = sb("epsb", [128, 1], F32)
        maskb = sb("maskb", [128, 4, 256], BF16)
        cmat = sb("cmat_s", [128, 3, 128], BF16)
        identf = sb("identf", [128, 128], F32)
        diag = sb("diag", [128, 24, 128], BF16)
        ident = cmat[:, 0, :]
        perm = cmat[:, 1, :]
        ones = cmat[:, 2, :]

        psG = [ps(f"psG{i}", [128, 512], F32) for i in range(2)]
        psS = [ps(f"psS{i}", [128, 1024], F32) for i in range(2)]
        psPT = ps("psPT", [128, 1024], BF16)
        psO = ps("psO", [128, 512], F32)

        sem_names = [PE, ACT, DVE, POOL]
        sems = {e: es.enter_context(nc.semaphore(f"s_{e}")) for e in sem_names}
        dsems = [es.enter_context(nc.semaphore(f"d{i}")) for i in range(DMA_SLOTS)]
        engs = {PE: nc.tensor, ACT: nc.scalar, DVE: nc.vector, POOL: nc.gpsimd, SP: nc.sync}

        def program(P):
            gctr = [0]

            G2 = [(psG[0], "psG0"), (psG[1], "psG1")]
            G6 = G2 + [(psS[0][:, 0:512], "psS0a"), (psS[0][:, 512:1024], "psS0b"),
                       (psS[1][:, 0:512], "psS1a"), (psS[1][:, 512:1024], "psS1b")]
            gpool = [G6]

            def psg():
                pool = gpool[0]
                i = gctr[0] % len(pool)
                gctr[0] += 1
                return pool[i]

            sqj = sq[:, 7, :]
            P.add(SP, lambda e: e.dma_start(out=gtab[:], in_=gtab_d[:, :]), writes=["gtab"], dma=True)
            P.add(SP, lambda e: e.dma_start(out=gattn[:], in_=gattn_d.rearrange("l p f -> p l f")), writes=["gattn"], dma=True)
            P.add(SP, lambda e: e.dma_start(out=sinkb[:], in_=sink_d[:, :]), writes=["sinkb"], dma=True)
            P.add(SP, lambda e: e.dma_start(out=identf[:], in_=cmat_d[:, 0:128]), writes=["identf"], dma=True)
            P.add(POOL, lambda e: e.dma_start(out=cmat[:], in_=cmat_d.rearrange("p (a n) -> p a n", a=3)), writes=["cmat"], dma=True)
            P.add(POOL, lambda e: e.dma_start(out=maskb[:], in_=mask_d.rearrange("p (a n) -> p a n", a=4)), writes=["maskb"], dma=True)
            P.add(POOL, lambda e: e.memset(epsb[:], EPS), writes=["epsb"])
            P.add(POOL, lambda e: e.tensor_scalar(out=negsink[:], in0=sinkb[:], scalar1=-1.0, scalar2=None, op0=ALU.mult),
                  reads=["sinkb"], writes=["negsink"])
            for l in range(2):
                P.add(POOL, lambda e, l=l: e.memset(kA[l][:], 0.0), writes=[f"kA{l}"])
                P.add(POOL, lambda e, l=l: e.memset(kB[l][:], 0.0), writes=[f"kB{l}"])
                P.add(POOL, lambda e, l=l: e.memset(Vt[l][:], 0.0), writes=[f"Vt{l}"])
                P.add(POOL, lambda e, l=l: e.memset(Ut[l][:], 0.0), writes=[f"Ut{l}"])
                for i in range(4):
                    for k in range(3):
                        col = l * NG_L + G_CW + k * 4 + i
                        P.add(POOL, lambda e, l=l, i=i, k=k, col=col: e.tensor_scalar(
                            out=diag[:, l * 12 + i * 3 + k, :], in0=identf[:], scalar1=gtab[:, col:col + 1],
                            scalar2=None, op0=ALU.mult),
                            reads=["identf", "gtab"], writes=[f"diag{l}"])

            ring_ctr = [0]
            PIECES = ["in0", "in1", "in2", "out", "up0", "up1", "up2", "up3", "dn0", "dn1", "dn2", "dn3"]

            def piece_src(l, name):
                if name.startswith("in"):
                    j = int(name[2:])
                    src = w_in[l, :, j * 768:(j + 1) * 768].rearrange("(kc p) n -> p kc n", p=128)
                    return src, (8, 768)
                if name == "out":
                    return w_out[l].rearrange("(kc p) n -> p kc n", p=128), (8, 1024)
                if name.startswith("up"):
                    j = int(name[2:])
                    return w_up[l, :, j * 1024:(j + 1) * 1024].rearrange("(kc p) n -> p kc n", p=128), (8, 1024)
                j = int(name[2:])
                return w_down[l, :, j * 256:(j + 1) * 256].rearrange("(kc p) n -> p kc n", p=128), (32, 256)

            SEQ = [(ti_, l_, nm_) for ti_ in range(len(tile_blocks)) for l_ in range(2) for nm_ in PIECES]
            issued = [0]
            LA = 2

            def emit_load(n):
                ti, l, name = SEQ[n]
                s = n % NSLOT
                src, (a, n_) = piece_src(l, name)
                view = ring[s][:, 0:a * n_].rearrange("p (a n) -> p a n", a=a)
                pi = l * 12 + PIECES.index(name)
                key = f"ring{s}"
                skey = f"scr{pi}"
                if ti == 0:
                    q = a // 4

                    def f(e):
                        return [e.dma_start(out=view[:, i * q:(i + 1) * q, :], in_=src[:, i * q:(i + 1) * q, :])
                                for i in range(4)]
                    P.add(POOL, f, writes=[key], dma=True, ndma=4)
                    P.add(SP, lambda e: e.dma_start(out=scr[pi, :, 0:a * n_], in_=ring[s][:, 0:a * n_]),
                          reads=[key], writes=[skey], dma=True)
                else:
                    def f(e):
                        h = (a * n_) // 2
                        return [e.dma_start(out=ring[s][:, i * h:(i + 1) * h], in_=scr[pi, :, i * h:(i + 1) * h])
                                for i in range(2)]
                    P.add(SP, f, reads=[skey], writes=[key], dma=True, ndma=2)

            def load_piece(ti, l, name):
                n = ring_ctr[0]
                ring_ctr[0] += 1
                assert SEQ[n] == (ti, l, name), (SEQ[n], ti, l, name)
                while issued[0] < min(n + 1 + LA, len(SEQ)):
                    emit_load(issued[0])
                    issued[0] += 1
                s = n % NSLOT
                _, (a, n_) = piece_src(l, name)
                view = ring[s][:, 0:a * n_].rearrange("p (a n) -> p a n", a=a)
                return view, f"ring{s}"

            def rms_stats(T, nchunks, inv_n, sqkeys):
                pt, pk = psg()
                for c in range(nchunks):
                    P.add(PE, lambda e, c=c, pt=pt: e.matmul(pt[:, 0:T], lhsT=ones, rhs=sq[:, c, 0:T],
                                                             start=(c == 0), stop=(c == nchunks - 1)),
                          reads=["cmat", sqkeys[c]], writes=[pk])
                P.add(ACT, lambda e, pt=pt: e.activation(out=lnv[:, 0:T], in_=pt[:, 0:T], func=AF.Ln,
                                                         bias=epsb[:, 0:1], scale=inv_n),
                      reads=[pk, "epsb"], writes=["lnv"])
                P.add(ACT, lambda e: e.activation(out=rstd[:, 0:T], in_=lnv[:, 0:T], func=AF.Exp, scale=-0.5),
                      reads=["lnv"], writes=["rstd"])

            def pre_norm(T, l, gcol):
                for c in range(NCH):
                    if c % 2 == 0:
                        P.add(ACT, lambda e, c=c: e.activation(out=sq[:, c, 0:T], in_=hT[:, c, 0:T], func=AF.Square),
                              reads=[f"h{c}"], writes=[f"sq{c}"])
                    else:
                        P.add(DVE, lambda e, c=c: e.tensor_tensor(out=sq[:, c, 0:T], in0=hT[:, c, 0:T], in1=hT[:, c, 0:T],
                                                                   op=ALU.mult),
                              reads=[f"h{c}"], writes=[f"sq{c}"])
                rms_stats(T, NCH, 1.0 / D, [f"sq{c}" for c in range(NCH)])
                for c in range(NCH):
                    col = l * NG_L + gcol + c
                    eng = DVE
                    P.add(eng, lambda e, c=c, col=col: e.scalar_tensor_tensor(
                        out=aT[:, c, 0:T], in0=hT[:, c, 0:T], scalar=gtab[:, col:col + 1], in1=rstd[:, 0:T],
                        op0=ALU.mult, op1=ALU.mult),
                        reads=[f"h{c}", "gtab", "rstd"], writes=[f"a{c}"])

            def stat_mm(T, c):
                P.add(PE, lambda e: e.matmul(psO[:, 0:T], lhsT=ones, rhs=sq[:, c, 0:T], start=(c == 0), stop=(c == NCH - 1)),
                      reads=["cmat", f"sq{c}"], writes=["psO"])

            def pre_norm_deferred(T, l, gcol):
                for c in range(NCH):
                    col = l * NG_L + gcol + c
                    P.add(ACT, lambda e, c=c, col=col: e.activation(out=aT[:, c, 0:T], in_=hT[:, c, 0:T], func=AF.Copy,
                                                                    scale=gtab[:, col:col + 1]),
                          reads=[f"h{c}", "gtab"], writes=[f"a{c}"])
                for c in range(NCH):
                    P.add(ACT, lambda e, c=c: e.activation(out=sq[:, c, 0:T], in_=hT[:, c, 0:T], func=AF.Square),
                          reads=[f"h{c}"], writes=[f"sq{c}"])
                pt, pk = psg()
                for c in range(NCH):
                    P.add(PE, lambda e, c=c, pt=pt: e.matmul(pt[:, 0:T], lhsT=ones, rhs=sq[:, c, 0:T],
                                                             start=(c == 0), stop=(c == NCH - 1)),
                          reads=["cmat", f"sq{c}"], writes=[pk])
                P.add(ACT, lambda e, pt=pt: e.activation(out=r2b[:, 0:T], in_=pt[:, 0:T], func=AF.Ln, bias=epsb[:, 0:1], scale=1.0 / D),
                      reads=[pk, "epsb"], writes=["r2b"])
                P.add(ACT, lambda e: e.activation(out=r2b[:, 0:T], in_=r2b[:, 0:T], func=AF.Exp, scale=-1.0),
                      reads=["r2b"], writes=["r2b"])

            def post_norm_update(T, l, deferred=False):
                if not deferred:
                    P.add(ACT, lambda e: e.activation(out=lnv[:, 0:T], in_=psO[:, 0:T], func=AF.Ln, bias=epsb[:, 0:1], scale=1.0 / D),
                          reads=["psO", "epsb"], writes=["lnv"])
                    P.add(ACT, lambda e: e.activation(out=rstd[:, 0:T], in_=lnv[:, 0:T], func=AF.Exp, scale=-0.5),
                          reads=["lnv"], writes=["rstd"])
                else:
                    P.add(DVE, lambda e: e.tensor_tensor(out=lnv[:, 0:T], in0=psO[:, 0:T], in1=r2b[:, 0:T], op=ALU.mult),
                          reads=["psO", "r2b"], writes=["lnv"])
                    P.add(DVE, lambda e: e.tensor_tensor(out=lnv[:, 0:T], in0=lnv[:, 0:T], in1=r2b[:, 0:T], op=ALU.mult),
                          reads=["lnv", "r2b"], writes=["lnv"])
                    P.add(ACT, lambda e: e.activation(out=lnv[:, 0:T], in_=lnv[:, 0:T], func=AF.Ln, bias=epsb[:, 0:1], scale=1.0 / D),
                          reads=["lnv", "epsb"], writes=["lnv"])
                    P.add(ACT, lambda e: e.activation(out=lnv[:, 0:T], in_=lnv[:, 0:T], func=AF.Exp, scale=-0.5),
                          reads=["lnv"], writes=["lnv"])
                    P.add(DVE, lambda e: e.tensor_tensor(out=rstd[:, 0:T], in0=lnv[:, 0:T], in1=r2b[:, 0:T], op=ALU.mult),
                          reads=["lnv", "r2b"], writes=["rstd"])
                for c in range(NCH):
                    P.add(DVE, lambda e, c=c: e.tensor_tensor(out=z[:, c, 0:T], in0=z[:, c, 0:T], in1=rstd[:, 0:T], op=ALU.mult),
                          reads=[f"z{c}", "rstd"], writes=[f"z{c}"])
                    P.add(DVE, lambda e, c=c: e.tensor_tensor(out=hT[:, c, 0:T], in0=hT[:, c, 0:T], in1=z[:, c, 0:T], op=ALU.add),
                          reads=[f"z{c}", f"h{c}"], writes=[f"h{c}"])

            def branch_evac(T, l, pt, pk, d, gcol):
                col = l * NG_L + gcol + d
                P.add(ACT, lambda e, d=d, pt=pt: e.activation(out=sq[:, d, 0:T], in_=pt[:, 0:T], func=AF.Square),
                      reads=[pk], writes=[f"sq{d}"])
                P.add(DVE, lambda e, d=d, pt=pt, col=col: e.tensor_scalar(out=z[:, d, 0:T], in0=pt[:, 0:T],
                                                                          scalar1=gtab[:, col:col + 1], scalar2=None, op0=ALU.mult),
                      reads=[pk, "gtab"], writes=[f"z{d}"])

            blk0 = 0
            for ti, nb in enumerate(tile_blocks):
                T = nb * 128
                tok0 = blk0 * 128
                P.add(SP, lambda e, tok0=tok0, T=T: e.dma_start(out=ctab[:, 0:T], in_=ctab_d[:, tok0:tok0 + T]),
                      writes=["ctab"], dma=True)
                P.add(SP, lambda e, tok0=tok0, T=T: e.dma_start(out=stab[:, 0:T], in_=stab_d[:, tok0:tok0 + T]),
                      writes=["stab"], dma=True)
                def x_load(tok_base, b):
                    i = b % 2
                    r0 = tok_base + b * 128

                    def f(e):
                        return [e.dma_start(out=rt1[i][:], in_=xin[r0:r0 + 128, 0:512]),
                                e.dma_start(out=rt2[i][:], in_=xin[r0:r0 + 128, 512:1024])]
                    P.add(SP, f, writes=[f"rt1_{i}", f"rt2_{i}"], dma=True, ndma=2)

                def x_transpose(b):
                    i = b % 2
                    pS = psS[b % 2]
                    pSk = f"psS{b % 2}"
                    for c in range(NCH):
                        src = rt1[i] if c < 4 else rt2[i]
                        sk = f"rt1_{i}" if c < 4 else f"rt2_{i}"
                        cc = c % 4
                        P.add(PE, lambda e, c=c, cc=cc, src=src: e.transpose(out=pS[:, c * 128:(c + 1) * 128],
                                                                             in_=src[:, cc * 128:(cc + 1) * 128], identity=identf[:]),
                              reads=[sk, "identf"], writes=[pSk + ("a" if c < 4 else "b")])
                    for hlf in range(2):
                        eng = ACT if hlf == 0 else DVE
                        if eng == ACT:
                            fn = lambda e, hlf=hlf: e.activation(
                                out=hT[:, hlf * 4:(hlf + 1) * 4, b * 128:(b + 1) * 128],
                                in_=pS[:, hlf * 512:(hlf + 1) * 512].rearrange("p (c t) -> p c t", c=4), func=AF.Copy)
                        else:
                            fn = lambda e, hlf=hlf: e.tensor_copy(
                                out=hT[:, hlf * 4:(hlf + 1) * 4, b * 128:(b + 1) * 128],
                                in_=pS[:, hlf * 512:(hlf + 1) * 512].rearrange("p (c t) -> p c t", c=4))
                        P.add(eng, fn, reads=[pSk + "ab"[hlf]], writes=[f"h{c}" for c in range(hlf * 4, hlf * 4 + 4)])

                if ti == 0:
                    for b in range(min(2, nb)):
                        x_load(tok0, b)
                for b in range(nb):
                    x_transpose(b)
                    if b + 2 < nb:
                        x_load(tok0, b + 2)

                for l in range(2):
                    for c in range(NCH):
                        col = l * NG_L + G_PRE + c
                        P.add(ACT, lambda e, c=c, col=col: e.activation(out=aT[:, c, 0:T], in_=hT[:, c, 0:T], func=AF.Copy,
                                                                        scale=gtab[:, col:col + 1]),
                              reads=[f"h{c}", "gtab"], writes=[f"a{c}"])
                    for c in range(NCH):
                        P.add(ACT, lambda e, c=c: e.activation(out=sq[:, c, 0:T], in_=hT[:, c, 0:T], func=AF.Square),
                              reads=[f"h{c}"], writes=[f"sq{c}"])
                    akeys = [f"a{c}" for c in range(NCH)]
                    wv0, wk0 = load_piece(ti, l, "in0")
                    for j in range(5):
                        pt, pk = psg()
                        for kc in range(NCH):
                            P.add(PE, lambda e, j=j, kc=kc, pt=pt: e.matmul(
                                pt[:, 0:T], lhsT=wv0[:, kc, j * 128:(j + 1) * 128], rhs=aT[:, kc, 0:T],
                                start=(kc == 0), stop=(kc == NCH - 1)),
                                reads=[wk0, f"a{kc}"], writes=[pk])
                        P.add(ACT, lambda e, j=j, pt=pt: e.activation(out=qpre[:, j, 0:T], in_=pt[:, 0:T], func=AF.Copy),
                              reads=[pk], writes=[f"qpre{j}", "Umix"])
                    pt, pk = psg()
                    for b in range(nb):
                        for kc in range(NCH):
                            P.add(PE, lambda e, b=b, kc=kc, pt=pt: e.matmul(
                                pt[:, b * 128:(b + 1) * 128], lhsT=aT[:, kc, b * 128:(b + 1) * 128], rhs=wv0[:, kc, 640:768],
                                start=(kc == 0), stop=(kc == NCH - 1)),
                                reads=[wk0, f"a{kc}"], writes=[pk])
                    ptv, pkv = pt, pk
                    pt, pk = psg()
                    for c in range(NCH):
                        P.add(PE, lambda e, c=c, pt=pt: e.matmul(pt[:, 0:T], lhsT=ones, rhs=sq[:, c, 0:T],
                                                                 start=(c == 0), stop=(c == NCH - 1)),
                              reads=["cmat", f"sq{c}"], writes=[pk])
                    P.add(ACT, lambda e, pt=pt: e.activation(out=r2b[:, 0:T], in_=pt[:, 0:T], func=AF.Ln, bias=epsb[:, 0:1], scale=1.0 / D),
                          reads=[pk, "epsb"], writes=["r2b"])
                    P.add(ACT, lambda e: e.activation(out=rstd[:, 0:T], in_=r2b[:, 0:T], func=AF.Exp, scale=-0.5),
                          reads=["r2b"], writes=["rstd"])
                    P.add(ACT, lambda e: e.activation(out=r2b[:, 0:T], in_=r2b[:, 0:T], func=AF.Exp, scale=-1.0),
                          reads=["r2b"], writes=["r2b"])
                    pt2, pk2 = psg()
                    for b in range(nb):
                        for c in range(NCH):
                            P.add(PE, lambda e, b=b, c=c, pt2=pt2: e.matmul(pt2[:, b:b + 1], lhsT=sq[:, c, b * 128:(b + 1) * 128], rhs=ones[:, 0:1],
                                                                         start=(c == 0), stop=(c == NCH - 1)),
                                  reads=["cmat", f"sq{c}"], writes=[pk2])
                    P.add(ACT, lambda e, pt2=pt2: e.activation(out=rtk[:, 0:nb], in_=pt2[:, 0:nb], func=AF.Ln, bias=epsb[:, 0:1], scale=1.0 / D),
                          reads=[pk2, "epsb"], writes=["rtok"])
                    P.add(ACT, lambda e: e.activation(out=rtk[:, 0:nb], in_=rtk[:, 0:nb], func=AF.Exp, scale=-0.5),
                          reads=["rtok"], writes=["rtok"])
                    for b in range(nb):
                        P.add(ACT, lambda e, b=b: e.activation(out=Vt[l][:, 1 + b, :], in_=ptv[:, b * 128:(b + 1) * 128], func=AF.Copy,
                                                               scale=rtk[:, b:b + 1]),
                              reads=[pkv, "rtok"], writes=[f"Vt{l}"])
                    Cs = yconv[:, 0, :]
                    Ss = yconv[:, 1, :]
                    P.add(DVE, lambda e: e.tensor_tensor(out=Cs[:, 0:T], in0=ctab[:, 0:T], in1=rstd[:, 0:T], op=ALU.mult),
                          reads=["ctab", "rstd"], writes=["yconv0"])
                    P.add(DVE, lambda e: e.tensor_tensor(out=Ss[:, 0:T], in0=stab[:, 0:T], in1=rstd[:, 0:T], op=ALU.mult),
                          reads=["stab", "rstd"], writes=["yconv1"])
                    for j in (4, 0, 1, 2, 3):
                        qc = rt1[j % 2][:, 0:256].bitcast(BF16)
                        qs = rt2[j % 2][:, 0:256].bitcast(BF16)
                        P.add(DVE, lambda e, j=j, qc=qc: e.tensor_tensor(out=qc[:, 0:T], in0=qpre[:, j, 0:T], in1=Cs[:, 0:T], op=ALU.mult),
                              reads=[f"qpre{j}", "yconv0"], writes=[f"rt1_{j % 2}"])
                        P.add(DVE, lambda e, j=j, qs=qs: e.tensor_tensor(out=qs[:, 0:T], in0=qpre[:, j, 0:T], in1=Ss[:, 0:T], op=ALU.mult),
                              reads=[f"qpre{j}", "yconv1"], writes=[f"rt2_{j % 2}"])
                        pr, prk = psg()
                        P.add(PE, lambda e, pr=pr, qc=qc: e.matmul(pr[:, 0:T], lhsT=ident, rhs=qc[:, 0:T], start=True, stop=False),
                              reads=["cmat", f"rt1_{j % 2}"], writes=[prk])
                        P.add(PE, lambda e, pr=pr, qs=qs: e.matmul(pr[:, 0:T], lhsT=perm, rhs=qs[:, 0:T], start=False, stop=True),
                              reads=["cmat", f"rt2_{j % 2}"], writes=[prk])
                        if j < 4:
                            P.add(ACT, lambda e, j=j, pr=pr: e.activation(out=qT[:, j, 0:T], in_=pr[:, 0:T], func=AF.Copy),
                                  reads=[prk], writes=[f"qT{j}", "Umix"])
                        else:
                            P.add(ACT, lambda e, pr=pr: e.activation(out=kA[l][0:64, 128:128 + T], in_=pr[0:64, 0:T], func=AF.Copy),
                                  reads=[prk], writes=[f"kA{l}"])
                            P.add(ACT, lambda e, pr=pr: e.activation(out=kB[l][64:128, 128:128 + T], in_=pr[64:128, 0:T], func=AF.Copy),
                                  reads=[prk], writes=[f"kB{l}"])

                    def conv_gen():
                        wv = wk = None
                        for i in range(4):
                            if i % 2 == 0:
                                wv, wk = load_piece(ti, l, f"in{1 + i // 2}")
                            base = (i % 2) * 384
                            ct = ctmp[i % 2]
                            bt = btmp[i % 2]
                            pt, pk = psg()
                            for kc in range(NCH):
                                P.add(PE, lambda e, kc=kc, pt=pt, wv=wv, base=base: e.matmul(
                                    pt[:, 0:T], lhsT=wv[:, kc, base:base + 128], rhs=aT[:, kc, 0:T], start=(kc == 0), stop=(kc == NCH - 1)),
                                    reads=[wk, f"a{kc}"], writes=[pk])
                            P.add(DVE, lambda e, pt=pt, ct=ct: e.tensor_tensor(out=ct[:, 0:T], in0=pt[:, 0:T], in1=r2b[:, 0:T], op=ALU.mult),
                                  reads=[pk, "r2b"], writes=[f"rt1_{i % 2}"])
                            yield
                            pt, pk = psg()
                            for kc in range(NCH):
                                P.add(PE, lambda e, kc=kc, pt=pt, wv=wv, base=base: e.matmul(
                                    pt[:, 0:T], lhsT=wv[:, kc, base + 128:base + 256], rhs=aT[:, kc, 0:T], start=(kc == 0), stop=(kc == NCH - 1)),
                                    reads=[wk, f"a{kc}"], writes=[pk])
                            P.add(DVE, lambda e, pt=pt, ct=ct, i=i: e.tensor_tensor(out=Ut[l][:, i, 2:2 + T], in0=pt[:, 0:T], in1=ct[:, 0:T], op=ALU.mult),
                                  reads=[pk, f"rt1_{i % 2}"], writes=[f"Ut{l}_{i}"])
                            yield
                            pt, pk = psg()
                            for kc in range(NCH):
                                P.add(PE, lambda e, kc=kc, pt=pt, wv=wv, base=base: e.matmul(
                                    pt[:, 0:T], lhsT=wv[:, kc, base + 256:base + 384], rhs=aT[:, kc, 0:T], start=(kc == 0), stop=(kc == NCH - 1)),
                                    reads=[wk, f"a{kc}"], writes=[pk])
                            P.add(DVE, lambda e, pt=pt, bt=bt: e.tensor_tensor(out=bt[:, 0:T], in0=pt[:, 0:T], in1=rstd[:, 0:T], op=ALU.mult),
                                  reads=[pk, "rstd"], writes=[f"rt2_{i % 2}"])
                            pt, pk = psg()
                            for k in range(3):
                                P.add(PE, lambda e, k=k, pt=pt, i=i: e.matmul(
                                    pt[:, 0:T], lhsT=diag[:, l * 12 + i * 3 + k, :], rhs=Ut[l][:, i, k:k + T], start=(k == 0), stop=(k == 2)),
                                    reads=[f"diag{l}", f"Ut{l}_{i}"], writes=[pk])
                            P.add(DVE, lambda e, pt=pt, bt=bt, i=i: e.tensor_tensor(out=yconv[:, i, 0:T], in0=pt[:, 0:T], in1=bt[:, 0:T], op=ALU.mult),
                                  reads=[pk, f"rt2_{i % 2}"], writes=[f"yconv{i}"])
                            P.add(ACT, lambda e, i=i: e.activation(out=sq[:, i, 0:T], in_=yconv[:, i, 0:T], func=AF.Square),
                                  reads=[f"yconv{i}"], writes=[f"sq{i}"])
                            P.add(POOL, lambda e, i=i: e.tensor_copy(out=Ut[l][:, i, 0:2], in_=Ut[l][:, i, T:T + 2]),
                                  reads=[f"Ut{l}_{i}"], writes=[f"Ut{l}_{i}"])
                            yield
                        rms_stats(T, 4, 1.0 / 512, [f"sq{i}" for i in range(4)])
                        for i in range(4):
                            col = l * NG_L + G_CONV + i
                            P.add(DVE, lambda e, i=i, col=col: e.scalar_tensor_tensor(
                                out=yT[:, 4 + i, 0:T], in0=yconv[:, i, 0:T], scalar=gtab[:, col:col + 1], in1=rstd[:, 0:T],
                                op0=ALU.mult, op1=ALU.mult),
                                reads=[f"yconv{i}", "gtab", "rstd"], writes=[f"yT{4 + i}", "Umix"])
                        yield

                    def unit(u):
                        b, g = u // 2, u % 2
                        return b, g, psS[u % 2], f"psS{u % 2}", st[u % 4], f"st{u % 4}", Pm[u % 2], f"Pm{u % 2}", PTs[u % 2], f"PTs{u % 2}"

                    def stage_A(u):
                        b, g, pS, pSk, sg, sgk, Pg, Pk, PTg, PTk = unit(u)
                        mv = min(blk0 + b, 3)
                        k0 = b * 128
                        kbuf = kA[l] if g == 0 else kB[l]
                        kkey = f"kA{l}" if g == 0 else f"kB{l}"
                        for j in range(4):
                            P.add(PE, lambda e, j=j: e.matmul(
                                pS[:, j * 256:(j + 1) * 256], lhsT=qT[:, j, b * 128:(b + 1) * 128], rhs=kbuf[:, k0:k0 + 256],
                                start=True, stop=False),
                                reads=[f"qT{j}", kkey], writes=[pSk + "ab"[j // 2]])
                            P.add(PE, lambda e, j=j: e.matmul(
                                pS[:, j * 256:(j + 1) * 256], lhsT=ident, rhs=maskb[:, mv, :], start=False, stop=True),
                                reads=["cmat", "maskb"], writes=[pSk + "ab"[j // 2]])
                        P.add(DVE, lambda e: e.reduce_max(out=sg[:, 0:4], in_=pS[:, :].rearrange("p (h k) -> p h k", h=4), axis=AX.X),
                              reads=[pSk + "a", pSk + "b"], writes=[sgk])
                        P.add(DVE, lambda e: e.scalar_tensor_tensor(
                            out=sg[:, 4:8], in0=sg[:, 0:4], scalar=-0.125, in1=negsink[:, l * 8 + g * 4:l * 8 + g * 4 + 4],
                            op0=ALU.mult, op1=ALU.min),
                            reads=[sgk, "negsink"], writes=[sgk])
                        P.add(POOL, lambda e: e.memset(sg[:, 8:12], 0.0), writes=[sgk + f"s{j}" for j in range(4)])
                        for j in range(4):
                            P.add(ACT, lambda e, j=j: e.activation(
                                out=Pg[:, j, :], in_=pS[:, j * 256:(j + 1) * 256], func=AF.Exp,
                                bias=sg[:, 4 + j:5 + j], scale=0.125, accum_out=sg[:, 8 + j:9 + j]),
                                reads=[pSk + "ab"[j // 2], sgk, sgk + f"s{j}"], writes=[Pk + f"_{j}", sgk + f"s{j}"])

                    def stage_B2(u):
                        b, g, pS, pSk, sg, sgk, Pg, Pk, PTg, PTk = unit(u)
                        P.add(DVE, lambda e: e.tensor_tensor(out=sg[:, 12:16], in0=sg[:, 4:8],
                                                              in1=sinkb[:, l * 8 + g * 4:l * 8 + g * 4 + 4], op=ALU.add),
                              reads=[sgk, "sinkb"], writes=[sgk + "t"])
                        P.add(ACT, lambda e: e.activation(out=sg[:, 16:20], in_=sg[:, 12:16], func=AF.Exp),
                              reads=[sgk + "t"], writes=[sgk + "e"])

                    def stage_B2b(u):
                        b, g, pS, pSk, sg, sgk, Pg, Pk, PTg, PTk = unit(u)
                        P.add(DVE, lambda e: e.tensor_tensor(out=sg[:, 20:24], in0=sg[:, 8:12], in1=sg[:, 16:20], op=ALU.add),
                              reads=[sgk + f"s{j}" for j in range(4)] + [sgk + "e"], writes=[sgk + "d"])
                        P.add(DVE, lambda e: e.reciprocal(out=sg[:, 24:28], in_=sg[:, 20:24]),
                              reads=[sgk + "d"], writes=[sgk + "r"])

                    def stage_C(u):
                        b, g, pS, pSk, sg, sgk, Pg, Pk, PTg, PTk = unit(u)
                        for j in range(4):
                            for kb in range(2):
                                P.add(PE, lambda e, j=j, kb=kb: e.transpose(
                                    out=psPT[:, (j * 2 + kb) * 128:(j * 2 + kb + 1) * 128], in_=Pg[:, j, kb * 128:(kb + 1) * 128], identity=ident),
                                    reads=[Pk + f"_{j}", "cmat"], writes=["psPT"])
                        if u % 2 == 0:
                            P.add(DVE, lambda e: e.tensor_copy(out=PTg[:, :, :], in_=psPT[:, :].rearrange("p (a q) -> p a q", a=8)),
                                  reads=["psPT"], writes=[PTk])
                        else:
                            P.add(ACT, lambda e: e.activation(out=PTg[:, :, :], in_=psPT[:, :].rearrange("p (a q) -> p a q", a=8), func=AF.Copy),
                                  reads=["psPT"], writes=[PTk])

                    def stage_D(u):
                        b, g, pS, pSk, sg, sgk, Pg, Pk, PTg, PTk = unit(u)
                        for j in range(4):
                            h = g * 4 + j
                            for kb in range(2):
                                P.add(PE, lambda e, j=j, kb=kb, h=h: e.matmul(
                                    psO[:, h * 64:(h + 1) * 64], lhsT=PTg[:, j * 2 + kb, :], rhs=Vt[l][:, b + kb, g * 64:(g + 1) * 64],
                                    start=(kb == 0), stop=(kb == 1)),
                                    reads=[PTk, f"Vt{l}"], writes=["psO"])
                        P.add(DVE, lambda e: e.tensor_tensor(
                            out=yats[b % 2][:, g * 256:(g + 1) * 256].rearrange("p (h d) -> p h d", h=4),
                            in0=psO[:, g * 256:(g + 1) * 256].rearrange("p (h d) -> p h d", h=4),
                            in1=sg[:, 24:28].unsqueeze(2).broadcast_to([128, 4, 64]), op=ALU.mult),
                            reads=["psO", sgk + "r"], writes=[f"yat{b % 2}"])
                        if g == 1:
                            stage_E1(b)

                    def stage_E1(b):
                        yat = yats[b % 2]
                        yk = f"yat{b % 2}"
                        yan = yans[b % 2]
                        ynk = f"yan{b % 2}"
                        s2 = st2[:, (b % 2) * 4:(b % 2) * 4 + 4]
                        s2k = f"st2_{b % 2}"
                        P.add(POOL, lambda e: e.memset(s2[:, 0:1], 0.0), writes=[s2k])
                        P.add(ACT, lambda e: e.activation(out=sqj[:], in_=yat[:], func=AF.Square, accum_out=s2[:, 0:1]),
                              reads=[yk, s2k], writes=["sq7", s2k])
                        P.add(ACT, lambda e: e.activation(out=s2[:, 1:2], in_=s2[:, 0:1], func=AF.Ln, bias=epsb[:, 0:1], scale=1.0 / 512),
                              reads=[s2k, "epsb"], writes=[s2k + "a"])
                        P.add(ACT, lambda e: e.activation(out=s2[:, 2:3], in_=s2[:, 1:2], func=AF.Exp, scale=-0.5),
                              reads=[s2k + "a"], writes=[s2k + "b"])
                        P.add(DVE, lambda e: e.scalar_tensor_tensor(out=yan[:], in0=yat[:], scalar=s2[:, 2:3], in1=gattn[:, l, :],
                                                                    op0=ALU.mult, op1=ALU.mult),
                              reads=[yk, s2k + "b", "gattn"], writes=[ynk])

                    def stage_E2(b):
                        yan = yans[b % 2]
                        ynk = f"yan{b % 2}"
                        pt, pk = psg()
                        ptb = pt[:, 0:256].bitcast(BF16)
                        for c in range(4):
                            P.add(PE, lambda e, c=c: e.transpose(out=ptb[:, c * 128:(c + 1) * 128], in_=yan[:, c * 128:(c + 1) * 128], identity=ident),
                                  reads=[ynk, "cmat"], writes=[pk])
                        P.add(ACT, lambda e: e.activation(out=yT[:, 0:4, b * 128:(b + 1) * 128],
                                                          in_=ptb[:, 0:512].rearrange("p (c q) -> p c q", c=4), func=AF.Copy),
                              reads=[pk], writes=[f"yT{c}" for c in range(4)] + ["Umix"])

                    cg = conv_gen()
                    nun = 2 * nb
                    gpool[0] = G2

                    def filler(n=1):
                        for _ in range(n):
                            next(cg, None)

                    for step in range(nun + 2):
                        if 1 <= step <= nun:
                            stage_B2(step - 1)
                        if step < nun:
                            stage_A(step)
                        if 1 <= step <= nun:
                            stage_B2b(step - 1)
                        filler(1)
                        if 1 <= step <= nun:
                            stage_C(step - 1)
                        if step >= 2:
                            stage_D(step - 2)
                        if step % 2 == 1:
                            filler(1)
                        if step >= 4 and step % 2 == 0:
                            stage_E2((step - 4) // 2)
                    stage_E2(nb - 1)
                    for _ in cg:
                        pass
                    gpool[0] = G6
                    P.add(POOL, lambda e: e.tensor_copy(out=kA[l][0:64, 0:128], in_=kA[l][0:64, T:T + 128]), reads=[f"kA{l}"], writes=[f"kA{l}"])
                    P.add(POOL, lambda e: e.tensor_copy(out=kB[l][64:128, 0:128], in_=kB[l][64:128, T:T + 128]), reads=[f"kB{l}"], writes=[f"kB{l}"])
                    P.add(POOL, lambda e: e.tensor_copy(out=Vt[l][:, 0, :], in_=Vt[l][:, nb, :]), reads=[f"Vt{l}"], writes=[f"Vt{l}"])


                    wv, wk = load_piece(ti, l, "out")
                    for d in range(NCH):
                        pt, pk = psg()
                        for ki, kc in enumerate((4, 5, 6, 7, 0, 1, 2, 3)):
                            P.add(PE, lambda e, d=d, kc=kc, ki=ki, pt=pt, wv=wv: e.matmul(
                                pt[:, 0:T], lhsT=wv[:, kc, d * 128:(d + 1) * 128], rhs=yT[:, kc, 0:T], start=(ki == 0), stop=(ki == NCH - 1)),
                                reads=[wk, f"yT{kc}"], writes=[pk])
                        branch_evac(T, l, pt, pk, d, G_POST)
                        if d >= 1:
                            stat_mm(T, d - 1)
                    stat_mm(T, NCH - 1)
                    post_norm_update(T, l)

                    pre_norm_deferred(T, l, G_MPRE)
                    for pj in range(4):
                        wv, wk = load_piece(ti, l, f"up{pj}")
                        for fi in range(8):
                            f = pj * 8 + fi
                            pt, pk = psg()
                            for kc in range(NCH):
                                P.add(PE, lambda e, fi=fi, kc=kc, pt=pt, wv=wv: e.matmul(
                                    pt[:, 0:T], lhsT=wv[:, kc, fi * 128:(fi + 1) * 128], rhs=aT[:, kc, 0:T], start=(kc == 0), stop=(kc == NCH - 1)),
                                    reads=[wk, f"a{kc}"], writes=[pk])
                            if f % 2 == 0:
                                P.add(ACT, lambda e, f=f, pt=pt: e.activation(out=hid[:, f, 0:T], in_=pt[:, 0:T], func=AF.Relu),
                                      reads=[pk], writes=[f"hid{f}", "Umlp"])
                                P.add(ACT, lambda e, f=f: e.activation(out=hid[:, f, 0:T], in_=hid[:, f, 0:T], func=AF.Square),
                                      reads=[f"hid{f}"], writes=[f"hid{f}", "Umlp"])
                            else:
                                P.add(DVE, lambda e, f=f, pt=pt: e.tensor_scalar(out=hid[:, f, 0:T], in0=pt[:, 0:T], scalar1=0.0, scalar2=None, op0=ALU.max),
                                      reads=[pk], writes=[f"hid{f}", "Umlp"])
                                P.add(DVE, lambda e, f=f: e.tensor_tensor(out=hid[:, f, 0:T], in0=hid[:, f, 0:T], in1=hid[:, f, 0:T], op=ALU.mult),
                                      reads=[f"hid{f}"], writes=[f"hid{f}", "Umlp"])
                    for pj in range(4):
                        wv, wk = load_piece(ti, l, f"dn{pj}")
                        for dd in range(2):
                            d = pj * 2 + dd
                            pt, pk = psg()
                            for kc in range(32):
                                P.add(PE, lambda e, dd=dd, kc=kc, pt=pt, wv=wv: e.matmul(
                                    pt[:, 0:T], lhsT=wv[:, kc, dd * 128:(dd + 1) * 128], rhs=hid[:, kc, 0:T], start=(kc == 0), stop=(kc == 31)),
                                    reads=[wk, f"hid{kc}"], writes=[pk])
                            branch_evac(T, l, pt, pk, d, G_MPOST)
                            if d >= 1:
                                stat_mm(T, d - 1)
                    stat_mm(T, NCH - 1)
                    if l == 1 and ti + 1 < len(tile_blocks):
                        for b_ in range(min(2, tile_blocks[ti + 1])):
                            x_load(tok0 + T, b_)
                    post_norm_update(T, l, deferred=True)

                for b in range(nb):
                    gb = blk0 + b
                    if gb < 2:
                        continue
                    xb = xs[b % 2]
                    xk = f"xs{b % 2}"
                    pS = psS[b % 2]
                    pSk = f"psS{b % 2}"
                    for c in range(NCH):
                        P.add(PE, lambda e, c=c, b=b, pS=pS: e.transpose(out=pS[:, c * 128:(c + 1) * 128],
                                                                         in_=hT[:, c, b * 128:(b + 1) * 128], identity=identf[:]),
                              reads=[f"h{c}", "identf"], writes=[pSk + ("a" if c < 4 else "b")])
                    P.add(ACT, lambda e, xb=xb, pS=pS: e.activation(out=xb[:, 0:512], in_=pS[:, 0:512], func=AF.Copy),
                          reads=[pSk + "a"], writes=[xk + "lo"])
                    P.add(DVE, lambda e, xb=xb, pS=pS: e.tensor_copy(out=xb[:, 512:1024], in_=pS[:, 512:1024]),
                          reads=[pSk + "b"], writes=[xk + "hi"])
                    r0 = (gb - 2) * 128
                    P.add(SP, lambda e, xb=xb, r0=r0: e.dma_start(out=out_d[r0:r0 + 128, :], in_=xb[:]), reads=[xk, xk + "lo", xk + "hi"],
                          writes=[xk], dma=True)
                blk0 += nb

        P = Prog()
        program(P)
        _alias_fix(P)
        P.schedule()
        P.start_emit(nc, engs, sems, dsems)
        program(P)
        P.finish_emit()
    return nc


def _alias_fix(P):
    mix_pref = ("qpre", "qT", "yT")
    for op in P.ops:
        keys = op.reads + op.writes
        is_mix = any(k.startswith(mix_pref) for k in keys)
        is_mlp = any(k.startswith("hid") for k in keys)
        op.reads = [k for k in op.reads if k not in ("Umix", "Umlp")]
        op.writes = [k for k in op.writes if k not in ("Umix", "Umlp")]
        if is_mix:
            op.writes.append("Ualias_mix")
        if is_mlp:
            op.writes.append("Ualias_mlp")
    side = None
    for op in P.ops:
        m = "Ualias_mix" in op.writes
        h = "Ualias_mlp" in op.writes
        op.writes = [k for k in op.writes if k not in ("Ualias_mix", "Ualias_mlp")]
        if not (m or h):
            continue
        s = "mix" if m else "mlp"
        if s != side:
            op.writes.append("Uphase")
            side = s
        else:
            op.reads.append("Uphase")


_NC_CACHE = {}


def _prep_shared(mix_pre_g, w_in, conv_w, sinks, attn_out_g, conv_out_g, w_out, mix_post_g,
                 mlp_pre_g, w_up, w_down, mlp_post_g):
    perm = _w_in_perm()
    w_in_p = np.ascontiguousarray(w_in[:, :, perm])
    gtab = np.zeros((128, NG), np.float32)
    for l in range(2):
        o = l * NG_L
        gtab[:, o + G_PRE:o + G_PRE + 8] = mix_pre_g[l].reshape(8, 128).T
        gtab[:, o + G_POST:o + G_POST + 8] = mix_post_g[l].reshape(8, 128).T
        gtab[:, o + G_MPRE:o + G_MPRE + 8] = mlp_pre_g[l].reshape(8, 128).T
        gtab[:, o + G_MPOST:o + G_MPOST + 8] = mlp_post_g[l].reshape(8, 128).T
        gtab[:, o + G_CONV:o + G_CONV + 4] = conv_out_g[l].reshape(4, 128).T
        for k in range(3):
            gtab[:, o + G_CW + k * 4:o + G_CW + k * 4 + 4] = conv_w[l, k].reshape(4, 128).T
    gattn = np.ascontiguousarray(np.broadcast_to(attn_out_g[:, None, :], (2, 128, 512))).astype(np.float32)
    sinkb = np.ascontiguousarray(np.broadcast_to(sinks.reshape(1, 16), (128, 16))).astype(np.float32)
    cmat = np.concatenate([np.eye(128, dtype=np.float32), _perm_matrix(), np.ones((128, 128), np.float32)], axis=1)
    return dict(w_in=w_in_p, w_out=np.ascontiguousarray(w_out), w_up=np.ascontiguousarray(w_up),
                w_down=np.ascontiguousarray(w_down), gtab=gtab, gattn=gattn, sinkb=sinkb, cmat=cmat)


def _core_inputs(x, meta_tokens, core):
    b, half = core // 2, core % 2
    xin = np.zeros((NBLK * 128, D), np.float32)
    if half == 0:
        xin[128 + 112:256] = meta_tokens
        xin[256:] = x[b, 0:4096]
        first_pos = -128 - 112
    else:
        xin[:] = x[b, 3840:8192]
        first_pos = N_META + 3840
    C, S = _rope_tables(first_pos)
    masks = _masks(half == 0)
    masks = np.ascontiguousarray(masks.transpose(1, 0, 2).reshape(128, 4 * 256))
    return dict(xin=xin, ctab=C, stab=S, masks=masks)


def kernel(x, meta_tokens, mix_pre_g, w_in, conv_w, sinks, attn_out_g, conv_out_g, w_out, mix_post_g,
           mlp_pre_g, w_up, w_down, mlp_post_g, _tile_blocks=None, _cores=None):
    x = np.asarray(x, np.float32)
    args = [np.asarray(a, np.float32) for a in (mix_pre_g, w_in, conv_w, sinks, attn_out_g, conv_out_g, w_out,
                                                mix_post_g, mlp_pre_g, w_up, w_down, mlp_post_g)]
    shared = _prep_shared(*args)
    key = tuple(_tile_blocks) if _tile_blocks is not None else None
    if key not in _NC_CACHE:
        _NC_CACHE[key] = build(_tile_blocks)
    nc = _NC_CACHE[key]
    cores = list(range(8)) if _cores is None else _cores
    in_maps = []
    for c in cores:
        m = dict(shared)
        m.update(_core_inputs(x, np.asarray(meta_tokens, np.float32), c))
        in_maps.append(m)
    res = run_bass_kernel_spmd(nc, in_maps, core_ids=list(range(len(cores))))
    if _tile_blocks is not None:
        return [r["out"] for r in res.results]
    out = np.zeros((4, 8192, D), np.float32)
    for i, c in enumerate(cores):
        b, half = c // 2, c % 2
        out[b, half * 4096:(half + 1) * 4096] = res.results[i]["out"]
    return out
```
